# Optimizing a Trainium2 kernel written in Bass

```python
import jax, jax.numpy as jnp
from jax import lax
import numpy as np

D_MODEL = 1024
BATCH = 4
SEQ = 4096
DEPTH = 4

N_MIXERS = 3
HEAD_DIM = 64
N_Q_HEADS = D_MODEL // HEAD_DIM
N_KV_HEADS = 4
GQA_GROUP = N_Q_HEADS // N_KV_HEADS
QKV_WIDTH = (N_Q_HEADS + 2 * N_KV_HEADS) * HEAD_DIM
ROPE_THETA = 10000.0
SWA_WINDOW = 128
RET_HEADS = 4
RET_KEY_DIM = D_MODEL // RET_HEADS
RET_VAL_DIM = 2 * RET_KEY_DIM
RET_IN_WIDTH = 2 * D_MODEL + 2 * RET_HEADS * RET_VAL_DIM
RET_CHUNK = 128
MOBA_BLOCK = 256
MOBA_TOPK = 3
MOBA_Q_CHUNK = 16
D_FF = ((8 * D_MODEL // 3 + 127) // 128) * 128
CONV_WIDTH = 3
EPS = 1e-6
NEG_INF = -1e30

kernel_name = "hybrid_swa_retnet_moba_convffn"


def rms_norm(x, g):
    xf = x.astype(jnp.float32)
    y = xf * lax.rsqrt(jnp.mean(xf * xf, axis=-1, keepdims=True) + EPS)
    return (y * g.astype(jnp.float32)).astype(x.dtype)


def rope_tables(positions, dim):
    inv = 1.0 / (ROPE_THETA ** (jnp.arange(0, dim, 2, dtype=jnp.float32) / dim))
    ang = positions.astype(jnp.float32)[..., None] * inv
    return jnp.cos(ang)[:, :, None, :], jnp.sin(ang)[:, :, None, :]


def apply_rope(x, cos, sin):
    x1, x2 = jnp.split(x.astype(jnp.float32), 2, axis=-1)
    return jnp.concatenate([x1 * cos - x2 * sin, x2 * cos + x1 * sin], axis=-1).astype(x.dtype)


def swa_sink_attention(h, w_in, w_out, sinks, positions):
    B, S, _ = h.shape
    W = SWA_WINDOW
    nb = S // W
    proj = h @ w_in
    q, k, v = jnp.split(proj, [N_Q_HEADS * HEAD_DIM, (N_Q_HEADS + N_KV_HEADS) * HEAD_DIM], axis=-1)
    q = q.reshape(B, S, N_KV_HEADS, GQA_GROUP, HEAD_DIM)
    k = k.reshape(B, S, N_KV_HEADS, HEAD_DIM)
    v = v.reshape(B, S, N_KV_HEADS, HEAD_DIM)
    cos, sin = rope_tables(positions, HEAD_DIM)
    q = apply_rope(q, cos[:, :, :, None, :], sin[:, :, :, None, :])
    k = apply_rope(k, cos, sin)
    qb = q.reshape(B, nb, W, N_KV_HEADS, GQA_GROUP, HEAD_DIM)
    kpad = jnp.pad(k, ((0, 0), (W, 0), (0, 0), (0, 0)))
    vpad = jnp.pad(v, ((0, 0), (W, 0), (0, 0), (0, 0)))
    kb = jnp.concatenate([kpad[:, :-W].reshape(B, nb, W, N_KV_HEADS, HEAD_DIM),
                          k.reshape(B, nb, W, N_KV_HEADS, HEAD_DIM)], axis=2)
    vb = jnp.concatenate([vpad[:, :-W].reshape(B, nb, W, N_KV_HEADS, HEAD_DIM),
                          v.reshape(B, nb, W, N_KV_HEADS, HEAD_DIM)], axis=2)
    s = jnp.einsum('bnqkgd,bnjkd->bnkgqj', qb, kb).astype(jnp.float32) * (HEAD_DIM ** -0.5)
    qi = jnp.arange(W)[:, None]
    kj = jnp.arange(2 * W)[None, :]
    rel = qi + W - kj
    band = (rel >= 0) & (rel < W)
    blk = jnp.arange(nb)[:, None, None]
    mask = band[None] & ((blk > 0) | (kj[None] >= W))
    s = jnp.where(mask[None, :, None, None], s, NEG_INF)
    sink = jnp.broadcast_to(sinks.astype(jnp.float32).reshape(1, 1, N_KV_HEADS, GQA_GROUP, 1, 1),
                            s.shape[:-1] + (1,))
    p = jax.nn.softmax(jnp.concatenate([s, sink], axis=-1), axis=-1)[..., :-1]
    o = jnp.einsum('bnkgqj,bnjkd->bnqkgd', p.astype(v.dtype), vb)
    return o.reshape(B, S, D_MODEL) @ w_out


def retention(h, w_in, w_out, gn_g, positions):
    B, S, _ = h.shape
    H, dk, dv, C = RET_HEADS, RET_KEY_DIM, RET_VAL_DIM, RET_CHUNK
    nc = S // C
    proj = h @ w_in
    q, k, v, g = jnp.split(proj, [D_MODEL, 2 * D_MODEL, 2 * D_MODEL + H * dv], axis=-1)
    cos, sin = rope_tables(positions, dk)
    q = apply_rope(q.reshape(B, S, H, dk), cos, sin)
    k = apply_rope(k.reshape(B, S, H, dk), cos, sin) * (dk ** -0.5)
    v = v.reshape(B, S, H, dv)
    log_gamma = jnp.log(1.0 - 2.0 ** (-5.0 - jnp.arange(H, dtype=jnp.float32)))
    idx = jnp.arange(C, dtype=jnp.float32)
    diff = idx[:, None] - idx[None, :]
    decay_in = jnp.where(diff[None] >= 0, jnp.exp(jnp.maximum(diff, 0.0)[None] * log_gamma[:, None, None]), 0.0)
    q_decay = jnp.exp((idx + 1.0)[None, :] * log_gamma[:, None])
    k_decay = jnp.exp((C - 1.0 - idx)[None, :] * log_gamma[:, None])
    chunk_decay = jnp.exp(C * log_gamma)
    qc = q.reshape(B, nc, C, H, dk)
    kc = k.reshape(B, nc, C, H, dk)
    vc = v.reshape(B, nc, C, H, dv)
    sc = jnp.einsum('bnihd,bnjhd->bnhij', qc, kc) * decay_in[None, None]
    inner = jnp.einsum('bnhij,bnjhe->bnihe', sc, vc)

    def step(R, xs):
        q_n, k_n, v_n = xs
        cross = jnp.einsum('bihd,hi,bhde->bihe', q_n, q_decay, R)
        R = chunk_decay[None, :, None, None] * R + jnp.einsum('bjhd,hj,bjhe->bhde', k_n, k_decay, v_n)
        return R, cross

    R0 = jnp.zeros((B, H, dk, dv), jnp.float32)
    _, cross = lax.scan(step, R0, (jnp.moveaxis(qc, 1, 0), jnp.moveaxis(kc, 1, 0), jnp.moveaxis(vc, 1, 0)))
    o = (inner + jnp.moveaxis(cross, 0, 1)).astype(jnp.float32).reshape(B, S, H, dv)
    mu = jnp.mean(o, axis=-1, keepdims=True)
    var = jnp.mean(jnp.square(o - mu), axis=-1, keepdims=True)
    o = (o - mu) * lax.rsqrt(var + EPS) * gn_g.astype(jnp.float32).reshape(H, dv)
    y = jax.nn.silu(g.astype(jnp.float32)) * o.reshape(B, S, H * dv)
    return y.astype(h.dtype) @ w_out


def moba_attention(h, w_in, w_out, positions):
    B, S, _ = h.shape
    BS, QC = MOBA_BLOCK, MOBA_Q_CHUNK
    nb = -(-S // BS)
    pad = nb * BS - S
    nq = S // QC
    topk = min(MOBA_TOPK, nb)
    scale = HEAD_DIM ** -0.5
    proj = h @ w_in
    q, k, v = jnp.split(proj, [N_Q_HEADS * HEAD_DIM, (N_Q_HEADS + N_KV_HEADS) * HEAD_DIM], axis=-1)
    cos, sin = rope_tables(positions, HEAD_DIM)
    q = apply_rope(q.reshape(B, S, N_Q_HEADS, HEAD_DIM), cos, sin)
    k = apply_rope(k.reshape(B, S, N_KV_HEADS, HEAD_DIM), cos, sin)
    v = v.reshape(B, S, N_KV_HEADS, HEAD_DIM)
    k = jnp.repeat(k, GQA_GROUP, axis=2)
    v = jnp.repeat(v, GQA_GROUP, axis=2)
    kp = jnp.pad(k, ((0, 0), (0, pad), (0, 0), (0, 0))).transpose(0, 2, 1, 3).reshape(B, N_Q_HEADS, nb, BS, HEAD_DIM)
    vp = jnp.pad(v, ((0, 0), (0, pad), (0, 0), (0, 0))).transpose(0, 2, 1, 3).reshape(B, N_Q_HEADS, nb, BS, HEAD_DIM)
    kmean = jnp.mean(kp.astype(jnp.float32), axis=3).astype(k.dtype)
    qh = q.transpose(0, 2, 1, 3)
    gather = jax.vmap(jax.vmap(lambda blocks, sel: blocks[sel]))

    def chunk(ci):
        start = ci * QC
        own = start // BS
        t = start + jnp.arange(QC)
        qc = lax.dynamic_slice_in_dim(qh, start, QC, axis=2)
        gate = jnp.einsum('bhqd,bhnd->bhqn', qc, kmean).astype(jnp.float32)
        gate = jnp.where(jnp.arange(nb) < own, gate, NEG_INF)
        _, sel = lax.top_k(gate, topk)
        sel_valid = jnp.arange(topk) < jnp.minimum(own, topk)
        ks = gather(kp, sel)
        vs = gather(vp, sel)
        s_sel = jnp.einsum('bhqd,bhqnjd->bhqnj', qc, ks).astype(jnp.float32) * scale
        s_sel = jnp.where(sel_valid[:, None], s_sel, NEG_INF).reshape(B, N_Q_HEADS, QC, topk * BS)
        k_own = lax.dynamic_index_in_dim(kp, own, axis=2, keepdims=False)
        v_own = lax.dynamic_index_in_dim(vp, own, axis=2, keepdims=False)
        s_own = jnp.einsum('bhqd,bhjd->bhqj', qc, k_own).astype(jnp.float32) * scale
        kpos = own * BS + jnp.arange(BS)
        s_own = jnp.where(kpos[None, :] <= t[:, None], s_own, NEG_INF)
        p = jax.nn.softmax(jnp.concatenate([s_sel, s_own], axis=-1), axis=-1).astype(v.dtype)
        p_sel = p[..., :topk * BS].reshape(B, N_Q_HEADS, QC, topk, BS)
        p_own = p[..., topk * BS:]
        return (jnp.einsum('bhqnj,bhqnjd->bhqd', p_sel, vs)
                + jnp.einsum('bhqj,bhjd->bhqd', p_own, v_own))

    o = lax.map(chunk, jnp.arange(nq))
    o = o.transpose(1, 0, 3, 2, 4).reshape(B, S, D_MODEL)
    return o @ w_out


def conv_glu_ffn(h, w_a, w_b, conv_w, conv_b, w_down):
    a = h @ w_a
    a = lax.conv_general_dilated(a, conv_w.reshape(CONV_WIDTH, 1, D_FF).astype(a.dtype),
                                 window_strides=(1,), padding=[(CONV_WIDTH - 1, 0)],
                                 dimension_numbers=('NWC', 'WIO', 'NWC'),
                                 feature_group_count=D_FF)
    return (jax.nn.silu(a + conv_b) * (h @ w_b)) @ w_down


def setup_inputs(seed: int = 0) -> dict:
    key = jax.random.key(seed)
    keys = iter(jax.random.split(key, 16 * DEPTH + 4))

    def dense(fan_in, fan_out):
        return jax.random.normal(next(keys), (fan_in, fan_out), jnp.float32) * (fan_in ** -0.5)

    def gain(n):
        return 1.0 + 0.02 * jax.random.normal(next(keys), (n,), jnp.float32)

    d = {}
    d["x"] = jax.random.normal(next(keys), (BATCH, SEQ, D_MODEL), jnp.float32)
    d["positions"] = jnp.broadcast_to(jnp.arange(SEQ, dtype=jnp.int32)[None, :], (BATCH, SEQ))
    for i in range(DEPTH):
        m = i % N_MIXERS
        p = "l%d_" % i
        d[p + "attn_norm"] = gain(D_MODEL)
        if m == 0:
            d[p + "w_in"] = dense(D_MODEL, QKV_WIDTH)
            d[p + "w_out"] = dense(D_MODEL, D_MODEL)
            d[p + "sinks"] = 0.5 * jax.random.normal(next(keys), (N_Q_HEADS,), jnp.float32)
        elif m == 1:
            d[p + "w_in"] = dense(D_MODEL, RET_IN_WIDTH)
            d[p + "w_out"] = dense(RET_HEADS * RET_VAL_DIM, D_MODEL)
            d[p + "gn_g"] = gain(RET_HEADS * RET_VAL_DIM)
        else:
            d[p + "w_in"] = dense(D_MODEL, QKV_WIDTH)
            d[p + "w_out"] = dense(D_MODEL, D_MODEL)
        d[p + "ffn_norm"] = gain(D_MODEL)
        d[p + "w_a"] = dense(D_MODEL, D_FF)
        d[p + "w_b"] = dense(D_MODEL, D_FF)
        d[p + "conv_w"] = jax.random.normal(next(keys), (CONV_WIDTH, D_FF), jnp.float32) * (CONV_WIDTH ** -0.5)
        d[p + "conv_b"] = 0.02 * jax.random.normal(next(keys), (D_FF,), jnp.float32)
        d[p + "w_down"] = dense(D_FF, D_MODEL)
    d["final_norm"] = gain(D_MODEL)
    return d


def reference(x, positions,
              l0_attn_norm, l0_w_in, l0_w_out, l0_sinks,
              l0_ffn_norm, l0_w_a, l0_w_b, l0_conv_w, l0_conv_b, l0_w_down,
              l1_attn_norm, l1_w_in, l1_w_out, l1_gn_g,
              l1_ffn_norm, l1_w_a, l1_w_b, l1_conv_w, l1_conv_b, l1_w_down,
              l2_attn_norm, l2_w_in, l2_w_out,
              l2_ffn_norm, l2_w_a, l2_w_b, l2_conv_w, l2_conv_b, l2_w_down,
              l3_attn_norm, l3_w_in, l3_w_out, l3_sinks,
              l3_ffn_norm, l3_w_a, l3_w_b, l3_conv_w, l3_conv_b, l3_w_down,
              final_norm):
    mixer_norms = [l0_attn_norm, l1_attn_norm, l2_attn_norm, l3_attn_norm]
    mixer_args = [(l0_w_in, l0_w_out, l0_sinks),
                  (l1_w_in, l1_w_out, l1_gn_g),
                  (l2_w_in, l2_w_out),
                  (l3_w_in, l3_w_out, l3_sinks)]
    ffn_norms = [l0_ffn_norm, l1_ffn_norm, l2_ffn_norm, l3_ffn_norm]
    ffn_args = [(l0_w_a, l0_w_b, l0_conv_w, l0_conv_b, l0_w_down),
                (l1_w_a, l1_w_b, l1_conv_w, l1_conv_b, l1_w_down),
                (l2_w_a, l2_w_b, l2_conv_w, l2_conv_b, l2_w_down),
                (l3_w_a, l3_w_b, l3_conv_w, l3_conv_b, l3_w_down)]
    for i in range(DEPTH):
        hn = rms_norm(x, mixer_norms[i])
        m = i % N_MIXERS
        if m == 0:
            y = swa_sink_attention(hn, *mixer_args[i], positions)
        elif m == 1:
            y = retention(hn, *mixer_args[i], positions)
        else:
            y = moba_attention(hn, *mixer_args[i], positions)
        x = x + y.astype(x.dtype)
        x = x + conv_glu_ffn(rms_norm(x, ffn_norms[i]), *ffn_args[i]).astype(x.dtype)
    return rms_norm(x, final_norm)
```

```python
import numpy as np
import concourse.bass as bass
import concourse.mybir as mybir
from concourse.bass_utils import run_bass_kernel_spmd
from contextlib import ExitStack
import ml_dtypes

F32 = mybir.dt.float32
BF16 = mybir.dt.bfloat16
I32 = mybir.dt.int32
AF = mybir.ActivationFunctionType
ALU = mybir.AluOpType
AX = mybir.AxisListType

D = 1024
NCH = 8
SEQ = 4096
BATCH = 4
NCORES = 8
T = 2048
DFF = 2816
NFF = 22
EPS = 1e-6
HD = 64
NQH = 16
NKV = 4

SAME_ENGINE_SYNC = True
DBG = {"stage": 99}


class Buf:
    __slots__ = ("name", "w", "r", "multi", "sem", "cnt", "excl")

    def __init__(self, name, multi=False, excl=False):
        self.name = name
        self.excl = excl
        self.w = {}
        self.r = {}
        self.multi = multi
        self.sem = None
        self.cnt = 0


class Sched:
    def __init__(self, nc, es, prefix=""):
        self.nc = nc
        self.es = es
        self.prefix = prefix
        self.q = {k: [] for k in ("pe", "act", "dve", "pool", "sp")}
        self.sem = {}
        self.cnt = {k: 0 for k in self.q}
        self.seen = {k: {} for k in self.q}
        self.semname = {}
        for k in self.q:
            self.sem[k] = self.new_sem("s_" + k)
        self.nsem = 5
        self.dma_keys = []

    def new_sem(self, name):
        if self.prefix:
            return self.nc.alloc_semaphore(name=self.prefix + name)
        return self.es.enter_context(self.nc.semaphore(name))

    def _collect(self, eng, reads, writes):
        waits = {}

        def merge(d):
            for s, (h, v) in d.items():
                if s not in waits or waits[s][1] < v:
                    waits[s] = (h, v)

        for b in reads:
            merge(b.w)
        for b in writes:
            merge(b.r)
            if not b.multi:
                merge(b.w)
        own = id(self.sem[eng])
        out = []
        seen = self.seen[eng]
        for s, (h, v) in waits.items():
            if s == own and (eng == "pe" or not SAME_ENGINE_SYNC):
                continue
            if seen.get(s, 0) >= v:
                continue
            seen[s] = v
            out.append((h, v))
        return out

    def _record(self, tok, reads, writes):
        s, h, v = tok
        for b in reads:
            if s not in b.r or b.r[s][1] < v:
                b.r[s] = (h, v)
        for b in writes:
            if b.multi:
                if b.r:
                    b.w = {}
                    b.r = {}
                if s not in b.w or b.w[s][1] < v:
                    b.w[s] = (h, v)
            else:
                b.w = {s: (h, v)}
                b.r = {}

    def op(self, eng, fn, reads=(), writes=()):
        ex = [b for b in reads if b.excl]
        if ex:
            reads = [b for b in reads if not b.excl]
            writes = list(writes) + ex
        waits = self._collect(eng, reads, writes)
        self.cnt[eng] += 1
        h = self.sem[eng]
        v = self.cnt[eng]
        if DBG.get("log") is not None:
            DBG["log"].append((eng, v, [(getattr(wh, "name", str(wh)), wv) for wh, wv in waits],
                               [b.name for b in reads], [b.name for b in writes]))
        q = self.q[eng]
        for (wh, wv) in waits:
            q.append(lambda e, wh=wh, wv=wv: e.wait_ge(wh, wv))
        q.append(lambda e, fn=fn, h=h: fn(e).then_inc(h, 1))
        tok = (id(h), h, v)
        self._record(tok, reads, writes)
        return tok

    def dma(self, queue, out, in_, reads, writes, key):
        waits = self._collect(queue, reads, writes)
        if key.sem is None:
            key.sem = self.new_sem("d_" + key.name)
            self.dma_keys.append(key)
            self.nsem += 1
        key.cnt += 16
        h = key.sem
        v = key.cnt
        q = self.q[queue]
        for (wh, wv) in waits:
            q.append(lambda e, wh=wh, wv=wv: e.wait_ge(wh, wv))
        q.append(lambda e, out=out, in_=in_, h=h: e.dma_start(out=out, in_=in_).then_inc(h, 16))
        tok = (id(h), h, v)
        self._record(tok, reads, writes)
        return tok

    def dma_group(self, queue, items, reads, writes, key):
        waits = self._collect(queue, reads, writes)
        if key.sem is None:
            key.sem = self.new_sem("d_" + key.name)
            self.dma_keys.append(key)
            self.nsem += 1
        h = key.sem
        q = self.q[queue]
        for (wh, wv) in waits:
            q.append(lambda e, wh=wh, wv=wv: e.wait_ge(wh, wv))
        for (out, in_) in items:
            key.cnt += 16
            q.append(lambda e, out=out, in_=in_, h=h: e.dma_start(out=out, in_=in_).then_inc(h, 16))
        tok = (id(h), h, key.cnt)
        self._record(tok, reads, writes)
        return tok

    def final_wait(self, eng, bufs):
        waits = {}
        for b in bufs:
            for s, (h, v) in b.w.items():
                if s not in waits or waits[s][1] < v:
                    waits[s] = (h, v)
        q = self.q[eng]
        for s, (h, v) in waits.items():
            q.append(lambda e, h=h, v=v: e.wait_ge(h, v))

    def emit(self):
        nc = self.nc
        for key in self.dma_keys:
            self.q["sp"].append(lambda e, h=key.sem, v=key.cnt: e.wait_ge(h, v))
        with nc.Block() as block:
            @block.tensor
            def _(e):
                for f in self.q["pe"]:
                    f(e)

            @block.scalar
            def _(e):
                for f in self.q["act"]:
                    f(e)

            @block.vector
            def _(e):
                for f in self.q["dve"]:
                    f(e)

            @block.gpsimd
            def _(e):
                for f in self.q["pool"]:
                    f(e)

            @block.sync
            def _(e):
                for f in self.q["sp"]:
                    f(e)


class Ctx:
    def __init__(self, nc=None, io=None, prefix="", opts=None):
        self.nc = nc if nc is not None else bass.Bass("TRN2", target_bir_lowering=False)
        self.io = io
        self.prefix = prefix
        self.opts = opts or {}
        self.es = ExitStack()
        self.S = Sched(self.nc, self.es, prefix)
        self.ps = []
        self.ps_rr = 0

    def sb(self, name, shape, dt):
        t = self.es.enter_context(self.nc.sbuf_tensor(self.prefix + name, shape, dt))
        return t

    def psum_banks(self, n=8):
        for i in range(n):
            t = self.es.enter_context(self.nc.psum_tensor(self.prefix + "ps%d" % i, [128, 512], F32))
            self.ps.append((t, Buf("ps%d" % i, excl=True)))

    def din(self, name, shape, dt):
        if self.io is not None:
            ap = self.io[name]
            assert tuple(ap.shape) == tuple(shape), (name, tuple(ap.shape), tuple(shape))
            return ap
        return self.nc.dram_tensor(name, shape, dt, kind="ExternalInput").ap()

    def dout(self, name, shape, dt):
        if self.io is not None:
            ap = self.io[name]
            assert tuple(ap.shape) == tuple(shape), (name, tuple(ap.shape), tuple(shape))
            return ap
        return self.nc.dram_tensor(name, shape, dt, kind="ExternalOutput").ap()

    def dint(self, name, shape, dt):
        return self.nc.dram_tensor(name, shape, dt, kind="Internal").ap()


def cols(tt, off=0):
    return slice(off + tt * 512, off + (tt + 1) * 512)


class Res:
    def __init__(self, cx, with_hn=True):
        self.cx = cx
        self.x_sb = cx.sb("x_sb", [128, NCH, T], F32)
        self.xb = [[Buf("x%d_%d" % (c, tt)) for tt in range(4)] for c in range(NCH)]
        self.xld = [Buf("xld%d" % tt) for tt in range(4)]
        self.cf = cx.sb("cf_sb", [128, 8], F32)
        self.eps_sb = self.cf
        self.epsb = Buf("cf")
        self.outb = Buf("xout", multi=True)
        self.ones = cx.sb("ones_sb", [128, 128], BF16)
        self.onesb = Buf("ones")
        if not with_hn:
            return
        self.hn_sb = cx.sb("hn_sb", [128, NCH, 2 + T], BF16)
        self.hb = [[Buf("h%d_%d" % (c, tt)) for tt in range(4)] for c in range(NCH)]
        self.hhalo = Buf("hhalo")
        self.xh_sb = cx.sb("xh_sb", [128, NCH, 2], F32)
        self.xhb = Buf("xh")
        self.sq = cx.sb("sq", [128, NCH, 512], BF16)
        self.sqq = [Buf("sqq%d" % i) for i in range(4)]
        self.rs = [cx.sb("rs%d" % i, [128, 512], F32) for i in range(2)]
        self.rsb = [Buf("rs%d" % i) for i in range(2)]
        self.rs_i = 0

    def load_x(self, xT_d, queue="sp"):
        S = self.cx.S
        for tt in range(4):
            S.dma_group(queue, [(self.x_sb[:, c, cols(tt)], xT_d[c, :, cols(tt)]) for c in range(NCH)],
                        [], [self.xb[c][tt] for c in range(NCH)], self.xld[tt])

    def load_consts(self, cf_d, ones_d):
        S = self.cx.S
        S.dma("sp", self.cf[:, :], cf_d[:, :], [], [self.epsb], self.epsb)
        S.dma("pool", self.ones[:, :], ones_d[:, :], [], [self.onesb], self.onesb)

    def store_x(self, xT_d, queue="sp"):
        S = self.cx.S
        for tt in range(4):
            S.dma_group(queue, [(xT_d[c, :, cols(tt)], self.x_sb[:, c, cols(tt)]) for c in range(NCH)],
                        [self.xb[c][tt] for c in range(NCH)], [self.outb], self.outb)


def emit_norm(cx, R, g_sb, gbuf, ss_bank, tiles=(0, 1, 2, 3), halo=False, out_f32=False):
    S = cx.S
    ps_t, ps_b = cx.ps[ss_bank]
    work = [("t", tt) for tt in tiles]
    if halo:
        work = [("h", 0)] + work
    for kind, tt in work:
        if kind == "t":
            n = 512
            xin = R.x_sb[:, :, cols(tt)]
            xbufs = [R.xb[c][tt] for c in range(NCH)]
            if out_f32:
                hout = lambda c: R.x_sb[:, c, cols(tt)]
                hbufs = [R.xb[c][tt] for c in range(NCH)]
            else:
                hout = lambda c: R.hn_sb[:, c, cols(tt, 2)]
                hbufs = [R.hb[c][tt] for c in range(NCH)]
        else:
            n = 2
            xin = R.xh_sb[:, :, :]
            xbufs = [R.xhb]
            hout = lambda c: R.hn_sb[:, c, 0:2]
            hbufs = [R.hhalo] * NCH
        sq = R.sq[:, :, 0:n]
        S.op("act", lambda e, sq=sq, xin=xin: e.activation(out=sq, in_=xin, func=AF.Square),
             reads=xbufs, writes=R.sqq)

        def mm(e, n=n):
            ins = None
            for c in range(NCH):
                ins = e.matmul(ps_t[:, 0:n], lhsT=R.ones[:, :], rhs=R.sq[:, c, 0:n],
                               start=(c == 0), stop=(c == NCH - 1))
            return ins
        S.op("pe", mm, reads=R.sqq + [R.onesb], writes=[ps_b])
        i = R.rs_i
        R.rs_i ^= 1
        rs = R.rs[i][:, 0:n]
        S.op("act", lambda e, rs=rs, n=n: e.activation(out=rs, in_=ps_t[:, 0:n], func=AF.Sqrt,
                                                      scale=1.0 / D, bias=R.eps_sb[:, 0:1]),
             reads=[ps_b, R.epsb], writes=[R.rsb[i]])
        S.op("dve", lambda e, rs=rs: e.reciprocal(out=rs, in_=rs), reads=[R.rsb[i]], writes=[R.rsb[i]])
        for c in range(NCH):
            xi = xin[:, c, :]
            S.op("dve", lambda e, c=c, xi=xi, rs=rs, ho=hout(c): e.scalar_tensor_tensor(
                out=ho, in0=xi, scalar=g_sb[:, c:c + 1], in1=rs, op0=ALU.mult, op1=ALU.mult),
                reads=[xbufs[c] if kind == "t" else R.xhb, R.rsb[i], gbuf], writes=[hbufs[c]])


class FFNRes:
    def __init__(self, cx, R):
        self.JG = 4
        self.wa = [cx.sb("wa%d" % i, [128, NCH, 512], BF16) for i in range(2)]
        self.wb = [cx.sb("wb%d" % i, [128, NCH, 512], BF16) for i in range(2)]
        self.wd = [cx.sb("wd%d" % i, [128, 4, D], BF16) for i in range(2)]
        self.wab = [Buf("wa%d" % i) for i in range(2)]
        self.wbb = [Buf("wb%d" % i) for i in range(2)]
        self.wdb = [Buf("wd%d" % i) for i in range(2)]
        self.u = cx.sb("u_sb", [128, 4, T], BF16)
        self.ub = [[Buf("u%d_%d" % (j, tt)) for tt in range(4)] for j in range(4)]
        self.af = cx.sb("a_full", [128, 4, 2 + T], F32)
        self.ab = [[Buf("a%d_%d" % (j, tt)) for tt in range(4)] for j in range(4)]
        self.ahb = [Buf("ah%d" % j) for j in range(4)]
        q = [R.sq[:, 2 * i:2 * i + 2, :].rearrange("p a b -> p (a b)").bitcast(F32) for i in range(4)]
        self.t1 = q[0:2]
        self.t1b = R.sqq[0:2]
        self.sg = q[2:4]
        self.sgb = R.sqq[2:4]
        self.cw = cx.sb("cw_sb", [128, 3, NFF], F32)
        self.cb = cx.sb("cb_sb", [128, NFF], F32)
        self.cwb = Buf("cw")
        self.g = cx.sb("g_ffn_sb", [128, NCH], F32)
        self.gb = Buf("g_ffn")
        self.rr = 0


def ffn_groups():
    gs = []
    j = 0
    while j < NFF:
        n = min(4, NFF - j)
        gs.append((j, n))
        j += n
    return gs


def emit_ffn(cx, R, Fr, w_a, w_b, w_down, cw_d, cb_d, g_d, banks):
    S = cx.S
    S.dma_group("sp", [(Fr.cw[:, :, :], cw_d[:, :, :]), (Fr.cb[:, :], cb_d[:, :])], [], [Fr.cwb], Fr.cwb)
    S.dma("sp", Fr.g[:, :], g_d[:, :], [], [Fr.gb], Fr.gb)
    groups = ffn_groups()

    def load_w(gi):
        j0, n = groups[gi]
        s = gi % 2
        S.dma_group("pool", [(Fr.wa[s][:, kc, 0:n * 128], w_a[kc * 128:(kc + 1) * 128, j0 * 128:(j0 + n) * 128])
                             for kc in range(NCH)], [], [Fr.wab[s]], Fr.wab[s])
        S.dma_group("pool", [(Fr.wb[s][:, kc, 0:n * 128], w_b[kc * 128:(kc + 1) * 128, j0 * 128:(j0 + n) * 128])
                             for kc in range(NCH)], [], [Fr.wbb[s]], Fr.wbb[s])
        S.dma_group("pool", [(Fr.wd[s][:, jj, :], w_down[(j0 + jj) * 128:(j0 + jj + 1) * 128, :])
                             for jj in range(n)], [], [Fr.wdb[s]], Fr.wdb[s])

    if DBG["stage"] < 1:
        return
    emit_norm(cx, R, Fr.g, Fr.gb, banks["ss"], halo=True)
    if DBG["stage"] < 2:
        return
    load_w(0)
    if len(groups) > 1:
        load_w(1)

    a_banks = banks["a"]
    b_banks = banks["b"]
    y_banks = banks["y"]
    h_bank = banks["ss"]
    st = {"a": 0, "b": 0, "y": 0, "t": 0}

    def AB(gi, tt):
        j0, n = groups[gi]
        s = gi % 2
        for jj in range(n):
            j = j0 + jj
            pa_t, pa_b = cx.ps[a_banks[st["a"] % len(a_banks)]]
            st["a"] += 1
            pb_t, pb_b = cx.ps[b_banks[st["b"] % len(b_banks)]]
            st["b"] += 1
            hbufs = [R.hb[k][tt] for k in range(NCH)]

            def mm(e, w, pt, jj=jj, tt=tt):
                ins = None
                for k in range(NCH):
                    ins = e.matmul(pt[:, :], lhsT=w[:, k, jj * 128:(jj + 1) * 128],
                                   rhs=R.hn_sb[:, k, cols(tt, 2)], start=(k == 0), stop=(k == NCH - 1))
                return ins
            S.op("pe", lambda e, w=Fr.wa[s], pt=pa_t, mm=mm: mm(e, w, pt), reads=hbufs + [Fr.wab[s]], writes=[pa_b])
            S.op("pe", lambda e, w=Fr.wb[s], pt=pb_t, mm=mm: mm(e, w, pt), reads=hbufs + [Fr.wbb[s]], writes=[pb_b])
            ti = st["t"] % 2
            st["t"] += 1
            t1 = Fr.t1[ti]
            sg = Fr.sg[ti]
            acur = Fr.af[:, jj, cols(tt, 2)]
            S.op("act", lambda e, acur=acur, pa_t=pa_t: e.activation(out=acur, in_=pa_t[:, :], func=AF.Copy),
                 reads=[pa_b], writes=[Fr.ab[jj][tt]])
            S.op("dve", lambda e, t1=t1, acur=acur, j=j: e.tensor_scalar(
                out=t1[:, :], in0=acur, scalar1=Fr.cw[:, 2, j:j + 1], scalar2=Fr.cb[:, j:j + 1],
                op0=ALU.mult, op1=ALU.add), reads=[Fr.ab[jj][tt], Fr.cwb], writes=[Fr.t1b[ti]])
            prevb = Fr.ab[jj][tt - 1] if tt > 0 else Fr.ahb[jj]
            for sh, wi in ((1, 1), (2, 0)):
                ash = Fr.af[:, jj, slice(2 + tt * 512 - sh, 2 + (tt + 1) * 512 - sh)]
                S.op("dve", lambda e, t1=t1, ash=ash, j=j, wi=wi: e.scalar_tensor_tensor(
                    out=t1[:, :], in0=ash, scalar=Fr.cw[:, wi, j:j + 1], in1=t1[:, :],
                    op0=ALU.mult, op1=ALU.add),
                    reads=[Fr.ab[jj][tt], prevb, Fr.t1b[ti], Fr.cwb], writes=[Fr.t1b[ti]])
            S.op("act", lambda e, t1=t1, sg=sg: e.activation(out=sg[:, :], in_=t1[:, :], func=AF.Silu),
                 reads=[Fr.t1b[ti]], writes=[Fr.sgb[ti]])
            uo = Fr.u[:, jj, cols(tt)]
            S.op("dve", lambda e, uo=uo, sg=sg, pb_t=pb_t: e.tensor_tensor(
                out=uo, in0=sg[:, :], in1=pb_t[:, :], op=ALU.mult),
                reads=[Fr.sgb[ti], pb_b], writes=[Fr.ub[jj][tt]])

    def HALO(gi):
        j0, n = groups[gi]
        s = gi % 2
        ph_t, ph_b = cx.ps[h_bank]

        def mm(e):
            ins = None
            for jj in range(n):
                for k in range(NCH):
                    ins = e.matmul(ph_t[:, jj * 2:jj * 2 + 2], lhsT=Fr.wa[s][:, k, jj * 128:(jj + 1) * 128],
                                   rhs=R.hn_sb[:, k, 0:2], start=(k == 0), stop=(k == NCH - 1))
            return ins
        S.op("pe", mm, reads=[R.hhalo, Fr.wab[s]], writes=[ph_b])
        for jj in range(n):
            S.op("act", lambda e, jj=jj: e.activation(out=Fr.af[:, jj, 0:2], in_=ph_t[:, jj * 2:jj * 2 + 2], func=AF.Copy),
                 reads=[ph_b], writes=[Fr.ahb[jj]])

    def DOWN(gi, tt):
        j0, n = groups[gi]
        s = gi % 2
        if tt not in DBG.get("down_tt", (0, 1, 2, 3)):
            return
        for m in range(DBG.get("down_m", NCH)):
            py_t, py_b = cx.ps[y_banks[st["y"] % len(y_banks)]]
            st["y"] += 1

            def mm(e, m=m, py_t=py_t):
                ins = None
                for jj in range(n):
                    lt = Fr.wa[s][:, jj, 0:128] if DBG.get("usewa") else Fr.wd[s][:, jj, m * 128:(m + 1) * 128]
                    ins = e.matmul(py_t[:, :], lhsT=lt,
                                   rhs=Fr.u[:, jj, cols(tt)], start=(jj == 0), stop=(jj == n - 1))
                return ins
            S.op("pe", mm, reads=[Fr.ub[jj][tt] for jj in range(n)] + [Fr.wab[s] if DBG.get("usewa") else Fr.wdb[s]], writes=[py_b])
            xo = R.x_sb[:, m, cols(tt)]
            if DBG.get("actcopy"):
                S.op("act", lambda e, xo=xo, py_t=py_t: e.activation(out=xo, in_=py_t[:, :], func=AF.Copy),
                     reads=[py_b, R.xb[m][tt]], writes=[R.xb[m][tt]])
            elif DBG.get("noadd"):
                S.op("dve", lambda e, xo=xo, py_t=py_t: e.tensor_copy(out=xo, in_=py_t[:, :]),
                     reads=[py_b, R.xb[m][tt]], writes=[R.xb[m][tt]])
            else:
                S.op("dve", lambda e, xo=xo, py_t=py_t: e.tensor_tensor(out=xo, in0=xo, in1=py_t[:, :], op=ALU.add),
                     reads=[py_b, R.xb[m][tt]], writes=[R.xb[m][tt]])
            if DBG.get("slack"):
                S.op("dve", lambda e: e.memset(R.cf[:, 7:8], 0.0), reads=[py_b], writes=[])

    for gi in range(len(groups)):
        HALO(gi)
        if DBG["stage"] < 3:
            return
        for tt in range(4):
            AB(gi, tt)
            if DBG["stage"] < 4:
                continue
            if tt >= 1:
                DOWN(gi, tt - 1)
        if DBG["stage"] < 4:
            return
        DOWN(gi, 3)
        if DBG["stage"] < 5:
            return
        if gi + 2 < len(groups):
            load_w(gi + 2)


def build_ffn_launch(cx=None, final_norm=False):
    cx = cx or Ctx()
    xT = cx.din("xT", [NCH, 128, T], F32)
    xh = cx.din("xh", [128, NCH, 2], F32) if not cx.opts.get("halo") else None
    cf = cx.din("cf", [128, 8], F32)
    ones = cx.din("ones", [128, 128], F32)
    w_a = cx.din("w_a", [D, DFF], F32)
    w_b = cx.din("w_b", [D, DFF], F32)
    w_down = cx.din("w_down", [DFF, D], F32)
    cw = cx.din("cw", [128, 3, NFF], F32)
    cb = cx.din("cb", [128, NFF], F32)
    g = cx.din("g", [128, NCH], F32)
    if final_norm:
        gf = cx.din("gf", [128, NCH], F32)
    xo = cx.dout("xo", [NCH, 128, T], F32)
    cx.psum_banks(8)
    R = Res(cx)
    Fr = FFNRes(cx, R)
    R.load_consts(cf, ones)
    R.load_x(xT)
    if cx.opts.get("halo") == "zero":
        cx.S.op("dve", lambda e: e.memset(R.xh_sb[:, :, :], 0.0), reads=[], writes=[R.xhb])
    elif cx.opts.get("halo") == "prev":
        xp = cx.io["x_prev"]
        cx.S.dma("sp", R.xh_sb[:, :, :], xp[:, :, T - 2:T].rearrange("c p t -> p c t"), [], [R.xhb], R.xhb)
    else:
        cx.S.dma("sp", R.xh_sb[:, :, :], xh[:, :, :], [], [R.xhb], R.xhb)
    banks = {"ss": 0, "a": DBG.get("abanks", [1, 2]), "b": DBG.get("bbanks", [3, 4]), "y": DBG.get("ybanks", [5, 6, 7])}
    emit_ffn(cx, R, Fr, w_a, w_b, w_down, cw, cb, g, banks)
    R.store_x(xo)
    cx.S.final_wait("sp", [R.outb])
    cx.S.emit()
    return cx


TWO_PI = float(2.0 * np.pi)


def emit_rope_tables(cx, R, tb, pos_f, posb, inv_col, sgn_col, tt):
    S = cx.S
    u = tb["u"]
    kf = tb["kf"]
    MAGIC = 12582912.0
    C1 = 6.28125
    C2 = float(2.0 * np.pi - 6.28125)
    ub, kb = tb["ub"], tb["kb"]
    S.op("dve", lambda e: e.tensor_scalar(out=u[:, :], in0=pos_f[:, cols(tt)], scalar1=inv_col, scalar2=None,
                                          op0=ALU.mult), reads=[posb, R.epsb], writes=[ub])
    S.op("dve", lambda e: e.tensor_scalar(out=kf[:, :], in0=u[:, :], scalar1=float(1.0 / (2.0 * np.pi)), scalar2=MAGIC,
                                          op0=ALU.mult, op1=ALU.add), reads=[ub], writes=[kb])
    S.op("dve", lambda e: e.tensor_scalar(out=kf[:, :], in0=kf[:, :], scalar1=MAGIC, scalar2=None,
                                          op0=ALU.subtract), reads=[kb], writes=[kb])
    S.op("dve", lambda e: e.scalar_tensor_tensor(out=u[:, :], in0=kf[:, :], scalar=-C1, in1=u[:, :],
                                                 op0=ALU.mult, op1=ALU.add), reads=[kb, ub], writes=[ub])
    S.op("dve", lambda e: e.scalar_tensor_tensor(out=u[:, :], in0=kf[:, :], scalar=-C2, in1=u[:, :],
                                                 op0=ALU.mult, op1=ALU.add), reads=[kb, ub], writes=[ub])
    PI_LO = 3.1415925
    S.op("dve", lambda e: e.tensor_scalar(out=u[:, :], in0=u[:, :], scalar1=PI_LO, scalar2=-PI_LO,
                                          op0=ALU.min, op1=ALU.max), reads=[ub], writes=[ub])
    S.op("dve", lambda e: e.scalar_tensor_tensor(out=kf[:, :], in0=u[:, :], scalar=-1.0, in1=u[:, :],
                                                 op0=ALU.mult, op1=ALU.max), reads=[ub, kb], writes=[kb])
    S.op("act", lambda e: e.activation(out=tb["S"][:, :], in_=u[:, :], func=AF.Sin, scale=sgn_col),
         reads=[ub, R.epsb], writes=[tb["Sb"]])
    S.op("act", lambda e: e.activation(out=tb["C"][:, :], in_=kf[:, :], func=AF.Sin, scale=-1.0, bias=R.cf[:, 1:2]),
         reads=[kb, R.epsb], writes=[tb["Cb"]])


def build_qkv_launch(cx=None):
    cx = cx or Ctx()
    S = cx.S
    xT = cx.din("xT", [NCH, 128, T], F32)
    cf = cx.din("cf", [128, 8], F32)
    ones = cx.din("ones", [128, 128], F32)
    perm = cx.din("perm", [128, 128], F32)
    g = cx.din("g", [128, NCH], F32)
    w_in = cx.din("w_in", [D, 1536], F32)
    pos = cx.din("pos", [T], I32)
    qT = cx.dout("qT", [8, 128, T], BF16)
    kT = cx.dout("kT", [2, 128, T], BF16)
    v = cx.dout("v", [16, 128, 256], BF16)
    cx.psum_banks(8)
    R = Res(cx)
    R.load_consts(cf, ones)
    R.load_x(xT)
    g_sb = cx.sb("g_sb", [128, NCH], F32)
    gb = Buf("g")
    S.dma("sp", g_sb[:, :], g[:, :], [], [gb], gb)
    w_sb = cx.sb("w_sb", [128, NCH, 1536], BF16)
    wb = Buf("w_in")
    S.dma_group("pool", [(w_sb[:, kc, :], w_in[kc * 128:(kc + 1) * 128, :]) for kc in range(NCH)], [], [wb], wb)
    perm_sb = cx.sb("perm_sb", [128, 128], BF16)
    permb = Buf("perm")
    S.dma("pool", perm_sb[:, :], perm[:, :], [], [permb], permb)
    pos_i = cx.sb("pos_i", [128, T], I32)
    pos_f = cx.sb("pos_f", [128, T], F32)
    posib = Buf("posi")
    posb = Buf("posf")
    S.dma("sp", pos_i[:, :], pos.partition_broadcast(128), [], [posib], posib)
    S.op("dve", lambda e: e.tensor_copy(out=pos_f[:, :], in_=pos_i[:, :]), reads=[posib], writes=[posb])
    emit_norm(cx, R, g_sb, gb, 0)
    tb = {"u": cx.sb("rp_u", [128, 512], F32), "ub": Buf("rp_u"), "kf": cx.sb("rp_k", [128, 512], F32), "kb": Buf("rp_k"),
          "C": cx.sb("Ctab", [128, 512], F32), "Cb": Buf("C"),
          "S": cx.sb("Stab", [128, 512], F32), "Sb": Buf("S")}
    qraw = [cx.sb("qraw%d" % i, [128, 512], BF16) for i in range(2)]
    qrawb = [Buf("qraw%d" % i) for i in range(2)]
    t1 = [cx.sb("rt1_%d" % i, [128, 512], F32) for i in range(2)]
    t1b = [Buf("rt1_%d" % i) for i in range(2)]
    qo = [cx.sb("qo%d" % i, [128, 512], BF16) for i in range(3)]
    qob = [Buf("qo%d" % i) for i in range(3)]
    vo = [cx.sb("vo%d" % i, [128, 256], BF16) for i in range(2)]
    vob = [Buf("vo%d" % i) for i in range(2)]
    outb = Buf("qkv_out", multi=True)
    n = 0
    nv = 0
    for tt in range(4):
        emit_rope_tables(cx, R, tb, pos_f, posb, R.cf[:, 3:4], R.cf[:, 4:5], tt)
        hbufs = [R.hb[k][tt] for k in range(NCH)]
        for c in range(10):
            pp_t, pp_b = cx.ps[1 + (n % 2)]
            pr_t, pr_b = cx.ps[3 + (n % 2)]
            i2 = n % 2
            i3 = n % 3
            n += 1

            def mm(e, c=c, pp_t=pp_t, tt=tt):
                ins = None
                for k in range(NCH):
                    ins = e.matmul(pp_t[:, :], lhsT=w_sb[:, k, c * 128:(c + 1) * 128],
                                   rhs=R.hn_sb[:, k, cols(tt, 2)], start=(k == 0), stop=(k == NCH - 1))
                return ins
            S.op("pe", mm, reads=hbufs + [wb], writes=[pp_b])
            S.op("act", lambda e, i2=i2, pp_t=pp_t: e.activation(out=qraw[i2][:, :], in_=pp_t[:, :], func=AF.Copy),
                 reads=[pp_b], writes=[qrawb[i2]])
            S.op("pe", lambda e, i2=i2, pr_t=pr_t: e.matmul(pr_t[:, :], lhsT=perm_sb[:, :], rhs=qraw[i2][:, :],
                                                          start=True, stop=True),
                 reads=[qrawb[i2], permb], writes=[pr_b])
            S.op("dve", lambda e, i2=i2: e.tensor_tensor(out=t1[i2][:, :], in0=qraw[i2][:, :], in1=tb["C"][:, :], op=ALU.mult),
                 reads=[qrawb[i2], tb["Cb"]], writes=[t1b[i2]])
            S.op("dve", lambda e, i3=i3, pr_t=pr_t: e.tensor_tensor(out=qo[i3][:, :], in0=pr_t[:, :], in1=tb["S"][:, :], op=ALU.mult),
                 reads=[pr_b, tb["Sb"]], writes=[qob[i3]])
            S.op("dve", lambda e, i3=i3, i2=i2: e.tensor_tensor(out=qo[i3][:, :], in0=qo[i3][:, :], in1=t1[i2][:, :], op=ALU.add),
                 reads=[qob[i3], t1b[i2]], writes=[qob[i3]])
            dst = qT[c, :, cols(tt)] if c < 8 else kT[c - 8, :, cols(tt)]
            S.dma("sp", dst, qo[i3][:, :], [qob[i3]], [outb], qob[i3])
        for st_ in range(4):
            pv_t, pv_b = cx.ps[5 + (nv % 2)]
            iv = nv % 2
            nv += 1
            c0 = 2 + tt * 512 + st_ * 128

            def mmv(e, pv_t=pv_t, c0=c0):
                ins = None
                for k in range(NCH):
                    ins = e.matmul(pv_t[:, 0:256], lhsT=R.hn_sb[:, k, c0:c0 + 128], rhs=w_sb[:, k, 1280:1536],
                                   start=(k == 0), stop=(k == NCH - 1))
                return ins
            S.op("pe", mmv, reads=hbufs + [wb], writes=[pv_b])
            S.op("act", lambda e, iv=iv, pv_t=pv_t: e.activation(out=vo[iv][:, :], in_=pv_t[:, 0:256], func=AF.Copy),
                 reads=[pv_b], writes=[vob[iv]])
            S.dma("sp", v[tt * 4 + st_, :, :], vo[iv][:, :], [vob[iv]], [outb], vob[iv])
    S.final_wait("sp", [outb])
    S.emit()
    return cx


def make_cf():
    cf = np.zeros((128, 8), np.float32)
    p = np.arange(128)
    cf[:, 0] = EPS
    cf[:, 1] = np.pi / 2
    sgn = np.where((p % 64) < 32, -1.0, 1.0)
    cf[:, 3] = (1.0 / (10000.0 ** (np.arange(0, 64, 2, dtype=np.float32) / 64)))[p % 32]
    cf[:, 4] = sgn
    cf[:, 5] = 1.0 / (10000.0 ** (np.arange(0, 256, 2, dtype=np.float32) / 256))
    return cf


def make_perm():
    pm = np.zeros((128, 128), np.float32)
    for m in range(128):
        k = m + 32 if (m % 64) < 32 else m - 32
        pm[k, m] = 1.0
    return pm


def fm(x2d):
    return np.ascontiguousarray(x2d.T.reshape(NCH, 128, x2d.shape[0]))


def vec_fm(vv, n):
    return np.ascontiguousarray(vv.reshape(n, 128).T)


def emit_outproj(cx, R, oT_sb, oTb, w_sb, wb, nk, banks):
    S = cx.S
    n = 0
    for tt in range(4):
        for m in range(NCH):
            py_t, py_b = cx.ps[banks[n % len(banks)]]
            n += 1

            def mm(e, m=m, tt=tt, py_t=py_t):
                ins = None
                for c in range(nk):
                    ins = e.matmul(py_t[:, :], lhsT=w_sb[:, c, m * 128:(m + 1) * 128], rhs=oT_sb[:, c, cols(tt)],
                                   start=(c == 0), stop=(c == nk - 1))
                return ins
            S.op("pe", mm, reads=[oTb[c][tt] for c in range(nk)] + [wb], writes=[py_b])
            xo = R.x_sb[:, m, cols(tt)]
            S.op("dve", lambda e, xo=xo, py_t=py_t: e.tensor_tensor(out=xo, in0=xo, in1=py_t[:, :], op=ALU.add),
                 reads=[py_b, R.xb[m][tt]], writes=[R.xb[m][tt]])


def build_swa_launch(cx=None):
    cx = cx or Ctx()
    S = cx.S
    NT = 17
    xT = cx.din("xT", [NCH, 128, T], F32)
    cf = cx.din("cf", [128, 8], F32)
    ones = cx.din("ones", [128, 128], F32)
    qT = cx.din("qT", [16, 64, T], BF16)
    fused = cx.opts.get("fused", False)
    half = cx.opts.get("half", 0)
    if not fused:
        kTf = cx.din("kTf", [4, 64, 128 + T], BF16)
        vaug = cx.din("vaug", [4, 128, NT, 192], BF16)
    masks = cx.din("masks", [128, 2, 2, 128], BF16)
    sinks = cx.din("sinks", [16], F32)
    w_out = cx.din("w_out", [D, D], F32)
    xo = cx.dout("xo", [NCH, 128, T], F32)
    cx.psum_banks(8)
    R = Res(cx, with_hn=False)
    R.load_consts(cf, ones)
    R.load_x(xT)
    mk = cx.sb("mk_sb", [128, 2, 2, 128], BF16)
    mkb = Buf("mk")
    S.dma("sp", mk[:, :, :, :], masks[:, :, :, :], [], [mkb], mkb)
    esk = cx.sb("esk", [128, 16], F32)
    eskb = Buf("esk")
    S.dma("sp", esk[:, :], sinks.partition_broadcast(128), [], [eskb], eskb)
    S.op("act", lambda e: e.activation(out=esk[:, :], in_=esk[:, :], func=AF.Exp), reads=[eskb], writes=[eskb])
    w_sb = cx.sb("wo_sb", [128, NCH, D], BF16)
    wb = Buf("w_out")
    S.dma_group("pool", [(w_sb[:, kc, :], w_out[kc * 128:(kc + 1) * 128, :]) for kc in range(NCH)], [], [wb], wb)
    oT = cx.sb("oT_sb", [128, NCH, T], BF16)
    oTb = [[Buf("oT%d_%d" % (c, tt)) for tt in range(4)] for c in range(NCH)]
    kg = [cx.sb("kg%d" % i, [64, 128 + T], BF16) for i in range(2)]
    kgb = [Buf("kg%d" % i) for i in range(2)]
    vg = [cx.sb("vg%d" % i, [128, NT, 192], BF16) for i in range(2)]
    vgb = [Buf("vg%d" % i) for i in range(2)]
    qg = [cx.sb("qg%d" % i, [64, 4, T], BF16) for i in range(2)]
    qgb = [Buf("qg%d" % i) for i in range(2)]
    P = [cx.sb("P%d" % i, [128, 2, 256], BF16) for i in range(4)]
    Pb = [Buf("P%d" % i) for i in range(4)]
    rec = [cx.sb("rec%d" % i, [128, 2, 128], F32) for i in range(2)]
    recb = [Buf("rec%d" % i) for i in range(2)]

    if fused:
        for s_ in range(2):
            S.op("dve", lambda e, s_=s_: e.memset(vg[s_][:, :, :], 1.0), reads=[], writes=[vgb[s_]])
            if half == 0:
                S.op("dve", lambda e, s_=s_: e.memset(kg[s_][:, 0:128], 0.0), reads=[], writes=[kgb[s_]])
        kown = cx.io["kT_own"].rearrange("c (h d) t -> (c h) d t", h=2)
        vown = cx.io["v_own"].rearrange("n p (g d) -> n p g d", g=4)
        if half == 1:
            kprev = cx.io["kT_prev"].rearrange("c (h d) t -> (c h) d t", h=2)
            vprev = cx.io["v_prev"].rearrange("n p (g d) -> n p g d", g=4)

    def load_g(g):
        s = g % 2
        if not fused:
            S.dma("sp", kg[s][:, :], kTf[g, :, :], [], [kgb[s]], kgb[s])
            S.dma("sp", vg[s][:, :, :], vaug[g, :, :, :], [], [vgb[s]], vgb[s])
        else:
            items = [(kg[s][:, 128:], kown[g, :, :])]
            if half == 1:
                items.append((kg[s][:, 0:128], kprev[g, :, T - 128:T]))
            S.dma_group("sp", items, [], [kgb[s]], kgb[s])
            items = [(vg[s][:, 1 + 4 * a:5 + 4 * a, 64:128], vown[4 * a:4 * a + 4, :, g, :].rearrange("n p d -> p n d"))
                     for a in range(4)]
            if half == 1:
                items.append((vg[s][:, 0, 64:128], vprev[15, :, g, :]))
            S.dma_group("sp", items, [], [vgb[s]], vgb[s])
        S.dma_group("sp", [(qg[s][:, hl, :], qT[4 * g + hl, :, :]) for hl in range(4)], [], [qgb[s]], qgb[s])

    load_g(0)
    np_ = 0
    for g in range(4):
        s = g % 2
        if g + 1 < 4:
            load_g(g + 1)
        for i in range(16):
            tt = i // 4
            for par in range(2):
                ps_t, ps_b = cx.ps[1 + (np_ % 2)]
                po_t, po_b = cx.ps[3 + (np_ % 4)]
                pi = np_ % 4
                ri = np_ % 2
                np_ += 1

                def mmqk(e, ps_t=ps_t, par=par, i=i, s=s):
                    ins = None
                    for kt in range(2):
                        for hh in range(2):
                            hl = 2 * hh + par
                            c0 = kt * 256 + hh * 128
                            ins = e.matmul(ps_t[:, c0:c0 + 128], lhsT=kg[s][:, (i + kt) * 128:(i + kt + 1) * 128],
                                           rhs=qg[s][:, hl, i * 128:(i + 1) * 128], start=True, stop=True)
                    return ins
                S.op("pe", mmqk, reads=[kgb[s], qgb[s]], writes=[ps_b])
                Pv = P[pi]
                S.op("act", lambda e, Pv=Pv, ps_t=ps_t: e.activation(
                    out=Pv[:, :, :], in_=ps_t[:, :].rearrange("p (k q) -> p k q", k=2), func=AF.Exp, scale=0.125),
                    reads=[ps_b], writes=[Pb[pi]])
                mv = 1 if i == 0 else 0
                P4 = Pv[:, :, :].rearrange("p k (h q) -> p k h q", h=2)
                S.op("pool", lambda e, P4=P4, mv=mv: e.tensor_tensor(
                    out=P4, in0=P4, in1=mk[:, mv, :, :].unsqueeze(2).broadcast_to([128, 2, 2, 128]), op=ALU.mult),
                    reads=[Pb[pi], mkb], writes=[Pb[pi]])
                vsl = slice(64, 192) if par == 0 else slice(0, 128)

                def mmpv(e, po_t=po_t, Pv=Pv, vsl=vsl, i=i, s=s):
                    ins = None
                    for kt in range(2):
                        ins = e.matmul(po_t[:, 0:256], lhsT=vg[s][:, i + kt, vsl], rhs=Pv[:, kt, :],
                                       start=(kt == 0), stop=(kt == 1))
                    return ins
                S.op("pe", mmpv, reads=[Pb[pi], vgb[s]], writes=[po_b])
                nlo, dlo = (0, 64) if par == 0 else (64, 0)
                num = po_t[nlo:nlo + 64, 0:256].rearrange("p (h q) -> p h q", h=2)
                den = po_t[dlo:dlo + 64, 0:256].rearrange("p (h q) -> p h q", h=2)
                rc = rec[ri][nlo:nlo + 64, :, :]
                ek = esk[nlo:nlo + 64, 4 * g + par:4 * g + par + 3:2].unsqueeze(2).broadcast_to([64, 2, 128])
                S.op("dve", lambda e, rc=rc, den=den, ek=ek: e.tensor_tensor(out=rc, in0=den, in1=ek, op=ALU.add),
                     reads=[po_b, eskb], writes=[recb[ri]])
                S.op("dve", lambda e, rc=rc: e.reciprocal(out=rc, in_=rc), reads=[recb[ri]], writes=[recb[ri]])
                oo = oT[nlo:nlo + 64, 2 * g:2 * g + 2, i * 128:(i + 1) * 128]
                S.op("dve", lambda e, oo=oo, num=num, rc=rc: e.tensor_tensor(out=oo, in0=num, in1=rc, op=ALU.mult),
                     reads=[po_b, recb[ri]], writes=[oTb[2 * g][tt], oTb[2 * g + 1][tt]])
    emit_outproj(cx, R, oT, oTb, w_sb, wb, NCH, [7, 0])
    R.store_x(xo)
    S.final_wait("sp", [R.outb])
    S.emit()
    return cx


_PROGS = {}
BF = ml_dtypes.bfloat16


def _prog(name, builder):
    if name not in _PROGS:
        _PROGS[name] = builder()
    return _PROGS[name]


def _run(name, builder, in_maps):
    cx = _prog(name, builder)
    res = run_bass_kernel_spmd(cx.nc, in_maps, core_ids=list(range(len(in_maps))))
    return res.results


def _consts():
    return {"cf": make_cf(), "ones": np.ones((128, 128), np.float32)}


def _swa_masks(half):
    k = np.arange(128)[:, None]
    q = np.arange(128)[None, :]
    m = np.zeros((128, 2, 2, 128), np.float32)
    m[:, 0, 0, :] = (k > q)
    m[:, 0, 1, :] = (k <= q)
    m[:, 1, 1, :] = (k <= q)
    if half == 1:
        m[:, 1, 0, :] = (k > q)
    return m.astype(BF)


def host_swa_exchange(qkv, ncores):
    outs = []
    for c in range(ncores):
        half = c % 2
        kT = np.asarray(qkv[c]["kT"]).reshape(4, 64, T)
        v = np.asarray(qkv[c]["v"]).reshape(T, 4, 64)
        kTf = np.zeros((4, 64, 128 + T), BF)
        kTf[:, :, 128:] = kT
        vfull = np.zeros((128 + T, 4, 64), BF)
        vfull[128:] = v
        if half == 1:
            pk = np.asarray(qkv[c - 1]["kT"]).reshape(4, 64, T)
            pv = np.asarray(qkv[c - 1]["v"]).reshape(T, 4, 64)
            kTf[:, :, :128] = pk[:, :, T - 128:]
            vfull[:128] = pv[T - 128:]
        vaug = np.ones((4, 128, 17, 192), BF)
        vaug[:, :, :, 64:128] = vfull.reshape(17, 128, 4, 64).transpose(2, 1, 0, 3)
        outs.append({"qT": np.asarray(qkv[c]["qT"]).reshape(16, 64, T), "kTf": kTf, "vaug": vaug,
                     "masks": _swa_masks(half)})
    return outs


def run_swa_layer(xTs, pos_c, g, w_in, w_out, sinks):
    nco = len(xTs)
    cst = _consts()
    ims = [dict(cst, xT=xTs[c], perm=make_perm(), g=vec_fm(g, 8), w_in=w_in, pos=pos_c[c]) for c in range(nco)]
    qkv = _run("qkv", build_qkv_launch, ims)
    ex = host_swa_exchange(qkv, nco)
    ims = [dict(cst, xT=xTs[c], sinks=sinks, w_out=w_out, **ex[c]) for c in range(nco)]
    r = _run("swa", build_swa_launch, ims)
    return [np.asarray(r[c]["xo"]) for c in range(nco)]


RET_GAMMA = [float(1.0 - 2.0 ** (-5.0 - h)) for h in range(4)]


def ret_consts():
    lg = np.log(np.asarray(RET_GAMMA, np.float64))
    idx = np.arange(128, dtype=np.float64)
    dec = np.zeros((128, 4, 128), np.float32)
    diff = idx[None, :] - idx[:, None]
    for h in range(4):
        dec[:, h, :] = np.where(diff >= 0, np.exp(np.maximum(diff, 0) * lg[h]), 0.0)
    qdec = np.zeros((128, 4, 128), np.float32)
    for h in range(4):
        qdec[:, h, :] = np.exp((idx + 1.0) * lg[h])[None, :]
    kdec = np.zeros((128, 4), np.float32)
    for h in range(4):
        kdec[:, h] = np.exp((127.0 - idx) * lg[h])
    cdec = [float(np.exp(128.0 * lg[h])) for h in range(4)]
    return dec, qdec, kdec, cdec


def build_ret1_launch(cx=None):
    cx = cx or Ctx()
    S = cx.S
    xT = cx.din("xT", [NCH, 128, T], F32)
    cf = cx.din("cf", [128, 8], F32)
    ones = cx.din("ones", [128, 128], F32)
    ident = cx.din("ident", [128, 128], F32)
    g = cx.din("g", [128, NCH], F32)
    w_in = cx.din("w_in", [D, 6144], F32)
    pos = cx.din("pos", [T], I32)
    qdec_d = cx.din("qdec", [128, 4, 128], F32)
    kdec_d = cx.din("kdec", [128, 4], F32)
    qT = cx.dout("qT", [8, 128, T], BF16)
    qdT = cx.dout("qdT", [8, 128, T], BF16)
    kT = cx.dout("kT", [8, 128, T], BF16)
    kd = cx.dout("kd", [16, 128, 1024], BF16)
    v = cx.dout("v", [16, 128, 2048], BF16)
    sgT = cx.dout("sgT", [16, 128, T], BF16)
    cx.psum_banks(8)
    R = Res(cx)
    R.load_consts(cf, ones)
    R.load_x(xT)
    g_sb = cx.sb("g_sb", [128, NCH], F32)
    gb = Buf("g")
    S.dma("sp", g_sb[:, :], g[:, :], [], [gb], gb)
    id_sb = cx.sb("id_sb", [128, 128], BF16)
    idb = Buf("ident")
    S.dma("pool", id_sb[:, :], ident[:, :], [], [idb], idb)
    qdec = cx.sb("qdec_sb", [128, 4, 128], F32)
    kdec = cx.sb("kdec_sb", [128, 4], F32)
    dcb = Buf("dec")
    S.dma_group("sp", [(qdec[:, :, :], qdec_d[:, :, :]), (kdec[:, :], kdec_d[:, :])], [], [dcb], dcb)
    wbuf = [cx.sb("wblk%d" % i, [128, NCH, 1024], BF16) for i in range(2)]
    wbb = [Buf("wblk%d" % i) for i in range(2)]
    nblk = [0]

    def load_wblock(col0):
        i = nblk[0] % 2
        nblk[0] += 1
        S.dma_group("pool", [(wbuf[i][:, kc, :], w_in[kc * 128:(kc + 1) * 128, col0:col0 + 1024]) for kc in range(NCH)],
                    [], [wbb[i]], wbb[i])
        return i

    pos_i = cx.sb("pos_i", [128, T], I32)
    pos_f = cx.sb("pos_f", [128, T], F32)
    posib = Buf("posi")
    posb = Buf("posf")
    S.dma("sp", pos_i[:, :], pos.partition_broadcast(128), [], [posib], posib)
    S.op("dve", lambda e: e.tensor_copy(out=pos_f[:, :], in_=pos_i[:, :]), reads=[posib], writes=[posb])
    wq = load_wblock(0)
    wk = load_wblock(1024)
    emit_norm(cx, R, g_sb, gb, 0)
    tb = {"u": cx.sb("rp_u", [128, 512], F32), "ub": Buf("rp_u"), "kf": cx.sb("rp_k", [128, 512], F32), "kb": Buf("rp_k"),
          "C": cx.sb("Ctab", [128, 512], F32), "Cb": Buf("C"),
          "S": cx.sb("Stab", [128, 512], F32), "Sb": Buf("S")}
    ta = [cx.sb("rta%d" % i, [128, 512], F32) for i in range(2)]
    tab_ = [Buf("rta%d" % i) for i in range(2)]
    ro = [cx.sb("ro%d" % i, [128, 512], BF16) for i in range(4)]
    rob = [Buf("ro%d" % i) for i in range(4)]
    rod = [cx.sb("rod%d" % i, [128, 512], BF16) for i in range(2)]
    rodb = [Buf("rod%d" % i) for i in range(2)]
    kdt = [cx.sb("kdt%d" % i, [128, 4, 1024], BF16) for i in range(2)]
    kdtb = [Buf("kdt%d" % i) for i in range(2)]
    outb = Buf("r1_out", multi=True)
    cnt = {"ro": 0, "rod": 0, "ta": 0}
    for which, wi in (("q", wq), ("k", wk)):
        if which == "k" and DBG.get("r1", 9) < 2:
            break
        for tt in range(4):
            emit_rope_tables(cx, R, tb, pos_f, posb, R.cf[:, 5:6], 1.0, tt)
            hbufs = [R.hb[k][tt] for k in range(NCH)]
            ki = tt % 2
            for h in range(4):
                pa_t, pa_b = cx.ps[1 + (h % 2) * 2]
                pb_t, pb_b = cx.ps[2 + (h % 2) * 2]
                for (pt, pbuf, c) in ((pa_t, pa_b, 2 * h), (pb_t, pb_b, 2 * h + 1)):
                    def mm(e, pt=pt, c=c, tt=tt, wi=wi):
                        ins = None
                        for k in range(NCH):
                            ins = e.matmul(pt[:, :], lhsT=wbuf[wi][:, k, c * 128:(c + 1) * 128],
                                           rhs=R.hn_sb[:, k, cols(tt, 2)], start=(k == 0), stop=(k == NCH - 1))
                        return ins
                    S.op("pe", mm, reads=hbufs + [wbb[wi]], writes=[pbuf])
                scl = 1.0 if which == "q" else 1.0 / 16.0
                for (X, Xb, Y, Yb, sign, c) in ((pa_t, pa_b, pb_t, pb_b, -1.0, 2 * h), (pb_t, pb_b, pa_t, pa_b, 1.0, 2 * h + 1)):
                    i_ta = cnt["ta"] % 2
                    cnt["ta"] += 1
                    i_ro = cnt["ro"] % 4
                    cnt["ro"] += 1
                    S.op("dve", lambda e, X=X, i_ta=i_ta: e.tensor_tensor(out=ta[i_ta][:, :], in0=X[:, :], in1=tb["C"][:, :], op=ALU.mult),
                         reads=[Xb, tb["Cb"]], writes=[tab_[i_ta]])
                    S.op("dve", lambda e, Y=Y, i_ro=i_ro: e.tensor_tensor(out=ro[i_ro][:, :], in0=Y[:, :], in1=tb["S"][:, :], op=ALU.mult),
                         reads=[Yb, tb["Sb"]], writes=[rob[i_ro]])
                    S.op("dve", lambda e, i_ro=i_ro, i_ta=i_ta, sign=sign: e.scalar_tensor_tensor(
                        out=ta[i_ta][:, :], in0=ro[i_ro][:, :], scalar=float(sign), in1=ta[i_ta][:, :], op0=ALU.mult, op1=ALU.add),
                        reads=[rob[i_ro], tab_[i_ta]], writes=[tab_[i_ta]])
                    S.op("act", lambda e, i_ro=i_ro, i_ta=i_ta, scl=scl: e.activation(
                        out=ro[i_ro][:, :], in_=ta[i_ta][:, :], func=AF.Copy, scale=float(scl)),
                        reads=[tab_[i_ta]], writes=[rob[i_ro]])
                    dst = (qT if which == "q" else kT)[c, :, cols(tt)]
                    S.dma("sp", dst, ro[i_ro][:, :], [rob[i_ro]], [outb], rob[i_ro])
                    if which == "q":
                        i_rd = cnt["rod"] % 2
                        cnt["rod"] += 1
                        S.op("dve", lambda e, i_rd=i_rd, i_ta=i_ta, h=h: e.tensor_tensor(
                            out=rod[i_rd][:, :].rearrange("p (a b) -> p a b", a=4),
                            in0=ta[i_ta][:, :].rearrange("p (a b) -> p a b", a=4),
                            in1=qdec[:, h:h + 1, :].broadcast_to([128, 4, 128]), op=ALU.mult),
                            reads=[tab_[i_ta], dcb], writes=[rodb[i_rd]])
                        S.dma("sp", qdT[c, :, cols(tt)], rod[i_rd][:, :], [rodb[i_rd]], [outb], rodb[i_rd])
                    else:
                        ptr_t, ptr_b = cx.ps[5 + (c % 2)]
                        ptv = ptr_t[:, 0:256].bitcast(BF16)

                        def tr(e, ptv=ptv, i_ro=i_ro):
                            ins = None
                            for a in range(4):
                                ins = e.transpose(ptv[:, a * 128:(a + 1) * 128], ro[i_ro][:, a * 128:(a + 1) * 128], id_sb[:, :])
                            return ins
                        S.op("pe", tr, reads=[rob[i_ro], idb], writes=[ptr_b])
                        S.op("act", lambda e, ptv=ptv, ki=ki, c=c, h=h: e.activation(
                            out=kdt[ki][:, :, c * 128:(c + 1) * 128], in_=ptv.rearrange("p (a b) -> p a b", a=4),
                            func=AF.Copy, scale=kdec[:, h:h + 1]), reads=[ptr_b, dcb], writes=[kdtb[ki]])
            if which == "k":
                S.dma_group("sp", [(kd[tt * 4 + a, :, :], kdt[ki][:, a, :]) for a in range(4)], [kdtb[ki]], [outb], kdtb[ki])
    vo = [cx.sb("vo%d" % i, [128, 512], BF16) for i in range(3)]
    vob = [Buf("vo%d" % i) for i in range(3)]
    nv = 0
    for half in range(2):
        if DBG.get("r1", 9) < 3:
            break
        wi = load_wblock(2048 + half * 1024)
        for n in range(DBG.get("r1v", 16)):
            tt = n // 4
            hbufs = [R.hb[k][tt] for k in range(NCH)]
            for hh in range(2):
                pv_t, pv_b = cx.ps[1 + (nv % 4)]
                iv = nv % 3
                nv += 1

                def mmv(e, pv_t=pv_t, n=n, hh=hh, wi=wi):
                    ins = None
                    for k in range(NCH):
                        ins = e.matmul(pv_t[:, :], lhsT=R.hn_sb[:, k, 2 + n * 128:2 + (n + 1) * 128],
                                       rhs=wbuf[wi][:, k, hh * 512:(hh + 1) * 512], start=(k == 0), stop=(k == NCH - 1))
                    return ins
                S.op("pe", mmv, reads=hbufs + [wbb[wi]], writes=[pv_b])
                S.op("act", lambda e, iv=iv, pv_t=pv_t: e.activation(out=vo[iv][:, :], in_=pv_t[:, :], func=AF.Copy),
                     reads=[pv_b], writes=[vob[iv]])
                hd = half * 2 + hh
                S.dma("sp", v[n, :, hd * 512:(hd + 1) * 512], vo[iv][:, :], [vob[iv]], [outb], vob[iv])
    for half in range(2):
        if DBG.get("r1", 9) < 4:
            break
        wi = load_wblock(4096 + half * 1024)
        for tt in range(4):
            hbufs = [R.hb[k][tt] for k in range(NCH)]
            for cc in range(8):
                pg_t, pg_b = cx.ps[1 + (nv % 4)]
                iv = nv % 3
                nv += 1

                def mmg(e, pg_t=pg_t, cc=cc, tt=tt, wi=wi):
                    ins = None
                    for k in range(NCH):
                        ins = e.matmul(pg_t[:, :], lhsT=wbuf[wi][:, k, cc * 128:(cc + 1) * 128],
                                       rhs=R.hn_sb[:, k, cols(tt, 2)], start=(k == 0), stop=(k == NCH - 1))
                    return ins
                S.op("pe", mmg, reads=hbufs + [wbb[wi]], writes=[pg_b])
                S.op("act", lambda e, iv=iv, pg_t=pg_t: e.activation(out=vo[iv][:, :], in_=pg_t[:, :], func=AF.Silu),
                     reads=[pg_b], writes=[vob[iv]])
                S.dma("sp", sgT[half * 8 + cc, :, cols(tt)], vo[iv][:, :], [vob[iv]], [outb], vob[iv])
    S.final_wait("sp", [outb])
    S.emit()
    return cx


class RetState:
    def __init__(self, cx):
        self.Rf = cx.sb("Rf", [128, 4, 2, 512], F32)
        self.Rb = cx.sb("Rb", [128, 4, 2, 512], BF16)
        self.Rfb = [[Buf("Rf%d_%d" % (h, dc)) for dc in range(2)] for h in range(4)]
        self.Rbb = [[Buf("Rb%d_%d" % (h, dc)) for dc in range(2)] for h in range(4)]
        self.n = 0


def emit_ret_state_update(cx, St, kd_n, kdb, v_n, vb, banks, cdec, want_bf=True):
    S = cx.S
    for h in range(4):
        for dc in range(2):
            pt, pb = cx.ps[banks[St.n % len(banks)]]
            St.n += 1
            S.op("pe", lambda e, pt=pt, h=h, dc=dc: e.matmul(
                pt[:, :], lhsT=kd_n[:, h * 256 + dc * 128:h * 256 + (dc + 1) * 128], rhs=v_n[:, h * 512:(h + 1) * 512],
                start=True, stop=True), reads=[kdb, vb], writes=[pb])
            S.op("dve", lambda e, pt=pt, h=h, dc=dc: e.scalar_tensor_tensor(
                out=St.Rf[:, h, dc, :], in0=St.Rf[:, h, dc, :], scalar=float(cdec[h]), in1=pt[:, :],
                op0=ALU.mult, op1=ALU.add), reads=[pb, St.Rfb[h][dc]], writes=[St.Rfb[h][dc]])
            if want_bf:
                S.op("act", lambda e, h=h, dc=dc: e.activation(out=St.Rb[:, h, dc, :], in_=St.Rf[:, h, dc, :], func=AF.Copy),
                     reads=[St.Rfb[h][dc]], writes=[St.Rbb[h][dc]])


def build_ret1b_launch(cx=None):
    cx = cx or Ctx()
    S = cx.S
    kd = cx.din("kd", [16, 128, 1024], BF16)
    v = cx.din("v", [16, 128, 2048], BF16)
    Rend = cx.dout("Rend", [4, 2, 128, 512], F32)
    cx.psum_banks(8)
    St = RetState(cx)
    _, _, _, cdec = ret_consts()
    for h in range(4):
        for dc in range(2):
            S.op("dve", lambda e, h=h, dc=dc: e.memset(St.Rf[:, h, dc, :], 0.0), reads=[], writes=[St.Rfb[h][dc]])
    kdn = [cx.sb("kdn%d" % i, [128, 1024], BF16) for i in range(2)]
    kdnb = [Buf("kdn%d" % i) for i in range(2)]
    vn = [cx.sb("vn%d" % i, [128, 2048], BF16) for i in range(2)]
    vnb = [Buf("vn%d" % i) for i in range(2)]
    for n in range(16):
        i = n % 2
        S.dma("sp", kdn[i][:, :], kd[n, :, :], [], [kdnb[i]], kdnb[i])
        S.dma("sp", vn[i][:, :], v[n, :, :], [], [vnb[i]], vnb[i])
        emit_ret_state_update(cx, St, kdn[i], kdnb[i], vn[i], vnb[i], [0, 1, 2, 3], cdec, want_bf=False)
    outb = Buf("rend_out", multi=True)
    S.dma_group("sp", [(Rend[h, dc, :, :], St.Rf[:, h, dc, :]) for h in range(4) for dc in range(2)],
                [St.Rfb[h][dc] for h in range(4) for dc in range(2)], [outb], outb)
    S.final_wait("sp", [outb])
    S.emit()
    return cx


def build_ret2_launch(cx=None):
    cx = cx or Ctx()
    S = cx.S
    xT = cx.din("xT", [NCH, 128, T], F32)
    cf = cx.din("cf", [128, 8], F32)
    ones = cx.din("ones", [128, 128], F32)
    qT = cx.din("qT", [8, 128, T], BF16)
    qdT = cx.din("qdT", [8, 128, T], BF16)
    kT = cx.din("kT", [8, 128, T], BF16)
    kd = cx.din("kd", [16, 128, 1024], BF16)
    v = cx.din("v", [16, 128, 2048], BF16)
    sgT = cx.din("sgT", [16, 128, T], BF16)
    r0mode = cx.opts.get("r0", "input")
    R0 = cx.din("R0", [4, 2, 128, 512], F32) if r0mode != "zero" else None
    Rend_o = cx.dout("Rend", [4, 2, 128, 512], F32) if cx.opts.get("fused") else None
    dec_d = cx.din("dec", [128, 4, 128], F32)
    gn_d = cx.din("gn", [128, 16], F32)
    w_out = cx.din("w_out", [2048, D], F32)
    xo = cx.dout("xo", [NCH, 128, T], F32)
    cx.psum_banks(8)
    R = Res(cx, with_hn=False)
    R.load_consts(cf, ones)
    R.load_x(xT)
    St = RetState(cx)
    _, _, _, cdec = ret_consts()
    if r0mode == "zero":
        for h in range(4):
            for dc in range(2):
                S.op("dve", lambda e, h=h, dc=dc: e.memset(St.Rf[:, h, dc, :], 0.0), reads=[], writes=[St.Rfb[h][dc]])
    else:
        S.dma_group("sp", [(St.Rf[:, h, dc, :], R0[h, dc, :, :]) for h in range(4) for dc in range(2)],
                    [], [St.Rfb[h][dc] for h in range(4) for dc in range(2)], Buf("R0ld"))
    for h in range(4):
        for dc in range(2):
            S.op("act", lambda e, h=h, dc=dc: e.activation(out=St.Rb[:, h, dc, :], in_=St.Rf[:, h, dc, :], func=AF.Copy),
                 reads=[St.Rfb[h][dc]], writes=[St.Rbb[h][dc]])
    dec = cx.sb("dec_sb", [128, 4, 128], F32)
    gn = cx.sb("gn_sb", [128, 16], F32)
    dcb = Buf("dec")
    S.dma_group("sp", [(dec[:, :, :], dec_d[:, :, :]), (gn[:, :], gn_d[:, :])], [], [dcb], dcb)
    w_sb = cx.sb("wo_sb", [128, 16, D], BF16)
    wb = Buf("w_out")
    S.dma_group("pool", [(w_sb[:, kc, :], w_out[kc * 128:(kc + 1) * 128, :]) for kc in range(16)], [], [wb], wb)
    qn = [cx.sb("qn%d" % i, [128, 8, 128], BF16) for i in range(2)]
    qdn = [cx.sb("qdn%d" % i, [128, 8, 128], BF16) for i in range(2)]
    kn = [cx.sb("kn%d" % i, [128, 8, 128], BF16) for i in range(2)]
    kdn = [cx.sb("kdn%d" % i, [128, 1024], BF16) for i in range(2)]
    vn = [cx.sb("vn%d" % i, [128, 2048], BF16) for i in range(2)]
    sgn = [cx.sb("sgn%d" % i, [128, 16, 128], BF16) for i in range(2)]
    inb = [Buf("rin%d" % i) for i in range(2)]
    scm = [cx.sb("scm%d" % i, [128, 4, 128], BF16) for i in range(2)]
    scmb = [Buf("scm%d" % i) for i in range(2)]
    ot = [cx.sb("ot%d" % i, [128, 4, 128], BF16) for i in range(2)]
    otb = [Buf("ot%d" % i) for i in range(2)]
    osq = [cx.sb("osq%d" % i, [128, 4, 128], BF16) for i in range(2)]
    osqb = [Buf("osq%d" % i) for i in range(2)]
    stt = [cx.sb("stt%d" % i, [128, 2, 128], F32) for i in range(2)]
    sttb = [Buf("stt%d" % i) for i in range(2)]
    tmp = [cx.sb("gtmp%d" % i, [128, 4, 128], F32) for i in range(2)]
    tmpb = [Buf("gtmp%d" % i) for i in range(2)]
    yT = [cx.sb("yT%d" % i, [128, 16, 512], BF16) for i in range(2)]
    yTb = [[[Buf("yT%d_%d_%d" % (i, h, a)) for a in range(4)] for h in range(4)] for i in range(2)]
    nh = 0
    ny = 0
    for n in range(16):
        i = n % 2
        tt = n // 4
        a = n % 4
        yi = tt % 2
        csl = slice(n * 128, (n + 1) * 128)
        S.dma_group("sp", [(qn[i][:, :, :], qT[:, :, csl].rearrange("c p t -> p c t")),
                           (qdn[i][:, :, :], qdT[:, :, csl].rearrange("c p t -> p c t")),
                           (kn[i][:, :, :], kT[:, :, csl].rearrange("c p t -> p c t")),
                           (kdn[i][:, :], kd[n, :, :]), (vn[i][:, :], v[n, :, :]),
                           (sgn[i][:, :, :], sgT[:, :, csl].rearrange("c p t -> p c t"))],
                    [], [inb[i]], inb[i])
        ps_t, ps_b = cx.ps[0]

        def mmsc(e, i=i):
            ins = None
            for h in range(4):
                for dc in range(2):
                    ins = e.matmul(ps_t[:, h * 128:(h + 1) * 128], lhsT=kn[i][:, 2 * h + dc, :], rhs=qn[i][:, 2 * h + dc, :],
                                   start=(dc == 0), stop=(dc == 1))
            return ins
        S.op("pe", mmsc, reads=[inb[i]], writes=[ps_b])
        S.op("dve", lambda e, i=i: e.tensor_tensor(out=scm[i][:, :, :], in0=ps_t[:, :].rearrange("p (h q) -> p h q", h=4),
                                                 in1=dec[:, :, :], op=ALU.mult), reads=[ps_b, dcb], writes=[scmb[i]])
        for h in range(4):
            po_t, po_b = cx.ps[1 + (nh % 2)]
            pst_t, pst_b = cx.ps[3 + (nh % 2)]
            j2 = nh % 2
            nh += 1

            def mmo(e, po_t=po_t, i=i, h=h):
                ins = None
                for ec in range(4):
                    ins = e.matmul(po_t[:, ec * 128:(ec + 1) * 128], lhsT=vn[i][:, h * 512 + ec * 128:h * 512 + (ec + 1) * 128],
                                   rhs=scm[i][:, h, :], start=True, stop=False)
                    for dc in range(2):
                        ins = e.matmul(po_t[:, ec * 128:(ec + 1) * 128], lhsT=St.Rb[:, h, dc, ec * 128:(ec + 1) * 128],
                                       rhs=qdn[i][:, 2 * h + dc, :], start=False, stop=(dc == 1))
                return ins
            S.op("pe", mmo, reads=[inb[i], scmb[i], St.Rbb[h][0], St.Rbb[h][1]], writes=[po_b])
            o4 = po_t[:, :].rearrange("p (c q) -> p c q", c=4)
            S.op("act", lambda e, j2=j2, o4=o4: e.activation(out=ot[j2][:, :, :], in_=o4, func=AF.Copy),
                 reads=[po_b], writes=[otb[j2]])
            S.op("act", lambda e, j2=j2: e.activation(out=osq[j2][:, :, :], in_=ot[j2][:, :, :], func=AF.Square),
                 reads=[otb[j2]], writes=[osqb[j2]])

            def mmst(e, pst_t=pst_t, j2=j2):
                ins = None
                for ec in range(4):
                    ins = e.matmul(pst_t[:, 0:128], lhsT=R.ones[:, :], rhs=ot[j2][:, ec, :], start=(ec == 0), stop=(ec == 3))
                for ec in range(4):
                    ins = e.matmul(pst_t[:, 128:256], lhsT=R.ones[:, :], rhs=osq[j2][:, ec, :], start=(ec == 0), stop=(ec == 3))
                return ins
            S.op("pe", mmst, reads=[otb[j2], osqb[j2], R.onesb], writes=[pst_b])
            st = stt[j2]
            S.op("dve", lambda e, st=st, pst_t=pst_t: e.tensor_copy(out=st[:, :, :], in_=pst_t[:, 0:256].rearrange("p (a q) -> p a q", a=2)),
                 reads=[pst_b], writes=[sttb[j2]])
            S.op("dve", lambda e, st=st, j2=j2: e.tensor_tensor(out=tmp[j2][:, 0, :], in0=st[:, 0, :], in1=st[:, 0, :], op=ALU.mult),
                 reads=[sttb[j2]], writes=[tmpb[j2]])
            S.op("dve", lambda e, st=st, j2=j2: e.tensor_tensor(out=st[:, 1, :], in0=st[:, 1, :], in1=tmp[j2][:, 0, :], op=ALU.subtract),
                 reads=[sttb[j2], tmpb[j2]], writes=[sttb[j2]])
            S.op("act", lambda e, st=st: e.activation(out=st[:, 1, :], in_=st[:, 1, :], func=AF.Sqrt, bias=R.cf[:, 0:1], scale=1.0),
                 reads=[sttb[j2], R.epsb], writes=[sttb[j2]])
            S.op("dve", lambda e, st=st: e.reciprocal(out=st[:, 1, :], in_=st[:, 1, :]), reads=[sttb[j2]], writes=[sttb[j2]])
            S.op("dve", lambda e, st=st, j2=j2: e.tensor_tensor(out=tmp[j2][:, :, :], in0=ot[j2][:, :, :],
                                                              in1=st[:, 0:1, :].broadcast_to([128, 4, 128]), op=ALU.subtract),
                 reads=[otb[j2], sttb[j2]], writes=[tmpb[j2]])
            S.op("dve", lambda e, st=st, j2=j2: e.tensor_tensor(out=tmp[j2][:, :, :], in0=tmp[j2][:, :, :],
                                                              in1=st[:, 1:2, :].broadcast_to([128, 4, 128]), op=ALU.mult),
                 reads=[tmpb[j2], sttb[j2]], writes=[tmpb[j2]])
            for ec in range(4):
                S.op("dve", lambda e, j2=j2, ec=ec, h=h, i=i, yi=yi, a=a: e.scalar_tensor_tensor(
                    out=yT[yi][:, 4 * h + ec, a * 128:(a + 1) * 128], in0=tmp[j2][:, ec, :], scalar=gn[:, 4 * h + ec:4 * h + ec + 1],
                    in1=sgn[i][:, 4 * h + ec, :], op0=ALU.mult, op1=ALU.mult),
                    reads=[tmpb[j2], dcb, inb[i]], writes=[yTb[yi][h][a]])
        emit_ret_state_update(cx, St, kdn[i], inb[i], vn[i], inb[i], [5, 6], cdec, want_bf=True)
        if a == 3:
            for m in range(NCH):
                py_t, py_b = cx.ps[7]

                def mmy(e, m=m, yi=yi, py_t=py_t):
                    ins = None
                    for c in range(16):
                        ins = e.matmul(py_t[:, :], lhsT=w_sb[:, c, m * 128:(m + 1) * 128], rhs=yT[yi][:, c, :],
                                       start=(c == 0), stop=(c == 15))
                    return ins
                S.op("pe", mmy, reads=[yTb[yi][h][a2] for h in range(4) for a2 in range(4)] + [wb], writes=[py_b])
                xo_ = R.x_sb[:, m, cols(tt)]
                S.op("dve", lambda e, xo_=xo_, py_t=py_t: e.tensor_tensor(out=xo_, in0=xo_, in1=py_t[:, :], op=ALU.add),
                     reads=[py_b, R.xb[m][tt]], writes=[R.xb[m][tt]])
    R.store_x(xo)
    if Rend_o is not None:
        S.dma_group("sp", [(Rend_o[h, dc, :, :], St.Rf[:, h, dc, :]) for h in range(4) for dc in range(2)],
                    [St.Rfb[h][dc] for h in range(4) for dc in range(2)], [R.outb], Buf("rend_o"))
    S.final_wait("sp", [R.outb])
    S.emit()
    return cx


def run_ret_layer(xTs, pos_c, g, w_in, w_out, gn_g):
    nco = len(xTs)
    cst = _consts()
    dec, qdec, kdec, cdec = ret_consts()
    ims = [dict(cst, xT=xTs[c], ident=np.eye(128, dtype=np.float32), g=vec_fm(g, 8), w_in=w_in, pos=pos_c[c],
                qdec=qdec, kdec=kdec) for c in range(nco)]
    r1 = _run("ret1", build_ret1_launch, ims)
    rb = _run("ret1b", build_ret1b_launch, [{"kd": np.asarray(r1[c]["kd"]), "v": np.asarray(r1[c]["v"])} for c in range(nco)])
    ims = []
    for c in range(nco):
        R0 = np.asarray(rb[c - 1]["Rend"]) if c % 2 == 1 else np.zeros((4, 2, 128, 512), np.float32)
        ims.append({"cf": cst["cf"], "ones": np.full((128, 128), 1.0 / 512, np.float32), "xT": xTs[c],
                    "qT": np.asarray(r1[c]["qT"]), "qdT": np.asarray(r1[c]["qdT"]), "kT": np.asarray(r1[c]["kT"]),
                    "kd": np.asarray(r1[c]["kd"]), "v": np.asarray(r1[c]["v"]), "sgT": np.asarray(r1[c]["sgT"]),
                    "R0": R0, "dec": dec, "gn": vec_fm(gn_g, 16), "w_out": w_out})
    r2 = _run("ret2", build_ret2_launch, ims)
    return [np.asarray(r2[c]["xo"]) for c in range(nco)]


NEG_BIG = -30000.0


def build_moba_launch(cx=None):
    cx = cx or Ctx()
    S = cx.S
    NKT = 32
    fused = cx.opts.get("fused", False)
    NG = 4 if fused else 2
    if not fused:
        qTc = cx.din("qTc", [8, 64, SEQ], BF16)
        kA = cx.din("kA", [2, 80, SEQ], BF16)
        vaug = cx.din("vaug", [2, 128, NKT, 192], BF16)
        oTc = cx.dout("oTc", [4, 128, SEQ], BF16)
    else:
        qH = [cx.io["qT_%d" % hf].rearrange("c (h d) t -> (c h) d t", h=2) for hf in range(2)]
        kH = [cx.io["kT_%d" % hf].rearrange("c (h d) t -> (c h) d t", h=2) for hf in range(2)]
        vH = [cx.io["v_%d" % hf].rearrange("n p (g d) -> n p g d", g=4) for hf in range(2)]
        oH = [cx.io["oT_%d" % hf] for hf in range(2)]
        onehot = cx.io["onehot"]
    ident = cx.din("ident", [128, 128], F32)
    sb1_d = cx.din("sb1", [128, 16, 16], F32)
    cmask = cx.din("cmask", [128, 2, 256], BF16)
    cx.psum_banks(8)
    id_sb = cx.sb("id_sb", [128, 128], BF16)
    idb = Buf("ident")
    S.dma("pool", id_sb[:, :], ident[:, :], [], [idb], idb)
    sb1 = cx.sb("sb1_sb", [128, 16, 16], F32)
    cm = cx.sb("cm_sb", [128, 2, 256], BF16)
    cb_ = Buf("mconst")
    S.dma_group("sp", [(sb1[:, :, :], sb1_d[:, :, :]), (cm[:, :, :], cmask[:, :, :])], [], [cb_], cb_)
    kAs = [cx.sb("kA%d" % i, [80, SEQ], BF16) for i in range(NG)]
    kAb = [Buf("kA%d" % i) for i in range(NG)]
    vg = [cx.sb("vg%d" % i, [128, NKT, 192], BF16) for i in range(NG)]
    vgb = [Buf("vg%d" % i) for i in range(NG)]
    for gl in range(NG):
        if not fused:
            S.dma("sp", kAs[gl][:, :], kA[gl, :, :], [], [kAb[gl]], kAb[gl])
            S.dma("sp", vg[gl][:, :, :], vaug[gl, :, :, :], [], [vgb[gl]], vgb[gl])
        else:
            S.dma_group("sp", [(kAs[gl][0:64, hf * T:(hf + 1) * T], kH[hf][gl, :, :]) for hf in range(2)]
                        + [(kAs[gl][64:80, :], onehot[:, :])], [], [kAb[gl]], kAb[gl])
            S.op("dve", lambda e, gl=gl: e.memset(vg[gl][:, :, :], 1.0), reads=[], writes=[vgb[gl]])
            S.dma_group("sp", [(vg[gl][:, hf * 16 + 4 * a:hf * 16 + 4 * a + 4, 64:128],
                                vH[hf][4 * a:4 * a + 4, :, gl, :].rearrange("n p d -> p n d"))
                               for hf in range(2) for a in range(4)], [], [vgb[gl]], vgb[gl])
    kms = cx.sb("kms", [64, NG, 16], F32)
    kmT = cx.sb("kmT", [64, NG, 16], BF16)
    kmb = Buf("km")
    for gl in range(NG):
        S.op("dve", lambda e, gl=gl: e.tensor_reduce(out=kms[:, gl, :], in_=kAs[gl][0:64, :].rearrange("p (n k) -> p n k", k=256),
                                                    axis=AX.X, op=ALU.add), reads=[kAb[gl]], writes=[kmb])
    S.op("act", lambda e: e.activation(out=kmT[:, :, :], in_=kms[:, :, :], func=AF.Copy, scale=1.0 / 256.0),
         reads=[kmb], writes=[kmb])
    QA = [cx.sb("QA%d" % i, [80, 4, 256], BF16) for i in range(2)]
    QAq = [Buf("QAq%d" % i) for i in range(2)]
    QAbias = [Buf("QAb%d" % i) for i in range(2)]
    gm = cx.sb("gm", [128, 8, 16], F32)
    gmb = Buf("gm")
    m8 = cx.sb("m8", [128, 8, 8], F32)
    m8b = Buf("m8")
    bq = [cx.sb("bq%d" % i, [128, 2, 4, 32], BF16) for i in range(2)]
    bqb = [Buf("bq%d" % i) for i in range(2)]
    for i in range(2):
        S.op("dve", lambda e, i=i: e.memset(bq[i][:, :, :, :], 0.0), reads=[], writes=[bqb[i]])
    P = [cx.sb("P%d" % i, [128, 2, 256], BF16) for i in range(4)]
    Pb = [Buf("P%d" % i) for i in range(4)]
    rec = [cx.sb("rec%d" % i, [128, 2, 256], F32) for i in range(2)]
    recb = [Buf("rec%d" % i) for i in range(2)]
    oo = [cx.sb("oo%d" % i, [128, 2, 256], BF16) for i in range(2)]
    oob = [Buf("oo%d" % i) for i in range(2)]
    outb = Buf("moba_out", multi=True)
    it = 0
    nps = 0
    for qb in range(16):
        for gl in range(NG):
            qi = it % 2
            it += 1
            if not fused:
                qsrc = [qTc[4 * gl + hl, :, qb * 256:(qb + 1) * 256] for hl in range(4)]
            else:
                qsrc = [qH[qb // 8][4 * gl + hl, :, (qb % 8) * 256:(qb % 8 + 1) * 256] for hl in range(4)]
            S.dma_group("sp", [(QA[qi][0:64, hl, :], qsrc[hl]) for hl in range(4)],
                        [], [QAq[qi]], QAq[qi])
            pg_t, pg_b = cx.ps[7]

            def mmg(e, qi=qi, gl=gl):
                ins = None
                for qt in range(2):
                    for hl in range(4):
                        idx = qt * 4 + hl
                        ins = e.matmul(pg_t[:, idx * 16:(idx + 1) * 16], lhsT=QA[qi][0:64, hl, qt * 128:(qt + 1) * 128],
                                       rhs=kmT[:, gl, :], start=True, stop=True)
                return ins
            S.op("pe", mmg, reads=[QAq[qi], kmb], writes=[pg_b])
            S.op("dve", lambda e, qb=qb: e.tensor_tensor(out=gm[:, :, :], in0=pg_t[:, 0:128].rearrange("p (a n) -> p a n", a=8),
                                                        in1=sb1[:, qb:qb + 1, :].broadcast_to([128, 8, 16]), op=ALU.add),
                 reads=[pg_b, cb_], writes=[gmb])
            for idx in range(8):
                S.op("dve", lambda e, idx=idx: e.max(out=m8[:, idx, :], in_=gm[:, idx, :]), reads=[gmb], writes=[m8b])
            for idx in range(8):
                S.op("dve", lambda e, idx=idx, qi=qi: e.tensor_scalar(
                    out=bq[qi][:, idx // 4, idx % 4, 0:16], in0=gm[:, idx, :], scalar1=m8[:, idx, 2:3], scalar2=NEG_BIG,
                    op0=ALU.is_lt, op1=ALU.mult), reads=[gmb, m8b], writes=[bqb[qi]])
            S.op("dve", lambda e, qi=qi, qb=qb: e.memset(bq[qi][:, :, :, qb:qb + 1], 0.0), reads=[], writes=[bqb[qi]])
            ptr_t, ptr_b = cx.ps[0]
            ptv = ptr_t[:, 0:128].bitcast(BF16)

            def tr(e, qi=qi, ptv=ptv):
                ins = None
                for qt in range(2):
                    ins = e.transpose(ptv[:, qt * 128:(qt + 1) * 128], bq[qi][:, qt, :, :].rearrange("p h n -> p (h n)"), id_sb[:, :])
                return ins
            S.op("pe", tr, reads=[bqb[qi], idb], writes=[ptr_b])
            for hl in range(4):
                S.op("dve", lambda e, hl=hl, qi=qi, ptv=ptv: e.tensor_copy(out=QA[qi][64:80, hl, :], in_=ptv[32 * hl:32 * hl + 16, :]),
                     reads=[ptr_b], writes=[QAbias[qi]])
            njt = 2 * (qb + 1)
            for j in range(njt):
                own = (j // 2 == qb)
                for par in range(2):
                    ps_t, ps_b = cx.ps[1 + (nps % 4)]
                    pi = nps % 4
                    nps += 1
                    po_t, po_b = cx.ps[5 + par]

                    def mmqk(e, ps_t=ps_t, par=par, j=j, gl=gl, qi=qi):
                        ins = None
                        for hh in range(2):
                            ins = e.matmul(ps_t[:, hh * 256:(hh + 1) * 256], lhsT=kAs[gl][0:80, j * 128:(j + 1) * 128],
                                           rhs=QA[qi][0:80, 2 * hh + par, :], start=True, stop=True)
                        return ins
                    S.op("pe", mmqk, reads=[kAb[gl], QAq[qi], QAbias[qi]], writes=[ps_b])
                    Pv = P[pi]
                    S.op("act", lambda e, Pv=Pv, ps_t=ps_t: e.activation(
                        out=Pv[:, :, :], in_=ps_t[:, :].rearrange("p (h q) -> p h q", h=2), func=AF.Exp, scale=0.125),
                        reads=[ps_b], writes=[Pb[pi]])
                    if own:
                        kt = j % 2
                        S.op("pool", lambda e, Pv=Pv, kt=kt: e.tensor_tensor(
                            out=Pv[:, :, :], in0=Pv[:, :, :], in1=cm[:, kt:kt + 1, :].broadcast_to([128, 2, 256]), op=ALU.mult),
                            reads=[Pb[pi], cb_], writes=[Pb[pi]])
                    vsl = slice(64, 192) if par == 0 else slice(0, 128)
                    S.op("pe", lambda e, po_t=po_t, Pv=Pv, vsl=vsl, j=j, gl=gl, njt=njt: e.matmul(
                        po_t[:, :], lhsT=vg[gl][:, j, vsl], rhs=Pv[:, :, :].rearrange("p h q -> p (h q)"),
                        start=(j == 0), stop=(j == njt - 1)), reads=[Pb[pi], vgb[gl]], writes=[po_b])
            oi = it % 2
            for par in range(2):
                po_t, po_b = cx.ps[5 + par]
                nlo, dlo = (0, 64) if par == 0 else (64, 0)
                num = po_t[nlo:nlo + 64, :].rearrange("p (h q) -> p h q", h=2)
                den = po_t[dlo:dlo + 64, :].rearrange("p (h q) -> p h q", h=2)
                rc = rec[par][nlo:nlo + 64, :, :]
                S.op("dve", lambda e, rc=rc, den=den: e.reciprocal(out=rc, in_=den), reads=[po_b], writes=[recb[par]])
                S.op("dve", lambda e, num=num, rc=rc, oi=oi, nlo=nlo: e.tensor_tensor(
                    out=oo[oi][nlo:nlo + 64, :, :], in0=num, in1=rc, op=ALU.mult),
                    reads=[po_b, recb[par]], writes=[oob[oi]])
            if not fused:
                odst = [oTc[2 * gl + hh, :, qb * 256:(qb + 1) * 256] for hh in range(2)]
            else:
                odst = [oH[qb // 8][2 * gl + hh, :, (qb % 8) * 256:(qb % 8 + 1) * 256] for hh in range(2)]
            S.dma_group("sp", [(odst[hh], oo[oi][:, hh, :]) for hh in range(2)],
                        [oob[oi]], [outb], oob[oi])
    S.final_wait("sp", [outb])
    S.emit()
    return cx


def build_outproj_launch(cx=None):
    cx = cx or Ctx()
    S = cx.S
    xT = cx.din("xT", [NCH, 128, T], F32)
    cf = cx.din("cf", [128, 8], F32)
    ones = cx.din("ones", [128, 128], F32)
    oT_d = cx.din("oT", [NCH, 128, T], BF16)
    w_out = cx.din("w_out", [D, D], F32)
    xo = cx.dout("xo", [NCH, 128, T], F32)
    cx.psum_banks(8)
    R = Res(cx, with_hn=False)
    R.load_consts(cf, ones)
    R.load_x(xT)
    w_sb = cx.sb("wo_sb", [128, NCH, D], BF16)
    wb = Buf("w_out")
    S.dma_group("pool", [(w_sb[:, kc, :], w_out[kc * 128:(kc + 1) * 128, :]) for kc in range(NCH)], [], [wb], wb)
    oT = cx.sb("oT_sb", [128, NCH, T], BF16)
    oTb = [[Buf("oT%d_%d" % (c, tt)) for tt in range(4)] for c in range(NCH)]
    oTld = [Buf("oTld%d" % tt) for tt in range(4)]
    for tt in range(4):
        S.dma_group("sp", [(oT[:, c, cols(tt)], oT_d[c, :, cols(tt)]) for c in range(NCH)], [],
                    [oTb[c][tt] for c in range(NCH)], oTld[tt])
    emit_outproj(cx, R, oT, oTb, w_sb, wb, NCH, [1, 2, 3, 4])
    R.store_x(xo)
    S.final_wait("sp", [R.outb])
    S.emit()
    return cx


def moba_consts():
    sb1 = np.zeros((128, 16, 16), np.float32)
    for qb in range(16):
        sb1[:, qb, qb:] = -1e30
    k = np.arange(128)[:, None]
    q = np.arange(256)[None, :]
    cm = np.zeros((128, 2, 256), np.float32)
    cm[:, 0, :] = (k <= q)
    cm[:, 1, :] = (k + 128 <= q)
    return sb1, cm.astype(BF)


def host_moba_exchange(qkv, ncores):
    outs = []
    onehot = np.zeros((16, SEQ), np.float32)
    for n in range(16):
        onehot[n, n * 256:(n + 1) * 256] = 1.0
    for c in range(ncores):
        half = c % 2
        c0 = c - half
        q_all = np.concatenate([np.asarray(qkv[c0 + hh]["qT"]).reshape(16, 64, T) for hh in range(2)], axis=2)
        k_all = np.concatenate([np.asarray(qkv[c0 + hh]["kT"]).reshape(4, 64, T) for hh in range(2)], axis=2)
        v_all = np.concatenate([np.asarray(qkv[c0 + hh]["v"]).reshape(T, 4, 64) for hh in range(2)], axis=0)
        kA = np.zeros((2, 80, SEQ), BF)
        vaug = np.ones((2, 128, 32, 192), BF)
        for gl in range(2):
            g = 2 * half + gl
            kA[gl, 0:64] = k_all[g]
            kA[gl, 64:80] = onehot.astype(BF)
            vaug[gl, :, :, 64:128] = v_all[:, g, :].reshape(32, 128, 64).transpose(1, 0, 2)
        outs.append({"qTc": np.ascontiguousarray(q_all[8 * half:8 * half + 8]), "kA": kA, "vaug": vaug})
    return outs


def run_moba_layer(xTs, pos_c, g, w_in, w_out):
    nco = len(xTs)
    cst = _consts()
    ims = [dict(cst, xT=xTs[c], perm=make_perm(), g=vec_fm(g, 8), w_in=w_in, pos=pos_c[c]) for c in range(nco)]
    qkv = _run("qkv", build_qkv_launch, ims)
    ex = host_moba_exchange(qkv, nco)
    sb1, cm = moba_consts()
    ims = [dict(ex[c], ident=np.eye(128, dtype=np.float32), sb1=sb1, cmask=cm) for c in range(nco)]
    r = _run("moba", build_moba_launch, ims)
    ims = []
    for c in range(nco):
        half = c % 2
        c0 = c - half
        oT = np.concatenate([np.asarray(r[c0 + hh]["oTc"])[:, :, half * T:(half + 1) * T] for hh in range(2)], axis=0)
        ims.append(dict(cst, xT=xTs[c], oT=np.ascontiguousarray(oT), w_out=w_out))
    r2 = _run("outproj", build_outproj_launch, ims)
    return [np.asarray(r2[c]["xo"]) for c in range(nco)]


def build_fnorm_launch(cx=None):
    cx = cx or Ctx()
    S = cx.S
    xT = cx.din("xT", [NCH, 128, T], F32)
    cf = cx.din("cf", [128, 8], F32)
    ones = cx.din("ones", [128, 128], F32)
    g = cx.din("g", [128, NCH], F32)
    xo = cx.dout("xo", [NCH, 128, T], F32)
    cx.psum_banks(8)
    R = Res(cx)
    R.load_consts(cf, ones)
    R.load_x(xT)
    g_sb = cx.sb("g_sb", [128, NCH], F32)
    gb = Buf("g")
    S.dma("sp", g_sb[:, :], g[:, :], [], [gb], gb)
    emit_norm(cx, R, g_sb, gb, 0, out_f32=True)
    R.store_x(xo)
    S.final_wait("sp", [R.outb])
    S.emit()
    return cx


def run_ffn_layer(xTs, g, w_a, w_b, conv_w, conv_b, w_down):
    nco = len(xTs)
    cst = _consts()
    cw = np.ascontiguousarray(conv_w.reshape(3, NFF, 128).transpose(2, 0, 1))
    cb = np.ascontiguousarray(conv_b.reshape(NFF, 128).T)
    ims = []
    for c in range(nco):
        xh = np.zeros((128, NCH, 2), np.float32)
        if c % 2 == 1:
            xh = np.ascontiguousarray(xTs[c - 1][:, :, T - 2:].transpose(1, 0, 2))
        ims.append(dict(cst, xT=xTs[c], xh=xh, w_a=w_a, w_b=w_b, w_down=w_down, cw=cw, cb=cb, g=vec_fm(g, 8)))
    r = _run("ffn", build_ffn_launch, ims)
    return [np.asarray(r[c]["xo"]) for c in range(nco)]


def run_fnorm(xTs, g):
    cst = _consts()
    r = _run("fnorm", build_fnorm_launch, [dict(cst, xT=xTs[c], g=vec_fm(g, 8)) for c in range(len(xTs))])
    return [np.asarray(r[c]["xo"]) for c in range(len(xTs))]


def kernel_unfused(**inp):
    f32 = lambda a: np.ascontiguousarray(np.asarray(a, dtype=np.float32))
    x = f32(inp["x"])
    pos = np.ascontiguousarray(np.asarray(inp["positions"], dtype=np.int32))
    xTs, pos_c = [], []
    for c in range(NCORES):
        b, half = c // 2, c % 2
        xTs.append(fm(x[b, half * T:(half + 1) * T]))
        pos_c.append(np.ascontiguousarray(pos[b, half * T:(half + 1) * T]))
    for i in range(4):
        p = "l%d_" % i
        m = i % 3
        if m == 0:
            xTs = run_swa_layer(xTs, pos_c, f32(inp[p + "attn_norm"]), f32(inp[p + "w_in"]), f32(inp[p + "w_out"]),
                                f32(inp[p + "sinks"]))
        elif m == 1:
            xTs = run_ret_layer(xTs, pos_c, f32(inp[p + "attn_norm"]), f32(inp[p + "w_in"]), f32(inp[p + "w_out"]),
                                f32(inp[p + "gn_g"]))
        else:
            xTs = run_moba_layer(xTs, pos_c, f32(inp[p + "attn_norm"]), f32(inp[p + "w_in"]), f32(inp[p + "w_out"]))
        xTs = run_ffn_layer(xTs, f32(inp[p + "ffn_norm"]), f32(inp[p + "w_a"]), f32(inp[p + "w_b"]),
                            f32(inp[p + "conv_w"]), f32(inp[p + "conv_b"]), f32(inp[p + "w_down"]))
    xTs = run_fnorm(xTs, f32(inp["final_norm"]))
    out = np.empty((BATCH, SEQ, D), np.float32)
    for c in range(NCORES):
        b, half = c // 2, c % 2
        out[b, half * T:(half + 1) * T] = xTs[c].reshape(D, T).T
    return out


LAYER_KEYS = [
    ("l0_attn_norm", "l0_w_in", "l0_w_out", "l0_sinks", "l0_ffn_norm", "l0_w_a", "l0_w_b", "l0_conv_w", "l0_conv_b", "l0_w_down"),
    ("l1_attn_norm", "l1_w_in", "l1_w_out", "l1_gn_g", "l1_ffn_norm", "l1_w_a", "l1_w_b", "l1_conv_w", "l1_conv_b", "l1_w_down"),
    ("l2_attn_norm", "l2_w_in", "l2_w_out", None, "l2_ffn_norm", "l2_w_a", "l2_w_b", "l2_conv_w", "l2_conv_b", "l2_w_down"),
    ("l3_attn_norm", "l3_w_in", "l3_w_out", "l3_sinks", "l3_ffn_norm", "l3_w_a", "l3_w_b", "l3_conv_w", "l3_conv_b", "l3_w_down"),
]


def build_fused():
    nc = bass.Bass("TRN2", target_bir_lowering=False)

    def ext(name, shape, dt):
        return nc.dram_tensor(name, shape, dt, kind="ExternalInput").ap()

    def scr(name, shape, dt):
        return nc.dram_tensor(name, shape, dt, kind="Internal").ap()

    X = [ext("x_%d" % h, [NCH, 128, T], F32) for h in range(2)]
    POS = [ext("pos_%d" % h, [T], I32) for h in range(2)]
    OUT = [nc.dram_tensor("out_%d" % h, [NCH, 128, T], F32, kind="ExternalOutput").ap() for h in range(2)]
    C = {"cf": ext("cf", [128, 8], F32), "ones": ext("ones", [128, 128], F32), "ones512": ext("ones512", [128, 128], F32),
         "perm": ext("perm", [128, 128], F32), "ident": ext("ident", [128, 128], F32),
         "masks_0": ext("masks_0", [128, 2, 2, 128], BF16), "masks_1": ext("masks_1", [128, 2, 2, 128], BF16),
         "qdec": ext("qdec", [128, 4, 128], F32), "kdec": ext("kdec", [128, 4], F32), "dec": ext("dec", [128, 4, 128], F32),
         "sb1": ext("sb1", [128, 16, 16], F32), "cmask": ext("cmask", [128, 2, 256], BF16),
         "onehot": ext("onehot", [16, SEQ], BF16), "gfinal": ext("gfinal", [128, NCH], F32)}
    W = []
    for i in range(4):
        m = i % 3
        win = 6144 if m == 1 else 1536
        wo_in = 2048 if m == 1 else D
        d = {"g_attn": ext("l%d_g_attn" % i, [128, NCH], F32), "w_in": ext("l%d_w_in" % i, [D, win], F32),
             "w_out": ext("l%d_w_out" % i, [wo_in, D], F32), "g_ffn": ext("l%d_g_ffn" % i, [128, NCH], F32),
             "w_a": ext("l%d_w_a" % i, [D, DFF], F32), "w_b": ext("l%d_w_b" % i, [D, DFF], F32),
             "w_down": ext("l%d_w_down" % i, [DFF, D], F32), "cw": ext("l%d_cw" % i, [128, 3, NFF], F32),
             "cb": ext("l%d_cb" % i, [128, NFF], F32)}
        if m == 0:
            d["sinks"] = ext("l%d_sinks" % i, [16], F32)
        if m == 1:
            d["gn"] = ext("l%d_gn" % i, [128, 16], F32)
        W.append(d)
    xs = [[scr("xs_%d_%d" % (h, k), [NCH, 128, T], F32) for k in range(2)] for h in range(2)]
    qT = [scr("qT_%d" % h, [8, 128, T], BF16) for h in range(2)]
    kT = [scr("kT_%d" % h, [2, 128, T], BF16) for h in range(2)]
    vv = [scr("v_%d" % h, [16, 128, 256], BF16) for h in range(2)]
    oT = [scr("oT_%d" % h, [8, 128, T], BF16) for h in range(2)]
    rq = [scr("rq_%d" % h, [8, 128, T], BF16) for h in range(2)]
    rqd = [scr("rqd_%d" % h, [8, 128, T], BF16) for h in range(2)]
    rk = [scr("rk_%d" % h, [8, 128, T], BF16) for h in range(2)]
    rkd = [scr("rkd_%d" % h, [16, 128, 1024], BF16) for h in range(2)]
    rv = [scr("rv_%d" % h, [16, 128, 2048], BF16) for h in range(2)]
    rsg = [scr("rsg_%d" % h, [16, 128, T], BF16) for h in range(2)]
    Rend = [scr("Rend_%d" % h, [4, 2, 128, 512], F32) for h in range(2)]
    cur = [X[0], X[1]]
    tog = [0, 0]
    nph = [0]

    def phase(name, builder, io, opts=None):
        nph[0] += 1
        with nc.cleanup_on_exit():
            cx = Ctx(nc, io, prefix="p%d%s_" % (nph[0], name), opts=opts)
            builder(cx)
            cx.es.close()
            nc.all_engine_barrier()

    def nxt(h):
        b = xs[h][tog[h]]
        tog[h] ^= 1
        return b

    base = {"cf": C["cf"], "ones": C["ones"]}
    for i in range(4):
        m = i % 3
        w = W[i]
        if m in (0, 2):
            for h in range(2):
                phase("qkv%d" % h, build_qkv_launch, dict(base, xT=cur[h], perm=C["perm"], g=w["g_attn"], w_in=w["w_in"],
                                                          pos=POS[h], qT=qT[h], kT=kT[h], v=vv[h]))
        if m == 0:
            for h in range(2):
                xo = nxt(h)
                io = dict(base, xT=cur[h], qT=qT[h].rearrange("c (h d) t -> (c h) d t", h=2), masks=C["masks_%d" % h],
                          sinks=w["sinks"], w_out=w["w_out"], xo=xo, kT_own=kT[h], v_own=vv[h])
                if h == 1:
                    io["kT_prev"] = kT[0]
                    io["v_prev"] = vv[0]
                phase("swa%d" % h, build_swa_launch, io, {"fused": True, "half": h})
                cur[h] = xo
        elif m == 1:
            for h in range(2):
                phase("ret1%d" % h, build_ret1_launch, dict(base, xT=cur[h], ident=C["ident"], g=w["g_attn"], w_in=w["w_in"],
                                                            pos=POS[h], qdec=C["qdec"], kdec=C["kdec"], qT=rq[h], qdT=rqd[h],
                                                            kT=rk[h], kd=rkd[h], v=rv[h], sgT=rsg[h]))
            for h in range(2):
                xo = nxt(h)
                io = {"cf": C["cf"], "ones": C["ones512"], "xT": cur[h], "qT": rq[h], "qdT": rqd[h], "kT": rk[h], "kd": rkd[h],
                      "v": rv[h], "sgT": rsg[h], "dec": C["dec"], "gn": w["gn"], "w_out": w["w_out"], "xo": xo, "Rend": Rend[h]}
                if h == 1:
                    io["R0"] = Rend[0]
                phase("ret2%d" % h, build_ret2_launch, io, {"fused": True, "r0": "zero" if h == 0 else "input"})
                cur[h] = xo
        else:
            io = {"ident": C["ident"], "sb1": C["sb1"], "cmask": C["cmask"], "onehot": C["onehot"]}
            for h in range(2):
                io["qT_%d" % h] = qT[h]
                io["kT_%d" % h] = kT[h]
                io["v_%d" % h] = vv[h]
                io["oT_%d" % h] = oT[h]
            phase("moba", build_moba_launch, io, {"fused": True})
            for h in range(2):
                xo = nxt(h)
                phase("oproj%d" % h, build_outproj_launch, dict(base, xT=cur[h], oT=oT[h], w_out=w["w_out"], xo=xo))
                cur[h] = xo
        xmid0 = cur[0]
        for h in range(2):
            xo = nxt(h)
            io = dict(base, xT=cur[h], w_a=w["w_a"], w_b=w["w_b"], w_down=w["w_down"], cw=w["cw"], cb=w["cb"], g=w["g_ffn"], xo=xo)
            if h == 1:
                io["x_prev"] = xmid0
            phase("ffn%d" % h, build_ffn_launch, io, {"halo": "zero" if h == 0 else "prev"})
            cur[h] = xo
    for h in range(2):
        phase("fnorm%d" % h, build_fnorm_launch, dict(base, xT=cur[h], g=C["gfinal"], xo=OUT[h]))
    return nc


_FUSED = {}


def kernel(**inp):
    f32 = lambda a: np.ascontiguousarray(np.asarray(a, dtype=np.float32))
    x = f32(inp["x"])
    pos = np.ascontiguousarray(np.asarray(inp["positions"], dtype=np.int32))
    if "nc" not in _FUSED:
        _FUSED["nc"] = build_fused()
    nc = _FUSED["nc"]
    dec, qdec, kdec, _ = ret_consts()
    sb1, cm = moba_consts()
    onehot = np.zeros((16, SEQ), np.float32)
    for n in range(16):
        onehot[n, n * 256:(n + 1) * 256] = 1.0
    shared = {"cf": make_cf(), "ones": np.ones((128, 128), np.float32), "ones512": np.full((128, 128), 1.0 / 512, np.float32),
              "perm": make_perm(), "ident": np.eye(128, dtype=np.float32), "masks_0": _swa_masks(0), "masks_1": _swa_masks(1),
              "qdec": qdec, "kdec": kdec, "dec": dec, "sb1": sb1, "cmask": cm, "onehot": onehot.astype(BF),
              "gfinal": vec_fm(f32(inp["final_norm"]), 8)}
    for i, keys in enumerate(LAYER_KEYS):
        k_an, k_win, k_wout, k_x, k_fn, k_wa, k_wb, k_cw, k_cb, k_wd = keys
        shared["l%d_g_attn" % i] = vec_fm(f32(inp[k_an]), 8)
        shared["l%d_w_in" % i] = f32(inp[k_win])
        shared["l%d_w_out" % i] = f32(inp[k_wout])
        shared["l%d_g_ffn" % i] = vec_fm(f32(inp[k_fn]), 8)
        shared["l%d_w_a" % i] = f32(inp[k_wa])
        shared["l%d_w_b" % i] = f32(inp[k_wb])
        shared["l%d_w_down" % i] = f32(inp[k_wd])
        shared["l%d_cw" % i] = np.ascontiguousarray(f32(inp[k_cw]).reshape(3, NFF, 128).transpose(2, 0, 1))
        shared["l%d_cb" % i] = np.ascontiguousarray(f32(inp[k_cb]).reshape(NFF, 128).T)
        if i % 3 == 0:
            shared["l%d_sinks" % i] = f32(inp[k_x])
        if i % 3 == 1:
            shared["l%d_gn" % i] = vec_fm(f32(inp[k_x]), 16)
    in_maps = []
    for c in range(NCORES):
        b = c % BATCH
        im = dict(shared)
        for h in range(2):
            im["x_%d" % h] = fm(x[b, h * T:(h + 1) * T])
            im["pos_%d" % h] = np.ascontiguousarray(pos[b, h * T:(h + 1) * T])
        in_maps.append(im)
    res = run_bass_kernel_spmd(nc, in_maps, core_ids=list(range(NCORES)))
    out = np.empty((BATCH, SEQ, D), np.float32)
    for b in range(BATCH):
        for h in range(2):
            out[b, h * T:(h + 1) * T] = np.asarray(res.results[b]["out_%d" % h]).reshape(D, T).T
    return out
```

```python
import numpy as np
import concourse.bass as bass
import concourse.mybir as mybir
from concourse.bass_utils import run_bass_kernel_spmd
from contextlib import ExitStack
import ml_dtypes

F32 = mybir.dt.float32
BF16 = mybir.dt.bfloat16
I32 = mybir.dt.int32
AF = mybir.ActivationFunctionType
ALU = mybir.AluOpType
AX = mybir.AxisListType

D = 1024
NCH = 8
SEQ = 4096
BATCH = 4
NCORES = 8
T = 2048
DFF = 2816
NFF = 22
EPS = 1e-6
HD = 64
NQH = 16
NKV = 4

SAME_ENGINE_SYNC = True
DBG = {"stage": 99}


class Buf:
    __slots__ = ("name", "w", "r", "multi", "sem", "cnt", "excl")

    def __init__(self, name, multi=False, excl=False):
        self.name = name
        self.excl = excl
        self.w = {}
        self.r = {}
        self.multi = multi
        self.sem = None
        self.cnt = 0


class Sched:
    def __init__(self, nc, es, prefix=""):
        self.nc = nc
        self.es = es
        self.prefix = prefix
        self.q = {k: [] for k in ("pe", "act", "dve", "pool", "sp")}
        self.sem = {}
        self.cnt = {k: 0 for k in self.q}
        self.seen = {k: {} for k in self.q}
        self.semname = {}
        for k in self.q:
            self.sem[k] = self.new_sem("s_" + k)
        self.nsem = 5
        self.dma_keys = []

    def new_sem(self, name):
        if self.prefix:
            return self.nc.alloc_semaphore(name=self.prefix + name)
        return self.es.enter_context(self.nc.semaphore(name))

    def _collect(self, eng, reads, writes):
        waits = {}

        def merge(d):
            for s, (h, v) in d.items():
                if s not in waits or waits[s][1] < v:
                    waits[s] = (h, v)

        for b in reads:
            merge(b.w)
        for b in writes:
            merge(b.r)
            if not b.multi:
                merge(b.w)
        own = id(self.sem[eng])
        out = []
        seen = self.seen[eng]
        for s, (h, v) in waits.items():
            if s == own and (eng == "pe" or not SAME_ENGINE_SYNC):
                continue
            if seen.get(s, 0) >= v:
                continue
            seen[s] = v
            out.append((h, v))
        return out

    def _record(self, tok, reads, writes):
        s, h, v = tok
        for b in reads:
            if s not in b.r or b.r[s][1] < v:
                b.r[s] = (h, v)
        for b in writes:
            if b.multi:
                if b.r:
                    b.w = {}
                    b.r = {}
                if s not in b.w or b.w[s][1] < v:
                    b.w[s] = (h, v)
            else:
                b.w = {s: (h, v)}
                b.r = {}

    def op(self, eng, fn, reads=(), writes=()):
        ex = [b for b in reads if b.excl]
        if ex:
            reads = [b for b in reads if not b.excl]
            writes = list(writes) + ex
        waits = self._collect(eng, reads, writes)
        self.cnt[eng] += 1
        h = self.sem[eng]
        v = self.cnt[eng]
        if DBG.get("log") is not None:
            DBG["log"].append((eng, v, [(getattr(wh, "name", str(wh)), wv) for wh, wv in waits],
                               [b.name for b in reads], [b.name for b in writes]))
        q = self.q[eng]
        for (wh, wv) in waits:
            q.append(lambda e, wh=wh, wv=wv: e.wait_ge(wh, wv))
        q.append(lambda e, fn=fn, h=h: fn(e).then_inc(h, 1))
        tok = (id(h), h, v)
        self._record(tok, reads, writes)
        return tok

    def dma(self, queue, out, in_, reads, writes, key):
        waits = self._collect(queue, reads, writes)
        if key.sem is None:
            key.sem = self.new_sem("d_" + key.name)
            self.dma_keys.append(key)
            self.nsem += 1
        key.cnt += 16
        h = key.sem
        v = key.cnt
        q = self.q[queue]
        for (wh, wv) in waits:
            q.append(lambda e, wh=wh, wv=wv: e.wait_ge(wh, wv))
        q.append(lambda e, out=out, in_=in_, h=h: e.dma_start(out=out, in_=in_).then_inc(h, 16))
        tok = (id(h), h, v)
        self._record(tok, reads, writes)
        return tok

    def dma_group(self, queue, items, reads, writes, key):
        waits = self._collect(queue, reads, writes)
        if key.sem is None:
            key.sem = self.new_sem("d_" + key.name)
            self.dma_keys.append(key)
            self.nsem += 1
        h = key.sem
        q = self.q[queue]
        for (wh, wv) in waits:
            q.append(lambda e, wh=wh, wv=wv: e.wait_ge(wh, wv))
        for (out, in_) in items:
            key.cnt += 16
            q.append(lambda e, out=out, in_=in_, h=h: e.dma_start(out=out, in_=in_).then_inc(h, 16))
        tok = (id(h), h, key.cnt)
        self._record(tok, reads, writes)
        return tok

    def final_wait(self, eng, bufs):
        waits = {}
        for b in bufs:
            for s, (h, v) in b.w.items():
                if s not in waits or waits[s][1] < v:
                    waits[s] = (h, v)
        q = self.q[eng]
        for s, (h, v) in waits.items():
            q.append(lambda e, h=h, v=v: e.wait_ge(h, v))

    def emit(self):
        nc = self.nc
        for key in self.dma_keys:
            self.q["sp"].append(lambda e, h=key.sem, v=key.cnt: e.wait_ge(h, v))
        with nc.Block() as block:
            @block.tensor
            def _(e):
                for f in self.q["pe"]:
                    f(e)

            @block.scalar
            def _(e):
                for f in self.q["act"]:
                    f(e)

            @block.vector
            def _(e):
                for f in self.q["dve"]:
                    f(e)

            @block.gpsimd
            def _(e):
                for f in self.q["pool"]:
                    f(e)

            @block.sync
            def _(e):
                for f in self.q["sp"]:
                    f(e)


class Ctx:
    def __init__(self, nc=None, io=None, prefix="", opts=None):
        self.nc = nc if nc is not None else bass.Bass("TRN2", target_bir_lowering=False)
        self.io = io
        self.prefix = prefix
        self.opts = opts or {}
        self.es = ExitStack()
        self.S = Sched(self.nc, self.es, prefix)
        self.ps = []
        self.ps_rr = 0

    def sb(self, name, shape, dt):
        t = self.es.enter_context(self.nc.sbuf_tensor(self.prefix + name, shape, dt))
        return t

    def psum_banks(self, n=8):
        for i in range(n):
            t = self.es.enter_context(self.nc.psum_tensor(self.prefix + "ps%d" % i, [128, 512], F32))
            self.ps.append((t, Buf("ps%d" % i, excl=True)))

    def din(self, name, shape, dt):
        if self.io is not None:
            ap = self.io[name]
            assert tuple(ap.shape) == tuple(shape), (name, tuple(ap.shape), tuple(shape))
            return ap
        return self.nc.dram_tensor(name, shape, dt, kind="ExternalInput").ap()

    def dout(self, name, shape, dt):
        if self.io is not None:
            ap = self.io[name]
            assert tuple(ap.shape) == tuple(shape), (name, tuple(ap.shape), tuple(shape))
            return ap
        return self.nc.dram_tensor(name, shape, dt, kind="ExternalOutput").ap()

    def dint(self, name, shape, dt):
        return self.nc.dram_tensor(name, shape, dt, kind="Internal").ap()


def cols(tt, off=0):
    return slice(off + tt * 512, off + (tt + 1) * 512)


class Res:
    def __init__(self, cx, with_hn=True):
        self.cx = cx
        self.x_sb = cx.opts.get("x_sb") if cx.opts.get("x_sb") is not None else cx.sb("x_sb", [128, NCH, T], F32)
        self.xb = [[Buf("x%d_%d" % (c, tt)) for tt in range(4)] for c in range(NCH)]
        self.xld = [Buf("xld%d" % tt) for tt in range(4)]
        self.cf = cx.sb("cf_sb", [128, 8], F32)
        self.eps_sb = self.cf
        self.epsb = Buf("cf")
        self.outb = Buf("xout", multi=True)
        self.ones = cx.sb("ones_sb", [128, 128], BF16)
        self.onesb = Buf("ones")
        if not with_hn:
            return
        self.hn_sb = cx.sb("hn_sb", [128, NCH, 2 + T], BF16)
        self.hb = [[Buf("h%d_%d" % (c, tt)) for tt in range(4)] for c in range(NCH)]
        self.hhalo = Buf("hhalo")
        self.xh_sb = cx.sb("xh_sb", [128, NCH, 2], F32)
        self.xhb = Buf("xh")
        self.sq = cx.sb("sq", [128, NCH, 512], BF16)
        self.sqq = [Buf("sqq%d" % i) for i in range(4)]
        self.rs = [cx.sb("rs%d" % i, [128, 512], F32) for i in range(2)]
        self.rsb = [Buf("rs%d" % i) for i in range(2)]
        self.rs_i = 0

    def load_x(self, xT_d, queue="sp"):
        S = self.cx.S
        if not self.cx.opts.get("load_x", True):
            return
        for tt in range(4):
            S.dma_group(queue, [(self.x_sb[:, c, cols(tt)], xT_d[c, :, cols(tt)]) for c in range(NCH)],
                        [], [self.xb[c][tt] for c in range(NCH)], self.xld[tt])

    def load_consts(self, cf_d, ones_d):
        S = self.cx.S
        S.dma("sp", self.cf[:, :], cf_d[:, :], [], [self.epsb], self.epsb)
        S.dma("pool", self.ones[:, :], ones_d[:, :], [], [self.onesb], self.onesb)

    def store_x(self, xT_d, queue="sp"):
        S = self.cx.S
        tail = self.cx.opts.get("tail_out")
        if tail is not None:
            S.dma(queue, tail[:, :, :], self.x_sb[:, :, T - 2:T], [self.xb[c][3] for c in range(NCH)], [self.outb], Buf("xtail"))
        if not self.cx.opts.get("store_x", True):
            return
        for tt in range(4):
            S.dma_group(queue, [(xT_d[c, :, cols(tt)], self.x_sb[:, c, cols(tt)]) for c in range(NCH)],
                        [self.xb[c][tt] for c in range(NCH)], [self.outb], self.outb)


def emit_norm(cx, R, g_sb, gbuf, ss_bank, tiles=(0, 1, 2, 3), halo=False, out_f32=False):
    S = cx.S
    ps_t, ps_b = cx.ps[ss_bank]
    work = [("t", tt) for tt in tiles]
    if halo:
        work = [("h", 0)] + work
    for kind, tt in work:
        if kind == "t":
            n = 512
            xin = R.x_sb[:, :, cols(tt)]
            xbufs = [R.xb[c][tt] for c in range(NCH)]
            if out_f32:
                hout = lambda c: R.x_sb[:, c, cols(tt)]
                hbufs = [R.xb[c][tt] for c in range(NCH)]
            else:
                hout = lambda c: R.hn_sb[:, c, cols(tt, 2)]
                hbufs = [R.hb[c][tt] for c in range(NCH)]
        else:
            n = 2
            xin = R.xh_sb[:, :, :]
            xbufs = [R.xhb]
            hout = lambda c: R.hn_sb[:, c, 0:2]
            hbufs = [R.hhalo] * NCH
        sq = R.sq[:, :, 0:n]
        S.op("act", lambda e, sq=sq, xin=xin: e.activation(out=sq, in_=xin, func=AF.Square),
             reads=xbufs, writes=R.sqq)

        def mm(e, n=n):
            ins = None
            for c in range(NCH):
                ins = e.matmul(ps_t[:, 0:n], lhsT=R.ones[:, :], rhs=R.sq[:, c, 0:n],
                               start=(c == 0), stop=(c == NCH - 1))
            return ins
        S.op("pe", mm, reads=R.sqq + [R.onesb], writes=[ps_b])
        i = R.rs_i
        R.rs_i ^= 1
        rs = R.rs[i][:, 0:n]
        S.op("act", lambda e, rs=rs, n=n: e.activation(out=rs, in_=ps_t[:, 0:n], func=AF.Sqrt,
                                                      scale=1.0 / D, bias=R.eps_sb[:, 0:1]),
             reads=[ps_b, R.epsb], writes=[R.rsb[i]])
        S.op("dve", lambda e, rs=rs: e.reciprocal(out=rs, in_=rs), reads=[R.rsb[i]], writes=[R.rsb[i]])
        for c in range(NCH):
            xi = xin[:, c, :]
            S.op("dve", lambda e, c=c, xi=xi, rs=rs, ho=hout(c): e.scalar_tensor_tensor(
                out=ho, in0=xi, scalar=g_sb[:, c:c + 1], in1=rs, op0=ALU.mult, op1=ALU.mult),
                reads=[xbufs[c] if kind == "t" else R.xhb, R.rsb[i], gbuf], writes=[hbufs[c]])


class FFNRes:
    def __init__(self, cx, R):
        self.JG = 4
        self.wa = [cx.sb("wa%d" % i, [128, NCH, 512], BF16) for i in range(2)]
        self.wb = [cx.sb("wb%d" % i, [128, NCH, 512], BF16) for i in range(2)]
        self.wd = [cx.sb("wd%d" % i, [128, 4, D], BF16) for i in range(2)]
        self.wab = [Buf("wa%d" % i) for i in range(2)]
        self.wbb = [Buf("wb%d" % i) for i in range(2)]
        self.wdb = [Buf("wd%d" % i) for i in range(2)]
        self.u = cx.sb("u_sb", [128, 4, T], BF16)
        self.ub = [[Buf("u%d_%d" % (j, tt)) for tt in range(4)] for j in range(4)]
        self.af = cx.sb("a_full", [128, 4, 2 + T], F32)
        self.ab = [[Buf("a%d_%d" % (j, tt)) for tt in range(4)] for j in range(4)]
        self.ahb = [Buf("ah%d" % j) for j in range(4)]
        q = [R.sq[:, 2 * i:2 * i + 2, :].rearrange("p a b -> p (a b)").bitcast(F32) for i in range(4)]
        self.t1 = q[0:2]
        self.t1b = R.sqq[0:2]
        self.sg = q[2:4]
        self.sgb = R.sqq[2:4]
        self.cw = cx.sb("cw_sb", [128, 3, NFF], F32)
        self.cb = cx.sb("cb_sb", [128, NFF], F32)
        self.cwb = Buf("cw")
        self.g = cx.sb("g_ffn_sb", [128, NCH], F32)
        self.gb = Buf("g_ffn")
        self.rr = 0


def ffn_groups():
    gs = []
    j = 0
    while j < NFF:
        n = min(4, NFF - j)
        gs.append((j, n))
        j += n
    return gs


def emit_ffn(cx, R, Fr, w_a, w_b, w_down, cw_d, cb_d, g_d, banks):
    S = cx.S
    S.dma_group("sp", [(Fr.cw[:, :, :], cw_d[:, :, :]), (Fr.cb[:, :], cb_d[:, :])], [], [Fr.cwb], Fr.cwb)
    S.dma("sp", Fr.g[:, :], g_d[:, :], [], [Fr.gb], Fr.gb)
    groups = ffn_groups()

    def load_w(gi):
        j0, n = groups[gi]
        s = gi % 2
        S.dma_group("pool", [(Fr.wa[s][:, kc, 0:n * 128], w_a[kc * 128:(kc + 1) * 128, j0 * 128:(j0 + n) * 128])
                             for kc in range(NCH)], [], [Fr.wab[s]], Fr.wab[s])
        S.dma_group("pool", [(Fr.wb[s][:, kc, 0:n * 128], w_b[kc * 128:(kc + 1) * 128, j0 * 128:(j0 + n) * 128])
                             for kc in range(NCH)], [], [Fr.wbb[s]], Fr.wbb[s])
        S.dma_group("pool", [(Fr.wd[s][:, jj, :], w_down[(j0 + jj) * 128:(j0 + jj + 1) * 128, :])
                             for jj in range(n)], [], [Fr.wdb[s]], Fr.wdb[s])

    if DBG["stage"] < 1:
        return
    emit_norm(cx, R, Fr.g, Fr.gb, banks["ss"], tiles=(0,), halo=True)
    if DBG["stage"] < 2:
        return
    load_w(0)
    if len(groups) > 1:
        load_w(1)

    a_banks = banks["a"]
    b_banks = banks["b"]
    y_banks = banks["y"]
    h_bank = banks["ss"]
    st = {"a": 0, "b": 0, "y": 0, "t": 0}

    def AB(gi, tt):
        j0, n = groups[gi]
        s = gi % 2
        for jj in range(n):
            j = j0 + jj
            pa_t, pa_b = cx.ps[a_banks[st["a"] % len(a_banks)]]
            st["a"] += 1
            pb_t, pb_b = cx.ps[b_banks[st["b"] % len(b_banks)]]
            st["b"] += 1
            hbufs = [R.hb[k][tt] for k in range(NCH)]

            def mm(e, w, pt, jj=jj, tt=tt):
                ins = None
                for k in range(NCH):
                    ins = e.matmul(pt[:, :], lhsT=w[:, k, jj * 128:(jj + 1) * 128],
                                   rhs=R.hn_sb[:, k, cols(tt, 2)], start=(k == 0), stop=(k == NCH - 1))
                return ins
            S.op("pe", lambda e, w=Fr.wa[s], pt=pa_t, mm=mm: mm(e, w, pt), reads=hbufs + [Fr.wab[s]], writes=[pa_b])
            S.op("pe", lambda e, w=Fr.wb[s], pt=pb_t, mm=mm: mm(e, w, pt), reads=hbufs + [Fr.wbb[s]], writes=[pb_b])
            ti = st["t"] % 2
            st["t"] += 1
            t1 = Fr.t1[ti]
            sg = Fr.sg[ti]
            acur = Fr.af[:, jj, cols(tt, 2)]
            S.op("act", lambda e, acur=acur, pa_t=pa_t: e.activation(out=acur, in_=pa_t[:, :], func=AF.Copy),
                 reads=[pa_b], writes=[Fr.ab[jj][tt]])
            S.op("dve", lambda e, t1=t1, acur=acur, j=j: e.tensor_scalar(
                out=t1[:, :], in0=acur, scalar1=Fr.cw[:, 2, j:j + 1], scalar2=Fr.cb[:, j:j + 1],
                op0=ALU.mult, op1=ALU.add), reads=[Fr.ab[jj][tt], Fr.cwb], writes=[Fr.t1b[ti]])
            prevb = Fr.ab[jj][tt - 1] if tt > 0 else Fr.ahb[jj]
            for sh, wi in ((1, 1), (2, 0)):
                ash = Fr.af[:, jj, slice(2 + tt * 512 - sh, 2 + (tt + 1) * 512 - sh)]
                S.op("dve", lambda e, t1=t1, ash=ash, j=j, wi=wi: e.scalar_tensor_tensor(
                    out=t1[:, :], in0=ash, scalar=Fr.cw[:, wi, j:j + 1], in1=t1[:, :],
                    op0=ALU.mult, op1=ALU.add),
                    reads=[Fr.ab[jj][tt], prevb, Fr.t1b[ti], Fr.cwb], writes=[Fr.t1b[ti]])
            S.op("act", lambda e, t1=t1, sg=sg: e.activation(out=sg[:, :], in_=t1[:, :], func=AF.Silu),
                 reads=[Fr.t1b[ti]], writes=[Fr.sgb[ti]])
            uo = Fr.u[:, jj, cols(tt)]
            S.op("dve", lambda e, uo=uo, sg=sg, pb_t=pb_t: e.tensor_tensor(
                out=uo, in0=sg[:, :], in1=pb_t[:, :], op=ALU.mult),
                reads=[Fr.sgb[ti], pb_b], writes=[Fr.ub[jj][tt]])

    def HALO(gi):
        j0, n = groups[gi]
        s = gi % 2
        ph_t, ph_b = cx.ps[h_bank]

        def mm(e):
            ins = None
            for jj in range(n):
                for k in range(NCH):
                    ins = e.matmul(ph_t[:, jj * 2:jj * 2 + 2], lhsT=Fr.wa[s][:, k, jj * 128:(jj + 1) * 128],
                                   rhs=R.hn_sb[:, k, 0:2], start=(k == 0), stop=(k == NCH - 1))
            return ins
        S.op("pe", mm, reads=[R.hhalo, Fr.wab[s]], writes=[ph_b])
        for jj in range(n):
            S.op("act", lambda e, jj=jj: e.activation(out=Fr.af[:, jj, 0:2], in_=ph_t[:, jj * 2:jj * 2 + 2], func=AF.Copy),
                 reads=[ph_b], writes=[Fr.ahb[jj]])

    def DOWN(gi, tt):
        j0, n = groups[gi]
        s = gi % 2
        if tt not in DBG.get("down_tt", (0, 1, 2, 3)):
            return
        for m in range(DBG.get("down_m", NCH)):
            py_t, py_b = cx.ps[y_banks[st["y"] % len(y_banks)]]
            st["y"] += 1

            def mm(e, m=m, py_t=py_t):
                ins = None
                for jj in range(n):
                    lt = Fr.wa[s][:, jj, 0:128] if DBG.get("usewa") else Fr.wd[s][:, jj, m * 128:(m + 1) * 128]
                    ins = e.matmul(py_t[:, :], lhsT=lt,
                                   rhs=Fr.u[:, jj, cols(tt)], start=(jj == 0), stop=(jj == n - 1))
                return ins
            S.op("pe", mm, reads=[Fr.ub[jj][tt] for jj in range(n)] + [Fr.wab[s] if DBG.get("usewa") else Fr.wdb[s]], writes=[py_b])
            xo = R.x_sb[:, m, cols(tt)]
            if DBG.get("actcopy"):
                S.op("act", lambda e, xo=xo, py_t=py_t: e.activation(out=xo, in_=py_t[:, :], func=AF.Copy),
                     reads=[py_b, R.xb[m][tt]], writes=[R.xb[m][tt]])
            elif DBG.get("noadd"):
                S.op("dve", lambda e, xo=xo, py_t=py_t: e.tensor_copy(out=xo, in_=py_t[:, :]),
                     reads=[py_b, R.xb[m][tt]], writes=[R.xb[m][tt]])
            else:
                S.op("dve", lambda e, xo=xo, py_t=py_t: e.tensor_tensor(out=xo, in0=xo, in1=py_t[:, :], op=ALU.add),
                     reads=[py_b, R.xb[m][tt]], writes=[R.xb[m][tt]])
            if DBG.get("slack"):
                S.op("dve", lambda e: e.memset(R.cf[:, 7:8], 0.0), reads=[py_b], writes=[])

    for gi in range(len(groups)):
        HALO(gi)
        if DBG["stage"] < 3:
            return
        for tt in range(4):
            if gi == 0 and tt >= 1:
                emit_norm(cx, R, Fr.g, Fr.gb, banks["ss"], tiles=(tt,))
            AB(gi, tt)
            if DBG["stage"] < 4:
                continue
            if tt >= 1:
                DOWN(gi, tt - 1)
        if DBG["stage"] < 4:
            return
        DOWN(gi, 3)
        if DBG["stage"] < 5:
            return
        if gi + 2 < len(groups):
            load_w(gi + 2)


def build_ffn_launch(cx=None, final_norm=False):
    cx = cx or Ctx()
    xT = cx.din("xT", [NCH, 128, T], F32)
    xh = cx.din("xh", [128, NCH, 2], F32) if not cx.opts.get("halo") else None
    cf = cx.din("cf", [128, 8], F32)
    ones = cx.din("ones", [128, 128], F32)
    w_a = cx.din("w_a", [D, DFF], F32)
    w_b = cx.din("w_b", [D, DFF], F32)
    w_down = cx.din("w_down", [DFF, D], F32)
    cw = cx.din("cw", [128, 3, NFF], F32)
    cb = cx.din("cb", [128, NFF], F32)
    g = cx.din("g", [128, NCH], F32)
    if final_norm:
        gf = cx.din("gf", [128, NCH], F32)
    xo = cx.dout("xo", [NCH, 128, T], F32)
    cx.psum_banks(8)
    R = Res(cx)
    Fr = FFNRes(cx, R)
    R.load_consts(cf, ones)
    R.load_x(xT)
    if cx.opts.get("halo") == "zero":
        cx.S.op("dve", lambda e: e.memset(R.xh_sb[:, :, :], 0.0), reads=[], writes=[R.xhb])
    elif cx.opts.get("halo") == "prev":
        cx.S.dma("sp", R.xh_sb[:, :, :], cx.io["x_tail"][:, :, :], [], [R.xhb], R.xhb)
    else:
        cx.S.dma("sp", R.xh_sb[:, :, :], xh[:, :, :], [], [R.xhb], R.xhb)
    banks = {"ss": 0, "a": DBG.get("abanks", [1, 2]), "b": DBG.get("bbanks", [3, 4]), "y": DBG.get("ybanks", [5, 6, 7])}
    emit_ffn(cx, R, Fr, w_a, w_b, w_down, cw, cb, g, banks)
    R.store_x(xo)
    cx.S.final_wait("sp", [R.outb])
    cx.S.emit()
    return cx


TWO_PI = float(2.0 * np.pi)


def emit_rope_tables(cx, R, tb, pos_f, posb, inv_col, sgn_col, tt):
    S = cx.S
    u = tb["u"]
    kf = tb["kf"]
    MAGIC = 12582912.0
    C1 = 6.28125
    C2 = float(2.0 * np.pi - 6.28125)
    ub, kb = tb["ub"], tb["kb"]
    S.op("dve", lambda e: e.tensor_scalar(out=u[:, :], in0=pos_f[:, cols(tt)], scalar1=inv_col, scalar2=None,
                                          op0=ALU.mult), reads=[posb, R.epsb], writes=[ub])
    S.op("dve", lambda e: e.tensor_scalar(out=kf[:, :], in0=u[:, :], scalar1=float(1.0 / (2.0 * np.pi)), scalar2=MAGIC,
                                          op0=ALU.mult, op1=ALU.add), reads=[ub], writes=[kb])
    S.op("dve", lambda e: e.tensor_scalar(out=kf[:, :], in0=kf[:, :], scalar1=MAGIC, scalar2=None,
                                          op0=ALU.subtract), reads=[kb], writes=[kb])
    S.op("dve", lambda e: e.scalar_tensor_tensor(out=u[:, :], in0=kf[:, :], scalar=-C1, in1=u[:, :],
                                                 op0=ALU.mult, op1=ALU.add), reads=[kb, ub], writes=[ub])
    S.op("dve", lambda e: e.scalar_tensor_tensor(out=u[:, :], in0=kf[:, :], scalar=-C2, in1=u[:, :],
                                                 op0=ALU.mult, op1=ALU.add), reads=[kb, ub], writes=[ub])
    PI_LO = 3.1415925
    S.op("dve", lambda e: e.tensor_scalar(out=u[:, :], in0=u[:, :], scalar1=PI_LO, scalar2=-PI_LO,
                                          op0=ALU.min, op1=ALU.max), reads=[ub], writes=[ub])
    S.op("dve", lambda e: e.scalar_tensor_tensor(out=kf[:, :], in0=u[:, :], scalar=-1.0, in1=u[:, :],
                                                 op0=ALU.mult, op1=ALU.max), reads=[ub, kb], writes=[kb])
    S.op("act", lambda e: e.activation(out=tb["S"][:, :], in_=u[:, :], func=AF.Sin, scale=sgn_col),
         reads=[ub, R.epsb], writes=[tb["Sb"]])
    S.op("act", lambda e: e.activation(out=tb["C"][:, :], in_=kf[:, :], func=AF.Sin, scale=-1.0, bias=R.cf[:, 1:2]),
         reads=[kb, R.epsb], writes=[tb["Cb"]])


def build_qkv_launch(cx=None):
    cx = cx or Ctx()
    S = cx.S
    xT = cx.din("xT", [NCH, 128, T], F32)
    cf = cx.din("cf", [128, 8], F32)
    ones = cx.din("ones", [128, 128], F32)
    perm = cx.din("perm", [128, 128], F32)
    g = cx.din("g", [128, NCH], F32)
    w_in = cx.din("w_in", [D, 1536], F32)
    pos = cx.din("pos", [T], I32)
    qT = cx.dout("qT", [8, 128, T], BF16)
    kT = cx.dout("kT", [2, 128, T], BF16)
    v = cx.dout("v", [16, 128, 256], BF16)
    cx.psum_banks(8)
    R = Res(cx)
    R.load_consts(cf, ones)
    R.load_x(xT)
    g_sb = cx.sb("g_sb", [128, NCH], F32)
    gb = Buf("g")
    S.dma("sp", g_sb[:, :], g[:, :], [], [gb], gb)
    w_sb = cx.sb("w_sb", [128, NCH, 1536], BF16)
    wb = Buf("w_in")
    S.dma_group("pool", [(w_sb[:, kc, :], w_in[kc * 128:(kc + 1) * 128, :]) for kc in range(NCH)], [], [wb], wb)
    perm_sb = cx.sb("perm_sb", [128, 128], BF16)
    permb = Buf("perm")
    S.dma("pool", perm_sb[:, :], perm[:, :], [], [permb], permb)
    pos_i = cx.sb("pos_i", [128, T], I32)
    pos_f = cx.sb("pos_f", [128, T], F32)
    posib = Buf("posi")
    posb = Buf("posf")
    S.dma("sp", pos_i[:, :], pos.partition_broadcast(128), [], [posib], posib)
    S.op("dve", lambda e: e.tensor_copy(out=pos_f[:, :], in_=pos_i[:, :]), reads=[posib], writes=[posb])
    emit_norm(cx, R, g_sb, gb, 0)
    tb = {"u": cx.sb("rp_u", [128, 512], F32), "ub": Buf("rp_u"), "kf": cx.sb("rp_k", [128, 512], F32), "kb": Buf("rp_k"),
          "C": cx.sb("Ctab", [128, 512], F32), "Cb": Buf("C"),
          "S": cx.sb("Stab", [128, 512], F32), "Sb": Buf("S")}
    qraw = [cx.sb("qraw%d" % i, [128, 512], BF16) for i in range(2)]
    qrawb = [Buf("qraw%d" % i) for i in range(2)]
    t1 = [cx.sb("rt1_%d" % i, [128, 512], F32) for i in range(2)]
    t1b = [Buf("rt1_%d" % i) for i in range(2)]
    qo = [cx.sb("qo%d" % i, [128, 512], BF16) for i in range(3)]
    qob = [Buf("qo%d" % i) for i in range(3)]
    vo = [cx.sb("vo%d" % i, [128, 256], BF16) for i in range(2)]
    vob = [Buf("vo%d" % i) for i in range(2)]
    outb = Buf("qkv_out", multi=True)
    n = 0
    nv = 0
    for tt in range(4):
        emit_rope_tables(cx, R, tb, pos_f, posb, R.cf[:, 3:4], R.cf[:, 4:5], tt)
        hbufs = [R.hb[k][tt] for k in range(NCH)]
        for c in range(10):
            pp_t, pp_b = cx.ps[1 + (n % 2)]
            pr_t, pr_b = cx.ps[3 + (n % 2)]
            i2 = n % 2
            i3 = n % 3
            n += 1

            def mm(e, c=c, pp_t=pp_t, tt=tt):
                ins = None
                for k in range(NCH):
                    ins = e.matmul(pp_t[:, :], lhsT=w_sb[:, k, c * 128:(c + 1) * 128],
                                   rhs=R.hn_sb[:, k, cols(tt, 2)], start=(k == 0), stop=(k == NCH - 1))
                return ins
            S.op("pe", mm, reads=hbufs + [wb], writes=[pp_b])
            S.op("act", lambda e, i2=i2, pp_t=pp_t: e.activation(out=qraw[i2][:, :], in_=pp_t[:, :], func=AF.Copy),
                 reads=[pp_b], writes=[qrawb[i2]])
            S.op("pe", lambda e, i2=i2, pr_t=pr_t: e.matmul(pr_t[:, :], lhsT=perm_sb[:, :], rhs=qraw[i2][:, :],
                                                          start=True, stop=True),
                 reads=[qrawb[i2], permb], writes=[pr_b])
            S.op("dve", lambda e, i2=i2: e.tensor_tensor(out=t1[i2][:, :], in0=qraw[i2][:, :], in1=tb["C"][:, :], op=ALU.mult),
                 reads=[qrawb[i2], tb["Cb"]], writes=[t1b[i2]])
            S.op("dve", lambda e, i3=i3, pr_t=pr_t: e.tensor_tensor(out=qo[i3][:, :], in0=pr_t[:, :], in1=tb["S"][:, :], op=ALU.mult),
                 reads=[pr_b, tb["Sb"]], writes=[qob[i3]])
            S.op("dve", lambda e, i3=i3, i2=i2: e.tensor_tensor(out=qo[i3][:, :], in0=qo[i3][:, :], in1=t1[i2][:, :], op=ALU.add),
                 reads=[qob[i3], t1b[i2]], writes=[qob[i3]])
            dst = qT[c, :, cols(tt)] if c < 8 else kT[c - 8, :, cols(tt)]
            S.dma("sp", dst, qo[i3][:, :], [qob[i3]], [outb], qob[i3])
        for st_ in range(4):
            pv_t, pv_b = cx.ps[5 + (nv % 2)]
            iv = nv % 2
            nv += 1
            c0 = 2 + tt * 512 + st_ * 128

            def mmv(e, pv_t=pv_t, c0=c0):
                ins = None
                for k in range(NCH):
                    ins = e.matmul(pv_t[:, 0:256], lhsT=R.hn_sb[:, k, c0:c0 + 128], rhs=w_sb[:, k, 1280:1536],
                                   start=(k == 0), stop=(k == NCH - 1))
                return ins
            S.op("pe", mmv, reads=hbufs + [wb], writes=[pv_b])
            S.op("act", lambda e, iv=iv, pv_t=pv_t: e.activation(out=vo[iv][:, :], in_=pv_t[:, 0:256], func=AF.Copy),
                 reads=[pv_b], writes=[vob[iv]])
            S.dma("sp", v[tt * 4 + st_, :, :], vo[iv][:, :], [vob[iv]], [outb], vob[iv])
    S.final_wait("sp", [outb])
    S.emit()
    return cx


def make_cf():
    cf = np.zeros((128, 8), np.float32)
    p = np.arange(128)
    cf[:, 0] = EPS
    cf[:, 1] = np.pi / 2
    sgn = np.where((p % 64) < 32, -1.0, 1.0)
    cf[:, 3] = (1.0 / (10000.0 ** (np.arange(0, 64, 2, dtype=np.float32) / 64)))[p % 32]
    cf[:, 4] = sgn
    cf[:, 5] = 1.0 / (10000.0 ** (np.arange(0, 256, 2, dtype=np.float32) / 256))
    return cf


def make_perm():
    pm = np.zeros((128, 128), np.float32)
    for m in range(128):
        k = m + 32 if (m % 64) < 32 else m - 32
        pm[k, m] = 1.0
    return pm


def fm(x2d):
    return np.ascontiguousarray(x2d.T.reshape(NCH, 128, x2d.shape[0]))


def vec_fm(vv, n):
    return np.ascontiguousarray(vv.reshape(n, 128).T)


def emit_outproj(cx, R, oT_sb, oTb, w_sb, wb, nk, banks):
    S = cx.S
    n = 0
    for tt in range(4):
        for m in range(NCH):
            py_t, py_b = cx.ps[banks[n % len(banks)]]
            n += 1

            def mm(e, m=m, tt=tt, py_t=py_t):
                ins = None
                for c in range(nk):
                    ins = e.matmul(py_t[:, :], lhsT=w_sb[:, c, m * 128:(m + 1) * 128], rhs=oT_sb[:, c, cols(tt)],
                                   start=(c == 0), stop=(c == nk - 1))
                return ins
            S.op("pe", mm, reads=[oTb[c][tt] for c in range(nk)] + [wb], writes=[py_b])
            xo = R.x_sb[:, m, cols(tt)]
            S.op("dve", lambda e, xo=xo, py_t=py_t: e.tensor_tensor(out=xo, in0=xo, in1=py_t[:, :], op=ALU.add),
                 reads=[py_b, R.xb[m][tt]], writes=[R.xb[m][tt]])


def build_swa_launch(cx=None):
    cx = cx or Ctx()
    S = cx.S
    NT = 17
    xT = cx.din("xT", [NCH, 128, T], F32)
    cf = cx.din("cf", [128, 8], F32)
    ones = cx.din("ones", [128, 128], F32)
    qT = cx.din("qT", [16, 64, T], BF16)
    fused = cx.opts.get("fused", False)
    half = cx.opts.get("half", 0)
    if not fused:
        kTf = cx.din("kTf", [4, 64, 128 + T], BF16)
        vaug = cx.din("vaug", [4, 128, NT, 192], BF16)
    masks = cx.din("masks", [128, 2, 2, 128], BF16)
    sinks = cx.din("sinks", [16], F32)
    w_out = cx.din("w_out", [D, D], F32)
    xo = cx.dout("xo", [NCH, 128, T], F32)
    cx.psum_banks(8)
    R = Res(cx, with_hn=False)
    R.load_consts(cf, ones)
    R.load_x(xT)
    mk = cx.sb("mk_sb", [128, 2, 2, 128], BF16)
    mkb = Buf("mk")
    S.dma("sp", mk[:, :, :, :], masks[:, :, :, :], [], [mkb], mkb)
    esk = cx.sb("esk", [128, 16], F32)
    eskb = Buf("esk")
    S.dma("sp", esk[:, :], sinks.partition_broadcast(128), [], [eskb], eskb)
    S.op("act", lambda e: e.activation(out=esk[:, :], in_=esk[:, :], func=AF.Exp), reads=[eskb], writes=[eskb])
    w_sb = cx.sb("wo_sb", [128, NCH, D], BF16)
    wb = Buf("w_out")
    S.dma_group("pool", [(w_sb[:, kc, :], w_out[kc * 128:(kc + 1) * 128, :]) for kc in range(NCH)], [], [wb], wb)
    oT = cx.sb("oT_sb", [128, NCH, T], BF16)
    oTb = [[Buf("oT%d_%d" % (c, tt)) for tt in range(4)] for c in range(NCH)]
    kg = [cx.sb("kg%d" % i, [64, 128 + T], BF16) for i in range(2)]
    kgb = [Buf("kg%d" % i) for i in range(2)]
    vg = [cx.sb("vg%d" % i, [128, NT, 192], BF16) for i in range(2)]
    vgb = [Buf("vg%d" % i) for i in range(2)]
    qg = [cx.sb("qg%d" % i, [64, 4, T], BF16) for i in range(2)]
    qgb = [Buf("qg%d" % i) for i in range(2)]
    P = [cx.sb("P%d" % i, [128, 2, 256], BF16) for i in range(4)]
    Pb = [Buf("P%d" % i) for i in range(4)]
    rec = [cx.sb("rec%d" % i, [128, 2, 128], F32) for i in range(2)]
    recb = [Buf("rec%d" % i) for i in range(2)]

    if fused:
        for s_ in range(2):
            S.op("dve", lambda e, s_=s_: e.memset(vg[s_][:, :, :], 1.0), reads=[], writes=[vgb[s_]])
            if half == 0:
                S.op("dve", lambda e, s_=s_: e.memset(kg[s_][:, 0:128], 0.0), reads=[], writes=[kgb[s_]])
        kown = cx.io["kT_own"].rearrange("c (h d) t -> (c h) d t", h=2)
        vown = cx.io["v_own"].rearrange("n p (g d) -> n p g d", g=4)
        if half == 1:
            kprev = cx.io["kT_prev"].rearrange("c (h d) t -> (c h) d t", h=2)
            vprev = cx.io["v_prev"].rearrange("n p (g d) -> n p g d", g=4)

    def load_g(g):
        s = g % 2
        if not fused:
            S.dma("sp", kg[s][:, :], kTf[g, :, :], [], [kgb[s]], kgb[s])
            S.dma("sp", vg[s][:, :, :], vaug[g, :, :, :], [], [vgb[s]], vgb[s])
        else:
            items = [(kg[s][:, 128:], kown[g, :, :])]
            if half == 1:
                items.append((kg[s][:, 0:128], kprev[g, :, T - 128:T]))
            S.dma_group("sp", items, [], [kgb[s]], kgb[s])
            items = [(vg[s][:, 1 + 4 * a:5 + 4 * a, 64:128], vown[4 * a:4 * a + 4, :, g, :].rearrange("n p d -> p n d"))
                     for a in range(4)]
            if half == 1:
                items.append((vg[s][:, 0, 64:128], vprev[15, :, g, :]))
            S.dma_group("sp", items, [], [vgb[s]], vgb[s])
        S.dma_group("sp", [(qg[s][:, hl, :], qT[4 * g + hl, :, :]) for hl in range(4)], [], [qgb[s]], qgb[s])

    load_g(0)
    np_ = 0
    for g in range(4):
        s = g % 2
        if g + 1 < 4:
            load_g(g + 1)
        for i in range(16):
            tt = i // 4
            for par in range(2):
                ps_t, ps_b = cx.ps[1 + (np_ % 2)]
                po_t, po_b = cx.ps[3 + (np_ % 4)]
                pi = np_ % 4
                ri = np_ % 2
                np_ += 1

                def mmqk(e, ps_t=ps_t, par=par, i=i, s=s):
                    ins = None
                    for kt in range(2):
                        for hh in range(2):
                            hl = 2 * hh + par
                            c0 = kt * 256 + hh * 128
                            ins = e.matmul(ps_t[:, c0:c0 + 128], lhsT=kg[s][:, (i + kt) * 128:(i + kt + 1) * 128],
                                           rhs=qg[s][:, hl, i * 128:(i + 1) * 128], start=True, stop=True)
                    return ins
                S.op("pe", mmqk, reads=[kgb[s], qgb[s]], writes=[ps_b])
                Pv = P[pi]
                S.op("act", lambda e, Pv=Pv, ps_t=ps_t: e.activation(
                    out=Pv[:, :, :], in_=ps_t[:, :].rearrange("p (k q) -> p k q", k=2), func=AF.Exp, scale=0.125),
                    reads=[ps_b], writes=[Pb[pi]])
                mv = 1 if i == 0 else 0
                P4 = Pv[:, :, :].rearrange("p k (h q) -> p k h q", h=2)
                S.op("pool", lambda e, P4=P4, mv=mv: e.tensor_tensor(
                    out=P4, in0=P4, in1=mk[:, mv, :, :].unsqueeze(2).broadcast_to([128, 2, 2, 128]), op=ALU.mult),
                    reads=[Pb[pi], mkb], writes=[Pb[pi]])
                vsl = slice(64, 192) if par == 0 else slice(0, 128)

                def mmpv(e, po_t=po_t, Pv=Pv, vsl=vsl, i=i, s=s):
                    ins = None
                    for kt in range(2):
                        ins = e.matmul(po_t[:, 0:256], lhsT=vg[s][:, i + kt, vsl], rhs=Pv[:, kt, :],
                                       start=(kt == 0), stop=(kt == 1))
                    return ins
                S.op("pe", mmpv, reads=[Pb[pi], vgb[s]], writes=[po_b])
                nlo, dlo = (0, 64) if par == 0 else (64, 0)
                num = po_t[nlo:nlo + 64, 0:256].rearrange("p (h q) -> p h q", h=2)
                den = po_t[dlo:dlo + 64, 0:256].rearrange("p (h q) -> p h q", h=2)
                rc = rec[ri][nlo:nlo + 64, :, :]
                ek = esk[nlo:nlo + 64, 4 * g + par:4 * g + par + 3:2].unsqueeze(2).broadcast_to([64, 2, 128])
                S.op("dve", lambda e, rc=rc, den=den, ek=ek: e.tensor_tensor(out=rc, in0=den, in1=ek, op=ALU.add),
                     reads=[po_b, eskb], writes=[recb[ri]])
                S.op("dve", lambda e, rc=rc: e.reciprocal(out=rc, in_=rc), reads=[recb[ri]], writes=[recb[ri]])
                oo = oT[nlo:nlo + 64, 2 * g:2 * g + 2, i * 128:(i + 1) * 128]
                S.op("dve", lambda e, oo=oo, num=num, rc=rc: e.tensor_tensor(out=oo, in0=num, in1=rc, op=ALU.mult),
                     reads=[po_b, recb[ri]], writes=[oTb[2 * g][tt], oTb[2 * g + 1][tt]])
    emit_outproj(cx, R, oT, oTb, w_sb, wb, NCH, [7, 0])
    R.store_x(xo)
    S.final_wait("sp", [R.outb])
    S.emit()
    return cx


_PROGS = {}
BF = ml_dtypes.bfloat16


def _prog(name, builder):
    if name not in _PROGS:
        _PROGS[name] = builder()
    return _PROGS[name]


def _run(name, builder, in_maps):
    cx = _prog(name, builder)
    res = run_bass_kernel_spmd(cx.nc, in_maps, core_ids=list(range(len(in_maps))))
    return res.results


def _consts():
    return {"cf": make_cf(), "ones": np.ones((128, 128), np.float32)}


def _swa_masks(half):
    k = np.arange(128)[:, None]
    q = np.arange(128)[None, :]
    m = np.zeros((128, 2, 2, 128), np.float32)
    m[:, 0, 0, :] = (k > q)
    m[:, 0, 1, :] = (k <= q)
    m[:, 1, 1, :] = (k <= q)
    if half == 1:
        m[:, 1, 0, :] = (k > q)
    return m.astype(BF)


def host_swa_exchange(qkv, ncores):
    outs = []
    for c in range(ncores):
        half = c % 2
        kT = np.asarray(qkv[c]["kT"]).reshape(4, 64, T)
        v = np.asarray(qkv[c]["v"]).reshape(T, 4, 64)
        kTf = np.zeros((4, 64, 128 + T), BF)
        kTf[:, :, 128:] = kT
        vfull = np.zeros((128 + T, 4, 64), BF)
        vfull[128:] = v
        if half == 1:
            pk = np.asarray(qkv[c - 1]["kT"]).reshape(4, 64, T)
            pv = np.asarray(qkv[c - 1]["v"]).reshape(T, 4, 64)
            kTf[:, :, :128] = pk[:, :, T - 128:]
            vfull[:128] = pv[T - 128:]
        vaug = np.ones((4, 128, 17, 192), BF)
        vaug[:, :, :, 64:128] = vfull.reshape(17, 128, 4, 64).transpose(2, 1, 0, 3)
        outs.append({"qT": np.asarray(qkv[c]["qT"]).reshape(16, 64, T), "kTf": kTf, "vaug": vaug,
                     "masks": _swa_masks(half)})
    return outs


def run_swa_layer(xTs, pos_c, g, w_in, w_out, sinks):
    nco = len(xTs)
    cst = _consts()
    ims = [dict(cst, xT=xTs[c], perm=make_perm(), g=vec_fm(g, 8), w_in=w_in, pos=pos_c[c]) for c in range(nco)]
    qkv = _run("qkv", build_qkv_launch, ims)
    ex = host_swa_exchange(qkv, nco)
    ims = [dict(cst, xT=xTs[c], sinks=sinks, w_out=w_out, **ex[c]) for c in range(nco)]
    r = _run("swa", build_swa_launch, ims)
    return [np.asarray(r[c]["xo"]) for c in range(nco)]


RET_GAMMA = [float(1.0 - 2.0 ** (-5.0 - h)) for h in range(4)]


def ret_consts():
    lg = np.log(np.asarray(RET_GAMMA, np.float64))
    idx = np.arange(128, dtype=np.float64)
    dec = np.zeros((128, 4, 128), np.float32)
    diff = idx[None, :] - idx[:, None]
    for h in range(4):
        dec[:, h, :] = np.where(diff >= 0, np.exp(np.maximum(diff, 0) * lg[h]), 0.0)
    qdec = np.zeros((128, 4, 128), np.float32)
    for h in range(4):
        qdec[:, h, :] = np.exp((idx + 1.0) * lg[h])[None, :]
    kdec = np.zeros((128, 4), np.float32)
    for h in range(4):
        kdec[:, h] = np.exp((127.0 - idx) * lg[h])
    cdec = [float(np.exp(128.0 * lg[h])) for h in range(4)]
    return dec, qdec, kdec, cdec


def build_ret1_launch(cx=None):
    cx = cx or Ctx()
    S = cx.S
    xT = cx.din("xT", [NCH, 128, T], F32)
    cf = cx.din("cf", [128, 8], F32)
    ones = cx.din("ones", [128, 128], F32)
    ident = cx.din("ident", [128, 128], F32)
    g = cx.din("g", [128, NCH], F32)
    w_in = cx.din("w_in", [D, 6144], F32)
    pos = cx.din("pos", [T], I32)
    qdec_d = cx.din("qdec", [128, 4, 128], F32)
    kdec_d = cx.din("kdec", [128, 4], F32)
    qT = cx.dout("qT", [8, 128, T], BF16)
    qdT = cx.dout("qdT", [8, 128, T], BF16)
    kT = cx.dout("kT", [8, 128, T], BF16)
    kd = cx.dout("kd", [16, 128, 1024], BF16)
    v = cx.dout("v", [16, 128, 2048], BF16)
    sgT = cx.dout("sgT", [16, 128, T], BF16)
    cx.psum_banks(8)
    R = Res(cx)
    R.load_consts(cf, ones)
    R.load_x(xT)
    g_sb = cx.sb("g_sb", [128, NCH], F32)
    gb = Buf("g")
    S.dma("sp", g_sb[:, :], g[:, :], [], [gb], gb)
    id_sb = cx.sb("id_sb", [128, 128], BF16)
    idb = Buf("ident")
    S.dma("pool", id_sb[:, :], ident[:, :], [], [idb], idb)
    qdec = cx.sb("qdec_sb", [128, 4, 128], F32)
    kdec = cx.sb("kdec_sb", [128, 4], F32)
    dcb = Buf("dec")
    S.dma_group("sp", [(qdec[:, :, :], qdec_d[:, :, :]), (kdec[:, :], kdec_d[:, :])], [], [dcb], dcb)
    wbuf = [cx.sb("wblk%d" % i, [128, NCH, 1024], BF16) for i in range(2)]
    wbb = [Buf("wblk%d" % i) for i in range(2)]
    nblk = [0]

    def load_wblock(col0):
        i = nblk[0] % 2
        nblk[0] += 1
        S.dma_group("pool", [(wbuf[i][:, kc, :], w_in[kc * 128:(kc + 1) * 128, col0:col0 + 1024]) for kc in range(NCH)],
                    [], [wbb[i]], wbb[i])
        return i

    pos_i = cx.sb("pos_i", [128, T], I32)
    pos_f = cx.sb("pos_f", [128, T], F32)
    posib = Buf("posi")
    posb = Buf("posf")
    S.dma("sp", pos_i[:, :], pos.partition_broadcast(128), [], [posib], posib)
    S.op("dve", lambda e: e.tensor_copy(out=pos_f[:, :], in_=pos_i[:, :]), reads=[posib], writes=[posb])
    wq = load_wblock(0)
    wk = load_wblock(1024)
    emit_norm(cx, R, g_sb, gb, 0)
    tb = {"u": cx.sb("rp_u", [128, 512], F32), "ub": Buf("rp_u"), "kf": cx.sb("rp_k", [128, 512], F32), "kb": Buf("rp_k"),
          "C": cx.sb("Ctab", [128, 512], F32), "Cb": Buf("C"),
          "S": cx.sb("Stab", [128, 512], F32), "Sb": Buf("S")}
    ta = [cx.sb("rta%d" % i, [128, 512], F32) for i in range(2)]
    tab_ = [Buf("rta%d" % i) for i in range(2)]
    ro = [cx.sb("ro%d" % i, [128, 512], BF16) for i in range(4)]
    rob = [Buf("ro%d" % i) for i in range(4)]
    rod = [cx.sb("rod%d" % i, [128, 512], BF16) for i in range(2)]
    rodb = [Buf("rod%d" % i) for i in range(2)]
    kdt = [cx.sb("kdt%d" % i, [128, 4, 1024], BF16) for i in range(2)]
    kdtb = [Buf("kdt%d" % i) for i in range(2)]
    outb = Buf("r1_out", multi=True)
    cnt = {"ro": 0, "rod": 0, "ta": 0}
    for which, wi in (("q", wq), ("k", wk)):
        if which == "k" and DBG.get("r1", 9) < 2:
            break
        for tt in range(4):
            emit_rope_tables(cx, R, tb, pos_f, posb, R.cf[:, 5:6], 1.0, tt)
            hbufs = [R.hb[k][tt] for k in range(NCH)]
            ki = tt % 2
            for h in range(4):
                pa_t, pa_b = cx.ps[1 + (h % 2) * 2]
                pb_t, pb_b = cx.ps[2 + (h % 2) * 2]
                for (pt, pbuf, c) in ((pa_t, pa_b, 2 * h), (pb_t, pb_b, 2 * h + 1)):
                    def mm(e, pt=pt, c=c, tt=tt, wi=wi):
                        ins = None
                        for k in range(NCH):
                            ins = e.matmul(pt[:, :], lhsT=wbuf[wi][:, k, c * 128:(c + 1) * 128],
                                           rhs=R.hn_sb[:, k, cols(tt, 2)], start=(k == 0), stop=(k == NCH - 1))
                        return ins
                    S.op("pe", mm, reads=hbufs + [wbb[wi]], writes=[pbuf])
                scl = 1.0 if which == "q" else 1.0 / 16.0
                for (X, Xb, Y, Yb, sign, c) in ((pa_t, pa_b, pb_t, pb_b, -1.0, 2 * h), (pb_t, pb_b, pa_t, pa_b, 1.0, 2 * h + 1)):
                    i_ta = cnt["ta"] % 2
                    cnt["ta"] += 1
                    i_ro = cnt["ro"] % 4
                    cnt["ro"] += 1
                    S.op("dve", lambda e, X=X, i_ta=i_ta: e.tensor_tensor(out=ta[i_ta][:, :], in0=X[:, :], in1=tb["C"][:, :], op=ALU.mult),
                         reads=[Xb, tb["Cb"]], writes=[tab_[i_ta]])
                    S.op("dve", lambda e, Y=Y, i_ro=i_ro: e.tensor_tensor(out=ro[i_ro][:, :], in0=Y[:, :], in1=tb["S"][:, :], op=ALU.mult),
                         reads=[Yb, tb["Sb"]], writes=[rob[i_ro]])
                    S.op("dve", lambda e, i_ro=i_ro, i_ta=i_ta, sign=sign: e.scalar_tensor_tensor(
                        out=ta[i_ta][:, :], in0=ro[i_ro][:, :], scalar=float(sign), in1=ta[i_ta][:, :], op0=ALU.mult, op1=ALU.add),
                        reads=[rob[i_ro], tab_[i_ta]], writes=[tab_[i_ta]])
                    S.op("act", lambda e, i_ro=i_ro, i_ta=i_ta, scl=scl: e.activation(
                        out=ro[i_ro][:, :], in_=ta[i_ta][:, :], func=AF.Copy, scale=float(scl)),
                        reads=[tab_[i_ta]], writes=[rob[i_ro]])
                    dst = (qT if which == "q" else kT)[c, :, cols(tt)]
                    S.dma("sp", dst, ro[i_ro][:, :], [rob[i_ro]], [outb], rob[i_ro])
                    if which == "q":
                        i_rd = cnt["rod"] % 2
                        cnt["rod"] += 1
                        S.op("dve", lambda e, i_rd=i_rd, i_ta=i_ta, h=h: e.tensor_tensor(
                            out=rod[i_rd][:, :].rearrange("p (a b) -> p a b", a=4),
                            in0=ta[i_ta][:, :].rearrange("p (a b) -> p a b", a=4),
                            in1=qdec[:, h:h + 1, :].broadcast_to([128, 4, 128]), op=ALU.mult),
                            reads=[tab_[i_ta], dcb], writes=[rodb[i_rd]])
                        S.dma("sp", qdT[c, :, cols(tt)], rod[i_rd][:, :], [rodb[i_rd]], [outb], rodb[i_rd])
                    else:
                        ptr_t, ptr_b = cx.ps[5 + (c % 2)]
                        ptv = ptr_t[:, 0:256].bitcast(BF16)

                        def tr(e, ptv=ptv, i_ro=i_ro):
                            ins = None
                            for a in range(4):
                                ins = e.transpose(ptv[:, a * 128:(a + 1) * 128], ro[i_ro][:, a * 128:(a + 1) * 128], id_sb[:, :])
                            return ins
                        S.op("pe", tr, reads=[rob[i_ro], idb], writes=[ptr_b])
                        S.op("act", lambda e, ptv=ptv, ki=ki, c=c, h=h: e.activation(
                            out=kdt[ki][:, :, c * 128:(c + 1) * 128], in_=ptv.rearrange("p (a b) -> p a b", a=4),
                            func=AF.Copy, scale=kdec[:, h:h + 1]), reads=[ptr_b, dcb], writes=[kdtb[ki]])
            if which == "k":
                S.dma_group("sp", [(kd[tt * 4 + a, :, :], kdt[ki][:, a, :]) for a in range(4)], [kdtb[ki]], [outb], kdtb[ki])
    vo = [cx.sb("vo%d" % i, [128, 512], BF16) for i in range(3)]
    vob = [Buf("vo%d" % i) for i in range(3)]
    nv = 0
    for half in range(2):
        if DBG.get("r1", 9) < 3:
            break
        wi = load_wblock(2048 + half * 1024)
        for n in range(DBG.get("r1v", 16)):
            tt = n // 4
            hbufs = [R.hb[k][tt] for k in range(NCH)]
            for hh in range(2):
                pv_t, pv_b = cx.ps[1 + (nv % 4)]
                iv = nv % 3
                nv += 1

                def mmv(e, pv_t=pv_t, n=n, hh=hh, wi=wi):
                    ins = None
                    for k in range(NCH):
                        ins = e.matmul(pv_t[:, :], lhsT=R.hn_sb[:, k, 2 + n * 128:2 + (n + 1) * 128],
                                       rhs=wbuf[wi][:, k, hh * 512:(hh + 1) * 512], start=(k == 0), stop=(k == NCH - 1))
                    return ins
                S.op("pe", mmv, reads=hbufs + [wbb[wi]], writes=[pv_b])
                S.op("act", lambda e, iv=iv, pv_t=pv_t: e.activation(out=vo[iv][:, :], in_=pv_t[:, :], func=AF.Copy),
                     reads=[pv_b], writes=[vob[iv]])
                hd = half * 2 + hh
                S.dma("sp", v[n, :, hd * 512:(hd + 1) * 512], vo[iv][:, :], [vob[iv]], [outb], vob[iv])
    for half in range(2):
        if DBG.get("r1", 9) < 4:
            break
        wi = load_wblock(4096 + half * 1024)
        for tt in range(4):
            hbufs = [R.hb[k][tt] for k in range(NCH)]
            for cc in range(8):
                pg_t, pg_b = cx.ps[1 + (nv % 4)]
                iv = nv % 3
                nv += 1

                def mmg(e, pg_t=pg_t, cc=cc, tt=tt, wi=wi):
                    ins = None
                    for k in range(NCH):
                        ins = e.matmul(pg_t[:, :], lhsT=wbuf[wi][:, k, cc * 128:(cc + 1) * 128],
                                       rhs=R.hn_sb[:, k, cols(tt, 2)], start=(k == 0), stop=(k == NCH - 1))
                    return ins
                S.op("pe", mmg, reads=hbufs + [wbb[wi]], writes=[pg_b])
                S.op("act", lambda e, iv=iv, pg_t=pg_t: e.activation(out=vo[iv][:, :], in_=pg_t[:, :], func=AF.Silu),
                     reads=[pg_b], writes=[vob[iv]])
                S.dma("sp", sgT[half * 8 + cc, :, cols(tt)], vo[iv][:, :], [vob[iv]], [outb], vob[iv])
    S.final_wait("sp", [outb])
    S.emit()
    return cx


class RetState:
    def __init__(self, cx):
        self.Rf = cx.sb("Rf", [128, 4, 2, 512], F32)
        self.Rb = cx.sb("Rb", [128, 4, 2, 512], BF16)
        self.Rfb = [[Buf("Rf%d_%d" % (h, dc)) for dc in range(2)] for h in range(4)]
        self.Rbb = [[Buf("Rb%d_%d" % (h, dc)) for dc in range(2)] for h in range(4)]
        self.n = 0


def emit_ret_state_update(cx, St, kd_n, kdb, v_n, vb, banks, cdec, want_bf=True):
    S = cx.S
    for h in range(4):
        for dc in range(2):
            pt, pb = cx.ps[banks[St.n % len(banks)]]
            St.n += 1
            S.op("pe", lambda e, pt=pt, h=h, dc=dc: e.matmul(
                pt[:, :], lhsT=kd_n[:, h * 256 + dc * 128:h * 256 + (dc + 1) * 128], rhs=v_n[:, h * 512:(h + 1) * 512],
                start=True, stop=True), reads=[kdb, vb], writes=[pb])
            S.op("dve", lambda e, pt=pt, h=h, dc=dc: e.scalar_tensor_tensor(
                out=St.Rf[:, h, dc, :], in0=St.Rf[:, h, dc, :], scalar=float(cdec[h]), in1=pt[:, :],
                op0=ALU.mult, op1=ALU.add), reads=[pb, St.Rfb[h][dc]], writes=[St.Rfb[h][dc]])
            if want_bf:
                S.op("act", lambda e, h=h, dc=dc: e.activation(out=St.Rb[:, h, dc, :], in_=St.Rf[:, h, dc, :], func=AF.Copy),
                     reads=[St.Rfb[h][dc]], writes=[St.Rbb[h][dc]])


def build_ret1b_launch(cx=None):
    cx = cx or Ctx()
    S = cx.S
    kd = cx.din("kd", [16, 128, 1024], BF16)
    v = cx.din("v", [16, 128, 2048], BF16)
    Rend = cx.dout("Rend", [4, 2, 128, 512], F32)
    cx.psum_banks(8)
    St = RetState(cx)
    _, _, _, cdec = ret_consts()
    for h in range(4):
        for dc in range(2):
            S.op("dve", lambda e, h=h, dc=dc: e.memset(St.Rf[:, h, dc, :], 0.0), reads=[], writes=[St.Rfb[h][dc]])
    kdn = [cx.sb("kdn%d" % i, [128, 1024], BF16) for i in range(2)]
    kdnb = [Buf("kdn%d" % i) for i in range(2)]
    vn = [cx.sb("vn%d" % i, [128, 2048], BF16) for i in range(2)]
    vnb = [Buf("vn%d" % i) for i in range(2)]
    for n in range(16):
        i = n % 2
        S.dma("sp", kdn[i][:, :], kd[n, :, :], [], [kdnb[i]], kdnb[i])
        S.dma("sp", vn[i][:, :], v[n, :, :], [], [vnb[i]], vnb[i])
        emit_ret_state_update(cx, St, kdn[i], kdnb[i], vn[i], vnb[i], [0, 1, 2, 3], cdec, want_bf=False)
    outb = Buf("rend_out", multi=True)
    S.dma_group("sp", [(Rend[h, dc, :, :], St.Rf[:, h, dc, :]) for h in range(4) for dc in range(2)],
                [St.Rfb[h][dc] for h in range(4) for dc in range(2)], [outb], outb)
    S.final_wait("sp", [outb])
    S.emit()
    return cx


def build_ret2_launch(cx=None):
    cx = cx or Ctx()
    S = cx.S
    xT = cx.din("xT", [NCH, 128, T], F32)
    cf = cx.din("cf", [128, 8], F32)
    ones = cx.din("ones", [128, 128], F32)
    qT = cx.din("qT", [8, 128, T], BF16)
    qdT = cx.din("qdT", [8, 128, T], BF16)
    kT = cx.din("kT", [8, 128, T], BF16)
    kd = cx.din("kd", [16, 128, 1024], BF16)
    v = cx.din("v", [16, 128, 2048], BF16)
    sgT = cx.din("sgT", [16, 128, T], BF16)
    r0mode = cx.opts.get("r0", "input")
    R0 = cx.din("R0", [4, 2, 128, 512], F32) if r0mode != "zero" else None
    Rend_o = cx.dout("Rend", [4, 2, 128, 512], F32) if cx.opts.get("fused") else None
    dec_d = cx.din("dec", [128, 4, 128], F32)
    gn_d = cx.din("gn", [128, 16], F32)
    w_out = cx.din("w_out", [2048, D], F32)
    xo = cx.dout("xo", [NCH, 128, T], F32)
    cx.psum_banks(8)
    R = Res(cx, with_hn=False)
    R.load_consts(cf, ones)
    R.load_x(xT)
    St = RetState(cx)
    _, _, _, cdec = ret_consts()
    if r0mode == "zero":
        for h in range(4):
            for dc in range(2):
                S.op("dve", lambda e, h=h, dc=dc: e.memset(St.Rf[:, h, dc, :], 0.0), reads=[], writes=[St.Rfb[h][dc]])
    else:
        S.dma_group("sp", [(St.Rf[:, h, dc, :], R0[h, dc, :, :]) for h in range(4) for dc in range(2)],
                    [], [St.Rfb[h][dc] for h in range(4) for dc in range(2)], Buf("R0ld"))
    for h in range(4):
        for dc in range(2):
            S.op("act", lambda e, h=h, dc=dc: e.activation(out=St.Rb[:, h, dc, :], in_=St.Rf[:, h, dc, :], func=AF.Copy),
                 reads=[St.Rfb[h][dc]], writes=[St.Rbb[h][dc]])
    dec = cx.sb("dec_sb", [128, 4, 128], F32)
    gn = cx.sb("gn_sb", [128, 16], F32)
    dcb = Buf("dec")
    S.dma_group("sp", [(dec[:, :, :], dec_d[:, :, :]), (gn[:, :], gn_d[:, :])], [], [dcb], dcb)
    w_sb = cx.sb("wo_sb", [128, 16, D], BF16)
    wb = Buf("w_out")
    S.dma_group("pool", [(w_sb[:, kc, :], w_out[kc * 128:(kc + 1) * 128, :]) for kc in range(16)], [], [wb], wb)
    qn = [cx.sb("qn%d" % i, [128, 8, 128], BF16) for i in range(2)]
    qdn = [cx.sb("qdn%d" % i, [128, 8, 128], BF16) for i in range(2)]
    kn = [cx.sb("kn%d" % i, [128, 8, 128], BF16) for i in range(2)]
    kdn = [cx.sb("kdn%d" % i, [128, 1024], BF16) for i in range(2)]
    vn = [cx.sb("vn%d" % i, [128, 2048], BF16) for i in range(2)]
    sgn = [cx.sb("sgn%d" % i, [128, 16, 128], BF16) for i in range(2)]
    inb = [Buf("rin%d" % i) for i in range(2)]
    scm = [cx.sb("scm%d" % i, [128, 4, 128], BF16) for i in range(2)]
    scmb = [Buf("scm%d" % i) for i in range(2)]
    ot = [cx.sb("ot%d" % i, [128, 4, 128], BF16) for i in range(2)]
    otb = [Buf("ot%d" % i) for i in range(2)]
    osq = [cx.sb("osq%d" % i, [128, 4, 128], BF16) for i in range(2)]
    osqb = [Buf("osq%d" % i) for i in range(2)]
    stt = [cx.sb("stt%d" % i, [128, 2, 128], F32) for i in range(2)]
    sttb = [Buf("stt%d" % i) for i in range(2)]
    tmp = [cx.sb("gtmp%d" % i, [128, 4, 128], F32) for i in range(2)]
    tmpb = [Buf("gtmp%d" % i) for i in range(2)]
    yT = [cx.sb("yT%d" % i, [128, 16, 512], BF16) for i in range(2)]
    yTb = [[[Buf("yT%d_%d_%d" % (i, h, a)) for a in range(4)] for h in range(4)] for i in range(2)]
    nh = 0
    ny = 0
    for n in range(16):
        i = n % 2
        tt = n // 4
        a = n % 4
        yi = tt % 2
        csl = slice(n * 128, (n + 1) * 128)
        S.dma_group("sp", [(qn[i][:, :, :], qT[:, :, csl].rearrange("c p t -> p c t")),
                           (qdn[i][:, :, :], qdT[:, :, csl].rearrange("c p t -> p c t")),
                           (kn[i][:, :, :], kT[:, :, csl].rearrange("c p t -> p c t")),
                           (kdn[i][:, :], kd[n, :, :]), (vn[i][:, :], v[n, :, :]),
                           (sgn[i][:, :, :], sgT[:, :, csl].rearrange("c p t -> p c t"))],
                    [], [inb[i]], inb[i])
        ps_t, ps_b = cx.ps[0]

        def mmsc(e, i=i):
            ins = None
            for h in range(4):
                for dc in range(2):
                    ins = e.matmul(ps_t[:, h * 128:(h + 1) * 128], lhsT=kn[i][:, 2 * h + dc, :], rhs=qn[i][:, 2 * h + dc, :],
                                   start=(dc == 0), stop=(dc == 1))
            return ins
        S.op("pe", mmsc, reads=[inb[i]], writes=[ps_b])
        S.op("dve", lambda e, i=i: e.tensor_tensor(out=scm[i][:, :, :], in0=ps_t[:, :].rearrange("p (h q) -> p h q", h=4),
                                                 in1=dec[:, :, :], op=ALU.mult), reads=[ps_b, dcb], writes=[scmb[i]])
        for h in range(4):
            po_t, po_b = cx.ps[1 + (nh % 2)]
            pst_t, pst_b = cx.ps[3 + (nh % 2)]
            j2 = nh % 2
            nh += 1

            def mmo(e, po_t=po_t, i=i, h=h):
                ins = None
                for ec in range(4):
                    ins = e.matmul(po_t[:, ec * 128:(ec + 1) * 128], lhsT=vn[i][:, h * 512 + ec * 128:h * 512 + (ec + 1) * 128],
                                   rhs=scm[i][:, h, :], start=True, stop=False)
                    for dc in range(2):
                        ins = e.matmul(po_t[:, ec * 128:(ec + 1) * 128], lhsT=St.Rb[:, h, dc, ec * 128:(ec + 1) * 128],
                                       rhs=qdn[i][:, 2 * h + dc, :], start=False, stop=(dc == 1))
                return ins
            S.op("pe", mmo, reads=[inb[i], scmb[i], St.Rbb[h][0], St.Rbb[h][1]], writes=[po_b])
            o4 = po_t[:, :].rearrange("p (c q) -> p c q", c=4)
            S.op("act", lambda e, j2=j2, o4=o4: e.activation(out=ot[j2][:, :, :], in_=o4, func=AF.Copy),
                 reads=[po_b], writes=[otb[j2]])
            S.op("act", lambda e, j2=j2: e.activation(out=osq[j2][:, :, :], in_=ot[j2][:, :, :], func=AF.Square),
                 reads=[otb[j2]], writes=[osqb[j2]])

            def mmst(e, pst_t=pst_t, j2=j2):
                ins = None
                for ec in range(4):
                    ins = e.matmul(pst_t[:, 0:128], lhsT=R.ones[:, :], rhs=ot[j2][:, ec, :], start=(ec == 0), stop=(ec == 3))
                for ec in range(4):
                    ins = e.matmul(pst_t[:, 128:256], lhsT=R.ones[:, :], rhs=osq[j2][:, ec, :], start=(ec == 0), stop=(ec == 3))
                return ins
            S.op("pe", mmst, reads=[otb[j2], osqb[j2], R.onesb], writes=[pst_b])
            st = stt[j2]
            S.op("dve", lambda e, st=st, pst_t=pst_t: e.tensor_copy(out=st[:, :, :], in_=pst_t[:, 0:256].rearrange("p (a q) -> p a q", a=2)),
                 reads=[pst_b], writes=[sttb[j2]])
            S.op("dve", lambda e, st=st, j2=j2: e.tensor_tensor(out=tmp[j2][:, 0, :], in0=st[:, 0, :], in1=st[:, 0, :], op=ALU.mult),
                 reads=[sttb[j2]], writes=[tmpb[j2]])
            S.op("dve", lambda e, st=st, j2=j2: e.tensor_tensor(out=st[:, 1, :], in0=st[:, 1, :], in1=tmp[j2][:, 0, :], op=ALU.subtract),
                 reads=[sttb[j2], tmpb[j2]], writes=[sttb[j2]])
            S.op("act", lambda e, st=st: e.activation(out=st[:, 1, :], in_=st[:, 1, :], func=AF.Sqrt, bias=R.cf[:, 0:1], scale=1.0),
                 reads=[sttb[j2], R.epsb], writes=[sttb[j2]])
            S.op("dve", lambda e, st=st: e.reciprocal(out=st[:, 1, :], in_=st[:, 1, :]), reads=[sttb[j2]], writes=[sttb[j2]])
            S.op("dve", lambda e, st=st, j2=j2: e.tensor_tensor(out=tmp[j2][:, :, :], in0=ot[j2][:, :, :],
                                                              in1=st[:, 0:1, :].broadcast_to([128, 4, 128]), op=ALU.subtract),
                 reads=[otb[j2], sttb[j2]], writes=[tmpb[j2]])
            S.op("dve", lambda e, st=st, j2=j2: e.tensor_tensor(out=tmp[j2][:, :, :], in0=tmp[j2][:, :, :],
                                                              in1=st[:, 1:2, :].broadcast_to([128, 4, 128]), op=ALU.mult),
                 reads=[tmpb[j2], sttb[j2]], writes=[tmpb[j2]])
            for ec in range(4):
                S.op("dve", lambda e, j2=j2, ec=ec, h=h, i=i, yi=yi, a=a: e.scalar_tensor_tensor(
                    out=yT[yi][:, 4 * h + ec, a * 128:(a + 1) * 128], in0=tmp[j2][:, ec, :], scalar=gn[:, 4 * h + ec:4 * h + ec + 1],
                    in1=sgn[i][:, 4 * h + ec, :], op0=ALU.mult, op1=ALU.mult),
                    reads=[tmpb[j2], dcb, inb[i]], writes=[yTb[yi][h][a]])
        emit_ret_state_update(cx, St, kdn[i], inb[i], vn[i], inb[i], [5, 6], cdec, want_bf=True)
        if a == 3:
            for m in range(NCH):
                py_t, py_b = cx.ps[7]

                def mmy(e, m=m, yi=yi, py_t=py_t):
                    ins = None
                    for c in range(16):
                        ins = e.matmul(py_t[:, :], lhsT=w_sb[:, c, m * 128:(m + 1) * 128], rhs=yT[yi][:, c, :],
                                       start=(c == 0), stop=(c == 15))
                    return ins
                S.op("pe", mmy, reads=[yTb[yi][h][a2] for h in range(4) for a2 in range(4)] + [wb], writes=[py_b])
                xo_ = R.x_sb[:, m, cols(tt)]
                S.op("dve", lambda e, xo_=xo_, py_t=py_t: e.tensor_tensor(out=xo_, in0=xo_, in1=py_t[:, :], op=ALU.add),
                     reads=[py_b, R.xb[m][tt]], writes=[R.xb[m][tt]])
    R.store_x(xo)
    if Rend_o is not None:
        S.dma_group("sp", [(Rend_o[h, dc, :, :], St.Rf[:, h, dc, :]) for h in range(4) for dc in range(2)],
                    [St.Rfb[h][dc] for h in range(4) for dc in range(2)], [R.outb], Buf("rend_o"))
    S.final_wait("sp", [R.outb])
    S.emit()
    return cx


def run_ret_layer(xTs, pos_c, g, w_in, w_out, gn_g):
    nco = len(xTs)
    cst = _consts()
    dec, qdec, kdec, cdec = ret_consts()
    ims = [dict(cst, xT=xTs[c], ident=np.eye(128, dtype=np.float32), g=vec_fm(g, 8), w_in=w_in, pos=pos_c[c],
                qdec=qdec, kdec=kdec) for c in range(nco)]
    r1 = _run("ret1", build_ret1_launch, ims)
    rb = _run("ret1b", build_ret1b_launch, [{"kd": np.asarray(r1[c]["kd"]), "v": np.asarray(r1[c]["v"])} for c in range(nco)])
    ims = []
    for c in range(nco):
        R0 = np.asarray(rb[c - 1]["Rend"]) if c % 2 == 1 else np.zeros((4, 2, 128, 512), np.float32)
        ims.append({"cf": cst["cf"], "ones": np.full((128, 128), 1.0 / 512, np.float32), "xT": xTs[c],
                    "qT": np.asarray(r1[c]["qT"]), "qdT": np.asarray(r1[c]["qdT"]), "kT": np.asarray(r1[c]["kT"]),
                    "kd": np.asarray(r1[c]["kd"]), "v": np.asarray(r1[c]["v"]), "sgT": np.asarray(r1[c]["sgT"]),
                    "R0": R0, "dec": dec, "gn": vec_fm(gn_g, 16), "w_out": w_out})
    r2 = _run("ret2", build_ret2_launch, ims)
    return [np.asarray(r2[c]["xo"]) for c in range(nco)]


NEG_BIG = -30000.0


def build_moba_launch(cx=None):
    cx = cx or Ctx()
    S = cx.S
    NKT = 32
    fused = cx.opts.get("fused", False)
    NG = 4 if fused else 2
    if not fused:
        qTc = cx.din("qTc", [8, 64, SEQ], BF16)
        kA = cx.din("kA", [2, 80, SEQ], BF16)
        vaug = cx.din("vaug", [2, 128, NKT, 192], BF16)
        oTc = cx.dout("oTc", [4, 128, SEQ], BF16)
    else:
        qH = [cx.io["qT_%d" % hf].rearrange("c (h d) t -> (c h) d t", h=2) for hf in range(2)]
        kH = [cx.io["kT_%d" % hf].rearrange("c (h d) t -> (c h) d t", h=2) for hf in range(2)]
        vH = [cx.io["v_%d" % hf].rearrange("n p (g d) -> n p g d", g=4) for hf in range(2)]
        oH = [cx.io["oT_%d" % hf] for hf in range(2)]
        onehot = cx.io["onehot"]
    ident = cx.din("ident", [128, 128], F32)
    sb1_d = cx.din("sb1", [128, 16, 16], F32)
    cmask = cx.din("cmask", [128, 2, 256], BF16)
    cx.psum_banks(8)
    id_sb = cx.sb("id_sb", [128, 128], BF16)
    idb = Buf("ident")
    S.dma("pool", id_sb[:, :], ident[:, :], [], [idb], idb)
    sb1 = cx.sb("sb1_sb", [128, 16, 16], F32)
    cm = cx.sb("cm_sb", [128, 2, 256], BF16)
    cb_ = Buf("mconst")
    S.dma_group("sp", [(sb1[:, :, :], sb1_d[:, :, :]), (cm[:, :, :], cmask[:, :, :])], [], [cb_], cb_)
    kAs = [cx.sb("kA%d" % i, [80, SEQ], BF16) for i in range(NG)]
    kAb = [Buf("kA%d" % i) for i in range(NG)]
    vg = [cx.sb("vg%d" % i, [128, NKT, 192], BF16) for i in range(NG)]
    vgb = [Buf("vg%d" % i) for i in range(NG)]
    for gl in range(NG):
        if not fused:
            S.dma("sp", kAs[gl][:, :], kA[gl, :, :], [], [kAb[gl]], kAb[gl])
            S.dma("sp", vg[gl][:, :, :], vaug[gl, :, :, :], [], [vgb[gl]], vgb[gl])
        else:
            S.dma_group("sp", [(kAs[gl][0:64, hf * T:(hf + 1) * T], kH[hf][gl, :, :]) for hf in range(2)]
                        + [(kAs[gl][64:80, :], onehot[:, :])], [], [kAb[gl]], kAb[gl])
            S.op("dve", lambda e, gl=gl: e.memset(vg[gl][:, :, :], 1.0), reads=[], writes=[vgb[gl]])
            S.dma_group("sp", [(vg[gl][:, hf * 16 + 4 * a:hf * 16 + 4 * a + 4, 64:128],
                                vH[hf][4 * a:4 * a + 4, :, gl, :].rearrange("n p d -> p n d"))
                               for hf in range(2) for a in range(4)], [], [vgb[gl]], vgb[gl])
    kms = cx.sb("kms", [64, NG, 16], F32)
    kmT = cx.sb("kmT", [64, NG, 16], BF16)
    kmb = Buf("km")
    for gl in range(NG):
        S.op("dve", lambda e, gl=gl: e.tensor_reduce(out=kms[:, gl, :], in_=kAs[gl][0:64, :].rearrange("p (n k) -> p n k", k=256),
                                                    axis=AX.X, op=ALU.add), reads=[kAb[gl]], writes=[kmb])
    S.op("act", lambda e: e.activation(out=kmT[:, :, :], in_=kms[:, :, :], func=AF.Copy, scale=1.0 / 256.0),
         reads=[kmb], writes=[kmb])
    QA = [cx.sb("QA%d" % i, [80, 4, 256], BF16) for i in range(2)]
    QAq = [Buf("QAq%d" % i) for i in range(2)]
    QAbias = [Buf("QAb%d" % i) for i in range(2)]
    gm = cx.sb("gm", [128, 8, 16], F32)
    gmb = Buf("gm")
    m8 = cx.sb("m8", [128, 8, 8], F32)
    m8b = Buf("m8")
    bq = [cx.sb("bq%d" % i, [128, 2, 4, 32], BF16) for i in range(2)]
    bqb = [Buf("bq%d" % i) for i in range(2)]
    for i in range(2):
        S.op("dve", lambda e, i=i: e.memset(bq[i][:, :, :, :], 0.0), reads=[], writes=[bqb[i]])
    P = [cx.sb("P%d" % i, [128, 2, 256], BF16) for i in range(4)]
    Pb = [Buf("P%d" % i) for i in range(4)]
    rec = [cx.sb("rec%d" % i, [128, 2, 256], F32) for i in range(2)]
    recb = [Buf("rec%d" % i) for i in range(2)]
    oo = [cx.sb("oo%d" % i, [128, 2, 256], BF16) for i in range(2)]
    oob = [Buf("oo%d" % i) for i in range(2)]
    outb = Buf("moba_out", multi=True)
    iters = [(qb, gl) for qb in range(16) for gl in range(NG)]
    st_ = {"nps": 0}

    def gate(k):
        qb, gl = iters[k]
        qi = k % 2
        if not fused:
            qsrc = [qTc[4 * gl + hl, :, qb * 256:(qb + 1) * 256] for hl in range(4)]
        else:
            qsrc = [qH[qb // 8][4 * gl + hl, :, (qb % 8) * 256:(qb % 8 + 1) * 256] for hl in range(4)]
        S.dma_group("sp", [(QA[qi][0:64, hl, :], qsrc[hl]) for hl in range(4)],
                    [], [QAq[qi]], QAq[qi])
        pg_t, pg_b = cx.ps[7]

        def mmg(e, qi=qi, gl=gl):
            ins = None
            for qt in range(2):
                for hl in range(4):
                    idx = qt * 4 + hl
                    ins = e.matmul(pg_t[:, idx * 16:(idx + 1) * 16], lhsT=QA[qi][0:64, hl, qt * 128:(qt + 1) * 128],
                                   rhs=kmT[:, gl, :], start=True, stop=True)
            return ins
        S.op("pe", mmg, reads=[QAq[qi], kmb], writes=[pg_b])
        S.op("dve", lambda e, qb=qb: e.tensor_tensor(out=gm[:, :, :], in0=pg_t[:, 0:128].rearrange("p (a n) -> p a n", a=8),
                                                    in1=sb1[:, qb:qb + 1, :].broadcast_to([128, 8, 16]), op=ALU.add),
             reads=[pg_b, cb_], writes=[gmb])
        for idx in range(8):
            S.op("dve", lambda e, idx=idx: e.max(out=m8[:, idx, :], in_=gm[:, idx, :]), reads=[gmb], writes=[m8b])
        for idx in range(8):
            S.op("dve", lambda e, idx=idx, qi=qi: e.tensor_scalar(
                out=bq[qi][:, idx // 4, idx % 4, 0:16], in0=gm[:, idx, :], scalar1=m8[:, idx, 2:3], scalar2=NEG_BIG,
                op0=ALU.is_lt, op1=ALU.mult), reads=[gmb, m8b], writes=[bqb[qi]])
        S.op("dve", lambda e, qi=qi, qb=qb: e.memset(bq[qi][:, :, :, qb:qb + 1], 0.0), reads=[], writes=[bqb[qi]])
        ptr_t, ptr_b = cx.ps[0]
        ptv = ptr_t[:, 0:128].bitcast(BF16)

        def tr(e, qi=qi, ptv=ptv):
            ins = None
            for qt in range(2):
                ins = e.transpose(ptv[:, qt * 128:(qt + 1) * 128], bq[qi][:, qt, :, :].rearrange("p h n -> p (h n)"), id_sb[:, :])
            return ins
        S.op("pe", tr, reads=[bqb[qi], idb], writes=[ptr_b])
        for hl in range(4):
            S.op("dve", lambda e, hl=hl, qi=qi, ptv=ptv: e.tensor_copy(out=QA[qi][64:80, hl, :], in_=ptv[32 * hl:32 * hl + 16, :]),
                 reads=[ptr_b], writes=[QAbias[qi]])

    def att(k):
        qb, gl = iters[k]
        qi = k % 2
        nps = st_["nps"]
        njt = 2 * (qb + 1)
        for j in range(njt):
            own = (j // 2 == qb)
            for par in range(2):
                ps_t, ps_b = cx.ps[1 + (nps % 4)]
                pi = nps % 4
                nps += 1
                po_t, po_b = cx.ps[5 + par]

                def mmqk(e, ps_t=ps_t, par=par, j=j, gl=gl, qi=qi):
                    ins = None
                    for hh in range(2):
                        ins = e.matmul(ps_t[:, hh * 256:(hh + 1) * 256], lhsT=kAs[gl][0:80, j * 128:(j + 1) * 128],
                                       rhs=QA[qi][0:80, 2 * hh + par, :], start=True, stop=True)
                    return ins
                S.op("pe", mmqk, reads=[kAb[gl], QAq[qi], QAbias[qi]], writes=[ps_b])
                Pv = P[pi]
                S.op("act", lambda e, Pv=Pv, ps_t=ps_t: e.activation(
                    out=Pv[:, :, :], in_=ps_t[:, :].rearrange("p (h q) -> p h q", h=2), func=AF.Exp, scale=0.125),
                    reads=[ps_b], writes=[Pb[pi]])
                if own:
                    kt = j % 2
                    S.op("pool", lambda e, Pv=Pv, kt=kt: e.tensor_tensor(
                        out=Pv[:, :, :], in0=Pv[:, :, :], in1=cm[:, kt:kt + 1, :].broadcast_to([128, 2, 256]), op=ALU.mult),
                        reads=[Pb[pi], cb_], writes=[Pb[pi]])
                vsl = slice(64, 192) if par == 0 else slice(0, 128)
                S.op("pe", lambda e, po_t=po_t, Pv=Pv, vsl=vsl, j=j, gl=gl, njt=njt: e.matmul(
                    po_t[:, :], lhsT=vg[gl][:, j, vsl], rhs=Pv[:, :, :].rearrange("p h q -> p (h q)"),
                    start=(j == 0), stop=(j == njt - 1)), reads=[Pb[pi], vgb[gl]], writes=[po_b])
        oi = k % 2
        for par in range(2):
            po_t, po_b = cx.ps[5 + par]
            nlo, dlo = (0, 64) if par == 0 else (64, 0)
            num = po_t[nlo:nlo + 64, :].rearrange("p (h q) -> p h q", h=2)
            den = po_t[dlo:dlo + 64, :].rearrange("p (h q) -> p h q", h=2)
            rc = rec[par][nlo:nlo + 64, :, :]
            S.op("dve", lambda e, rc=rc, den=den: e.reciprocal(out=rc, in_=den), reads=[po_b], writes=[recb[par]])
            S.op("dve", lambda e, num=num, rc=rc, oi=oi, nlo=nlo: e.tensor_tensor(
                out=oo[oi][nlo:nlo + 64, :, :], in0=num, in1=rc, op=ALU.mult),
                reads=[po_b, recb[par]], writes=[oob[oi]])
        if not fused:
            odst = [oTc[2 * gl + hh, :, qb * 256:(qb + 1) * 256] for hh in range(2)]
        else:
            odst = [oH[qb // 8][2 * gl + hh, :, (qb % 8) * 256:(qb % 8 + 1) * 256] for hh in range(2)]
        S.dma_group("sp", [(odst[hh], oo[oi][:, hh, :]) for hh in range(2)],
                    [oob[oi]], [outb], oob[oi])

        st_["nps"] = nps

    gate(0)
    for k in range(len(iters)):
        if k + 1 < len(iters):
            gate(k + 1)
        att(k)
    S.final_wait("sp", [outb])
    S.emit()
    return cx


def build_outproj_launch(cx=None):
    cx = cx or Ctx()
    S = cx.S
    xT = cx.din("xT", [NCH, 128, T], F32)
    cf = cx.din("cf", [128, 8], F32)
    ones = cx.din("ones", [128, 128], F32)
    oT_d = cx.din("oT", [NCH, 128, T], BF16)
    w_out = cx.din("w_out", [D, D], F32)
    xo = cx.dout("xo", [NCH, 128, T], F32)
    cx.psum_banks(8)
    R = Res(cx, with_hn=False)
    R.load_consts(cf, ones)
    R.load_x(xT)
    w_sb = cx.sb("wo_sb", [128, NCH, D], BF16)
    wb = Buf("w_out")
    S.dma_group("pool", [(w_sb[:, kc, :], w_out[kc * 128:(kc + 1) * 128, :]) for kc in range(NCH)], [], [wb], wb)
    oT = cx.sb("oT_sb", [128, NCH, T], BF16)
    oTb = [[Buf("oT%d_%d" % (c, tt)) for tt in range(4)] for c in range(NCH)]
    oTld = [Buf("oTld%d" % tt) for tt in range(4)]
    for tt in range(4):
        S.dma_group("sp", [(oT[:, c, cols(tt)], oT_d[c, :, cols(tt)]) for c in range(NCH)], [],
                    [oTb[c][tt] for c in range(NCH)], oTld[tt])
    emit_outproj(cx, R, oT, oTb, w_sb, wb, NCH, [1, 2, 3, 4])
    R.store_x(xo)
    S.final_wait("sp", [R.outb])
    S.emit()
    return cx


def moba_consts():
    sb1 = np.zeros((128, 16, 16), np.float32)
    for qb in range(16):
        sb1[:, qb, qb:] = -1e30
    k = np.arange(128)[:, None]
    q = np.arange(256)[None, :]
    cm = np.zeros((128, 2, 256), np.float32)
    cm[:, 0, :] = (k <= q)
    cm[:, 1, :] = (k + 128 <= q)
    return sb1, cm.astype(BF)


def host_moba_exchange(qkv, ncores):
    outs = []
    onehot = np.zeros((16, SEQ), np.float32)
    for n in range(16):
        onehot[n, n * 256:(n + 1) * 256] = 1.0
    for c in range(ncores):
        half = c % 2
        c0 = c - half
        q_all = np.concatenate([np.asarray(qkv[c0 + hh]["qT"]).reshape(16, 64, T) for hh in range(2)], axis=2)
        k_all = np.concatenate([np.asarray(qkv[c0 + hh]["kT"]).reshape(4, 64, T) for hh in range(2)], axis=2)
        v_all = np.concatenate([np.asarray(qkv[c0 + hh]["v"]).reshape(T, 4, 64) for hh in range(2)], axis=0)
        kA = np.zeros((2, 80, SEQ), BF)
        vaug = np.ones((2, 128, 32, 192), BF)
        for gl in range(2):
            g = 2 * half + gl
            kA[gl, 0:64] = k_all[g]
            kA[gl, 64:80] = onehot.astype(BF)
            vaug[gl, :, :, 64:128] = v_all[:, g, :].reshape(32, 128, 64).transpose(1, 0, 2)
        outs.append({"qTc": np.ascontiguousarray(q_all[8 * half:8 * half + 8]), "kA": kA, "vaug": vaug})
    return outs


def run_moba_layer(xTs, pos_c, g, w_in, w_out):
    nco = len(xTs)
    cst = _consts()
    ims = [dict(cst, xT=xTs[c], perm=make_perm(), g=vec_fm(g, 8), w_in=w_in, pos=pos_c[c]) for c in range(nco)]
    qkv = _run("qkv", build_qkv_launch, ims)
    ex = host_moba_exchange(qkv, nco)
    sb1, cm = moba_consts()
    ims = [dict(ex[c], ident=np.eye(128, dtype=np.float32), sb1=sb1, cmask=cm) for c in range(nco)]
    r = _run("moba", build_moba_launch, ims)
    ims = []
    for c in range(nco):
        half = c % 2
        c0 = c - half
        oT = np.concatenate([np.asarray(r[c0 + hh]["oTc"])[:, :, half * T:(half + 1) * T] for hh in range(2)], axis=0)
        ims.append(dict(cst, xT=xTs[c], oT=np.ascontiguousarray(oT), w_out=w_out))
    r2 = _run("outproj", build_outproj_launch, ims)
    return [np.asarray(r2[c]["xo"]) for c in range(nco)]


def build_fnorm_launch(cx=None):
    cx = cx or Ctx()
    S = cx.S
    xT = cx.din("xT", [NCH, 128, T], F32)
    cf = cx.din("cf", [128, 8], F32)
    ones = cx.din("ones", [128, 128], F32)
    g = cx.din("g", [128, NCH], F32)
    xo = cx.dout("xo", [NCH, 128, T], F32)
    cx.psum_banks(8)
    R = Res(cx)
    R.load_consts(cf, ones)
    R.load_x(xT)
    g_sb = cx.sb("g_sb", [128, NCH], F32)
    gb = Buf("g")
    S.dma("sp", g_sb[:, :], g[:, :], [], [gb], gb)
    emit_norm(cx, R, g_sb, gb, 0, out_f32=True)
    R.store_x(xo)
    S.final_wait("sp", [R.outb])
    S.emit()
    return cx


def run_ffn_layer(xTs, g, w_a, w_b, conv_w, conv_b, w_down):
    nco = len(xTs)
    cst = _consts()
    cw = np.ascontiguousarray(conv_w.reshape(3, NFF, 128).transpose(2, 0, 1))
    cb = np.ascontiguousarray(conv_b.reshape(NFF, 128).T)
    ims = []
    for c in range(nco):
        xh = np.zeros((128, NCH, 2), np.float32)
        if c % 2 == 1:
            xh = np.ascontiguousarray(xTs[c - 1][:, :, T - 2:].transpose(1, 0, 2))
        ims.append(dict(cst, xT=xTs[c], xh=xh, w_a=w_a, w_b=w_b, w_down=w_down, cw=cw, cb=cb, g=vec_fm(g, 8)))
    r = _run("ffn", build_ffn_launch, ims)
    return [np.asarray(r[c]["xo"]) for c in range(nco)]


def run_fnorm(xTs, g):
    cst = _consts()
    r = _run("fnorm", build_fnorm_launch, [dict(cst, xT=xTs[c], g=vec_fm(g, 8)) for c in range(len(xTs))])
    return [np.asarray(r[c]["xo"]) for c in range(len(xTs))]


def kernel_unfused(**inp):
    f32 = lambda a: np.ascontiguousarray(np.asarray(a, dtype=np.float32))
    x = f32(inp["x"])
    pos = np.ascontiguousarray(np.asarray(inp["positions"], dtype=np.int32))
    xTs, pos_c = [], []
    for c in range(NCORES):
        b, half = c // 2, c % 2
        xTs.append(fm(x[b, half * T:(half + 1) * T]))
        pos_c.append(np.ascontiguousarray(pos[b, half * T:(half + 1) * T]))
    for i in range(4):
        p = "l%d_" % i
        m = i % 3
        if m == 0:
            xTs = run_swa_layer(xTs, pos_c, f32(inp[p + "attn_norm"]), f32(inp[p + "w_in"]), f32(inp[p + "w_out"]),
                                f32(inp[p + "sinks"]))
        elif m == 1:
            xTs = run_ret_layer(xTs, pos_c, f32(inp[p + "attn_norm"]), f32(inp[p + "w_in"]), f32(inp[p + "w_out"]),
                                f32(inp[p + "gn_g"]))
        else:
            xTs = run_moba_layer(xTs, pos_c, f32(inp[p + "attn_norm"]), f32(inp[p + "w_in"]), f32(inp[p + "w_out"]))
        xTs = run_ffn_layer(xTs, f32(inp[p + "ffn_norm"]), f32(inp[p + "w_a"]), f32(inp[p + "w_b"]),
                            f32(inp[p + "conv_w"]), f32(inp[p + "conv_b"]), f32(inp[p + "w_down"]))
    xTs = run_fnorm(xTs, f32(inp["final_norm"]))
    out = np.empty((BATCH, SEQ, D), np.float32)
    for c in range(NCORES):
        b, half = c // 2, c % 2
        out[b, half * T:(half + 1) * T] = xTs[c].reshape(D, T).T
    return out


LAYER_KEYS = [
    ("l0_attn_norm", "l0_w_in", "l0_w_out", "l0_sinks", "l0_ffn_norm", "l0_w_a", "l0_w_b", "l0_conv_w", "l0_conv_b", "l0_w_down"),
    ("l1_attn_norm", "l1_w_in", "l1_w_out", "l1_gn_g", "l1_ffn_norm", "l1_w_a", "l1_w_b", "l1_conv_w", "l1_conv_b", "l1_w_down"),
    ("l2_attn_norm", "l2_w_in", "l2_w_out", None, "l2_ffn_norm", "l2_w_a", "l2_w_b", "l2_conv_w", "l2_conv_b", "l2_w_down"),
    ("l3_attn_norm", "l3_w_in", "l3_w_out", "l3_sinks", "l3_ffn_norm", "l3_w_a", "l3_w_b", "l3_conv_w", "l3_conv_b", "l3_w_down"),
]


def build_fused():
    nc = bass.Bass("TRN2", target_bir_lowering=False)

    def ext(name, shape, dt):
        return nc.dram_tensor(name, shape, dt, kind="ExternalInput").ap()

    def scr(name, shape, dt):
        return nc.dram_tensor(name, shape, dt, kind="Internal").ap()

    X = [ext("x_%d" % h, [NCH, 128, T], F32) for h in range(2)]
    POS = [ext("pos_%d" % h, [T], I32) for h in range(2)]
    OUT = [nc.dram_tensor("out_%d" % h, [NCH, 128, T], F32, kind="ExternalOutput").ap() for h in range(2)]
    C = {"cf": ext("cf", [128, 8], F32), "ones": ext("ones", [128, 128], F32), "ones512": ext("ones512", [128, 128], F32),
         "perm": ext("perm", [128, 128], F32), "ident": ext("ident", [128, 128], F32),
         "masks_0": ext("masks_0", [128, 2, 2, 128], BF16), "masks_1": ext("masks_1", [128, 2, 2, 128], BF16),
         "qdec": ext("qdec", [128, 4, 128], F32), "kdec": ext("kdec", [128, 4], F32), "dec": ext("dec", [128, 4, 128], F32),
         "sb1": ext("sb1", [128, 16, 16], F32), "cmask": ext("cmask", [128, 2, 256], BF16),
         "onehot": ext("onehot", [16, SEQ], BF16), "gfinal": ext("gfinal", [128, NCH], F32)}
    W = []
    for i in range(4):
        m = i % 3
        win = 6144 if m == 1 else 1536
        wo_in = 2048 if m == 1 else D
        d = {"g_attn": ext("l%d_g_attn" % i, [128, NCH], F32), "w_in": ext("l%d_w_in" % i, [D, win], F32),
             "w_out": ext("l%d_w_out" % i, [wo_in, D], F32), "g_ffn": ext("l%d_g_ffn" % i, [128, NCH], F32),
             "w_a": ext("l%d_w_a" % i, [D, DFF], F32), "w_b": ext("l%d_w_b" % i, [D, DFF], F32),
             "w_down": ext("l%d_w_down" % i, [DFF, D], F32), "cw": ext("l%d_cw" % i, [128, 3, NFF], F32),
             "cb": ext("l%d_cb" % i, [128, NFF], F32)}
        if m == 0:
            d["sinks"] = ext("l%d_sinks" % i, [16], F32)
        if m == 1:
            d["gn"] = ext("l%d_gn" % i, [128, 16], F32)
        W.append(d)
    xs = [[scr("xs_%d_%d" % (h, k), [NCH, 128, T], F32) for k in range(2)] for h in range(2)]
    qT = [scr("qT_%d" % h, [8, 128, T], BF16) for h in range(2)]
    kT = [scr("kT_%d" % h, [2, 128, T], BF16) for h in range(2)]
    vv = [scr("v_%d" % h, [16, 128, 256], BF16) for h in range(2)]
    oT = [scr("oT_%d" % h, [8, 128, T], BF16) for h in range(2)]
    rq = [scr("rq_%d" % h, [8, 128, T], BF16) for h in range(2)]
    rqd = [scr("rqd_%d" % h, [8, 128, T], BF16) for h in range(2)]
    rk = [scr("rk_%d" % h, [8, 128, T], BF16) for h in range(2)]
    rkd = [scr("rkd_%d" % h, [16, 128, 1024], BF16) for h in range(2)]
    rv = [scr("rv_%d" % h, [16, 128, 2048], BF16) for h in range(2)]
    rsg = [scr("rsg_%d" % h, [16, 128, T], BF16) for h in range(2)]
    Rend = [scr("Rend_%d" % h, [4, 2, 128, 512], F32) for h in range(2)]
    cur = [X[0], X[1]]
    tog = [0, 0]
    nph = [0]
    es_top = ExitStack()
    x_res = es_top.enter_context(nc.sbuf_tensor("x_res", [128, NCH, T], F32))
    x_tail = scr("x_tail", [128, NCH, 2], F32)

    def phase(name, builder, io, opts=None):
        nph[0] += 1
        o = dict(opts or {})
        o["x_sb"] = x_res
        with nc.cleanup_on_exit():
            cx = Ctx(nc, io, prefix="p%d%s_" % (nph[0], name), opts=o)
            builder(cx)
            cx.es.close()
            nc.all_engine_barrier()

    def nxt(h):
        b = xs[h][tog[h]]
        tog[h] ^= 1
        return b

    base = {"cf": C["cf"], "ones": C["ones"]}
    for i in range(4):
        m = i % 3
        w = W[i]
        last = (i == 3)
        if m == 2:
            for h in range(2):
                phase("qkv%d" % h, build_qkv_launch, dict(base, xT=cur[h], perm=C["perm"], g=w["g_attn"], w_in=w["w_in"],
                                                          pos=POS[h], qT=qT[h], kT=kT[h], v=vv[h]))
            io = {"ident": C["ident"], "sb1": C["sb1"], "cmask": C["cmask"], "onehot": C["onehot"]}
            for h in range(2):
                io["qT_%d" % h] = qT[h]
                io["kT_%d" % h] = kT[h]
                io["v_%d" % h] = vv[h]
                io["oT_%d" % h] = oT[h]
            phase("moba", build_moba_launch, io, {"fused": True})
        for h in range(2):
            mix_opts = {"store_x": False}
            if h == 0:
                mix_opts["tail_out"] = x_tail
            dummy = xs[h][0]
            if m == 0:
                phase("qkv%d" % h, build_qkv_launch, dict(base, xT=cur[h], perm=C["perm"], g=w["g_attn"], w_in=w["w_in"],
                                                          pos=POS[h], qT=qT[h], kT=kT[h], v=vv[h]))
                io = dict(base, xT=cur[h], qT=qT[h].rearrange("c (h d) t -> (c h) d t", h=2), masks=C["masks_%d" % h],
                          sinks=w["sinks"], w_out=w["w_out"], xo=dummy, kT_own=kT[h], v_own=vv[h])
                if h == 1:
                    io["kT_prev"] = kT[0]
                    io["v_prev"] = vv[0]
                phase("swa%d" % h, build_swa_launch, io, dict(mix_opts, fused=True, half=h, load_x=False))
            elif m == 1:
                phase("ret1%d" % h, build_ret1_launch, dict(base, xT=cur[h], ident=C["ident"], g=w["g_attn"], w_in=w["w_in"],
                                                            pos=POS[h], qdec=C["qdec"], kdec=C["kdec"], qT=rq[h], qdT=rqd[h],
                                                            kT=rk[h], kd=rkd[h], v=rv[h], sgT=rsg[h]))
                io = {"cf": C["cf"], "ones": C["ones512"], "xT": cur[h], "qT": rq[h], "qdT": rqd[h], "kT": rk[h], "kd": rkd[h],
                      "v": rv[h], "sgT": rsg[h], "dec": C["dec"], "gn": w["gn"], "w_out": w["w_out"], "xo": dummy, "Rend": Rend[h]}
                if h == 1:
                    io["R0"] = Rend[0]
                phase("ret2%d" % h, build_ret2_launch, io, dict(mix_opts, fused=True, r0="zero" if h == 0 else "input", load_x=False))
            else:
                phase("oproj%d" % h, build_outproj_launch, dict(base, xT=cur[h], oT=oT[h], w_out=w["w_out"], xo=dummy), dict(mix_opts))
            xo = nxt(h)
            io = dict(base, xT=cur[h], w_a=w["w_a"], w_b=w["w_b"], w_down=w["w_down"], cw=w["cw"], cb=w["cb"], g=w["g_ffn"], xo=xo)
            if h == 1:
                io["x_tail"] = x_tail
            phase("ffn%d" % h, build_ffn_launch, io, {"halo": "zero" if h == 0 else "prev", "load_x": False, "store_x": not last})
            cur[h] = xo
            if last:
                phase("fnorm%d" % h, build_fnorm_launch, dict(base, xT=cur[h], g=C["gfinal"], xo=OUT[h]), {"load_x": False})
    es_top.close()
    return nc


_FUSED = {}


def kernel(**inp):
    f32 = lambda a: np.ascontiguousarray(np.asarray(a, dtype=np.float32))
    x = f32(inp["x"])
    pos = np.ascontiguousarray(np.asarray(inp["positions"], dtype=np.int32))
    if "nc" not in _FUSED:
        _FUSED["nc"] = build_fused()
    nc = _FUSED["nc"]
    dec, qdec, kdec, _ = ret_consts()
    sb1, cm = moba_consts()
    onehot = np.zeros((16, SEQ), np.float32)
    for n in range(16):
        onehot[n, n * 256:(n + 1) * 256] = 1.0
    shared = {"cf": make_cf(), "ones": np.ones((128, 128), np.float32), "ones512": np.full((128, 128), 1.0 / 512, np.float32),
              "perm": make_perm(), "ident": np.eye(128, dtype=np.float32), "masks_0": _swa_masks(0), "masks_1": _swa_masks(1),
              "qdec": qdec, "kdec": kdec, "dec": dec, "sb1": sb1, "cmask": cm, "onehot": onehot.astype(BF),
              "gfinal": vec_fm(f32(inp["final_norm"]), 8)}
    for i, keys in enumerate(LAYER_KEYS):
        k_an, k_win, k_wout, k_x, k_fn, k_wa, k_wb, k_cw, k_cb, k_wd = keys
        shared["l%d_g_attn" % i] = vec_fm(f32(inp[k_an]), 8)
        shared["l%d_w_in" % i] = f32(inp[k_win])
        shared["l%d_w_out" % i] = f32(inp[k_wout])
        shared["l%d_g_ffn" % i] = vec_fm(f32(inp[k_fn]), 8)
        shared["l%d_w_a" % i] = f32(inp[k_wa])
        shared["l%d_w_b" % i] = f32(inp[k_wb])
        shared["l%d_w_down" % i] = f32(inp[k_wd])
        shared["l%d_cw" % i] = np.ascontiguousarray(f32(inp[k_cw]).reshape(3, NFF, 128).transpose(2, 0, 1))
        shared["l%d_cb" % i] = np.ascontiguousarray(f32(inp[k_cb]).reshape(NFF, 128).T)
        if i % 3 == 0:
            shared["l%d_sinks" % i] = f32(inp[k_x])
        if i % 3 == 1:
            shared["l%d_gn" % i] = vec_fm(f32(inp[k_x]), 16)
    in_maps = []
    for c in range(NCORES):
        b = c % BATCH
        im = dict(shared)
        for h in range(2):
            im["x_%d" % h] = fm(x[b, h * T:(h + 1) * T])
            im["pos_%d" % h] = np.ascontiguousarray(pos[b, h * T:(h + 1) * T])
        in_maps.append(im)
    res = run_bass_kernel_spmd(nc, in_maps, core_ids=list(range(NCORES)))
    out = np.empty((BATCH, SEQ, D), np.float32)
    for b in range(BATCH):
        for h in range(2):
            out[b, h * T:(h + 1) * T] = np.asarray(res.results[b]["out_%d" % h]).reshape(D, T).T
    return out
```
